# Optimizing a Trainium2 kernel written in Bass

```python
import jax, jax.numpy as jnp
from jax import lax
import numpy as np

D_MODEL = 1024
BATCH = 2
SEQ = 8192
DEPTH = 2
DEC_BATCH = 32
DEC_SEQ = 8
PAST_LEN = 16384
PAGE_SIZE = 128

A_WIDTH = 512
A_GROUPS = 4
A_GROUP_DIM = A_WIDTH // A_GROUPS
CHUNK = 128
B_WIDTH = 512
CONV_W = 31
C_HEADS = 8
C_HEAD_DIM = 64
C_GROUP_WIDTH = C_HEADS * C_HEAD_DIM
C_CONFIGS = ((128, 1), (512, 4), (2048, 16))
C_GROUPS = len(C_CONFIGS)
C_KEYS = C_CONFIGS[0][0] // C_CONFIGS[0][1] + 1
Q_BLOCK = 128
ATTN_SCALE = C_HEAD_DIM ** -0.5
N_BRANCH = 3
BRANCH_WIDTH = 512
D_FF = -(-8 * D_MODEL // (3 * 256)) * 256
EPS = 1e-6
NEG_INF = -1e30
OFF_AU = 0
OFF_AV = OFF_AU + A_WIDTH
OFF_B = OFF_AV + A_WIDTH
OFF_CQ = OFF_B + 2 * B_WIDTH
OFF_CK = OFF_CQ + C_GROUPS * C_GROUP_WIDTH
OFF_CV = OFF_CK + C_GROUPS * C_GROUP_WIDTH
OFF_G = OFF_CV + C_GROUPS * C_GROUP_WIDTH
D_IN = OFF_G + N_BRANCH * D_MODEL

kernel_name = 'hybrid_gmlp_conformer_dilated_attn_step'


def rms_norm(x, g):
    xf = x.astype(jnp.float32)
    y = xf * lax.rsqrt(jnp.mean(xf * xf, axis=-1, keepdims=True) + EPS)
    return (y * g.astype(jnp.float32)).astype(x.dtype)


def layer_norm(x, g, b):
    xf = x.astype(jnp.float32)
    mu = jnp.mean(xf, axis=-1, keepdims=True)
    var = jnp.mean(jnp.square(xf - mu), axis=-1, keepdims=True)
    y = (xf - mu) * lax.rsqrt(var + EPS)
    return (y * g.astype(jnp.float32) + b.astype(jnp.float32)).astype(x.dtype)


def alibi_slopes():
    n = C_GROUPS * C_HEADS
    s = jnp.exp2(-8.0 * (jnp.arange(n, dtype=jnp.float32) + 1.0) / n)
    return s.reshape(C_HEADS, C_GROUPS).T


def chunk_token_mlp(u, vn, w_s, b_s, chunk_len):
    bsz, t, _ = u.shape
    n_chunks = t // chunk_len
    causal = jnp.tril(jnp.ones((CHUNK, CHUNK), dtype=w_s.dtype))
    w = (w_s * causal)[:, :chunk_len, :chunk_len]
    vr = vn.reshape(bsz, n_chunks, chunk_len, A_GROUPS, A_GROUP_DIM)
    mix = jnp.einsum('gts,bcsgd->bctgd', w, vr) + b_s[:, :chunk_len].T[None, None, :, :, None]
    return u * mix.reshape(bsz, t, A_WIDTH)


def conv_module(z, buf, conv_w, conv_b, ln_g, ln_b):
    a, gate = jnp.split(z, 2, axis=-1)
    h = a * jax.nn.sigmoid(gate)
    padded = jnp.concatenate([buf.astype(h.dtype), h], axis=1)
    y = lax.conv_general_dilated(padded, conv_w[:, None, :].astype(h.dtype), window_strides=(1,),
                                 padding='VALID', dimension_numbers=('NWC', 'WIO', 'NWC'),
                                 feature_group_count=B_WIDTH) + conv_b
    y = jax.nn.silu(layer_norm(y, ln_g, ln_b))
    return y, padded[:, -(CONV_W - 1):]


def dilated_window_attention(q, k, v, q_idx, dilation, slopes):
    steps = jnp.arange(C_KEYS)
    idx = q_idx[:, None] - dilation * steps[None, :]
    valid = idx >= 0
    idx = jnp.maximum(idx, 0)
    kg = jnp.take(k, idx, axis=1)
    vg = jnp.take(v, idx, axis=1)
    s = jnp.einsum('bqhe,bqkhe->bqhk', q, kg, preferred_element_type=jnp.float32) * ATTN_SCALE
    s = s - slopes[:, None] * (dilation * steps).astype(jnp.float32)[None, :]
    s = jnp.where(valid[None, :, None, :], s, NEG_INF)
    m = jnp.max(s, axis=-1, keepdims=True)
    p = jnp.exp(s - m)
    den = jnp.sum(p, axis=-1, keepdims=True)
    o = jnp.einsum('bqhk,bqkhe->bqhe', (p / den).astype(v.dtype), vg)
    return o, (m + jnp.log(den))[..., 0]


def merge_dilations(outs, lses):
    w = jax.nn.softmax(jnp.stack(lses, axis=0), axis=0)
    o = jnp.sum(w[..., None] * jnp.stack(outs, axis=0).astype(jnp.float32), axis=0)
    bsz, t = o.shape[0], o.shape[1]
    return o.reshape(bsz, t, C_GROUP_WIDTH).astype(outs[0].dtype)


def dilated_attention_prompt(qs, ks, vs, slopes):
    bsz, t = qs[0].shape[0], qs[0].shape[1]

    def block(i):
        t0 = i * Q_BLOCK
        q_idx = t0 + jnp.arange(Q_BLOCK)
        outs, lses = [], []
        for g, (_, dil) in enumerate(C_CONFIGS):
            qb = lax.dynamic_slice_in_dim(qs[g], t0, Q_BLOCK, axis=1)
            o, l = dilated_window_attention(qb, ks[g], vs[g], q_idx, dil, slopes[g])
            outs.append(o)
            lses.append(l)
        return merge_dilations(outs, lses)

    y = lax.map(block, jnp.arange(t // Q_BLOCK))
    return jnp.swapaxes(y, 0, 1).reshape(bsz, t, C_GROUP_WIDTH)


def dilated_attention_sample(qs, ks, vs, caches, slopes):
    t = qs[0].shape[1]
    outs, lses = [], []
    for g, (_, dil) in enumerate(C_CONFIGS):
        kc, vc = caches[g]
        lb = kc.shape[1]
        k_src = jnp.concatenate([kc.astype(ks[g].dtype), ks[g]], axis=1)
        v_src = jnp.concatenate([vc.astype(vs[g].dtype), vs[g]], axis=1)
        o, l = dilated_window_attention(qs[g], k_src, v_src, lb + jnp.arange(t), dil, slopes[g])
        outs.append(o)
        lses.append(l)
    return merge_dilations(outs, lses)


def mixer_block(xn, w_in, a_norm_g, a_norm_b, a_w_s, a_b_s, b_conv_w, b_conv_b, b_norm_g,
                b_norm_b, w_branch, w_out, is_prompt, conv_buf, caches, slopes):
    bsz, t, _ = xn.shape
    h = xn @ w_in
    u = h[..., OFF_AU:OFF_AV]
    vn = layer_norm(h[..., OFF_AV:OFF_B], a_norm_g, a_norm_b)
    o_a = chunk_token_mlp(u, vn, a_w_s, a_b_s, CHUNK if is_prompt else t)
    if is_prompt:
        conv_buf = jnp.zeros((bsz, CONV_W - 1, B_WIDTH), h.dtype)
    o_b, new_buf = conv_module(h[..., OFF_B:OFF_CQ], conv_buf, b_conv_w, b_conv_b, b_norm_g, b_norm_b)

    def heads(off):
        return [h[..., off + g * C_GROUP_WIDTH: off + (g + 1) * C_GROUP_WIDTH]
                .reshape(bsz, t, C_HEADS, C_HEAD_DIM) for g in range(C_GROUPS)]
    qs, ks, vs = heads(OFF_CQ), heads(OFF_CK), heads(OFF_CV)
    if is_prompt:
        o_c = dilated_attention_prompt(qs, ks, vs, slopes)
        new_kv = [(ks[g][:, t - min(w, t):], vs[g][:, t - min(w, t):])
                  for g, (w, _) in enumerate(C_CONFIGS)]
    else:
        o_c = dilated_attention_sample(qs, ks, vs, caches, slopes)
        new_kv = [(ks[g], vs[g]) for g in range(C_GROUPS)]
    gates = jax.nn.sigmoid(h[..., OFF_G:].reshape(bsz, t, N_BRANCH, D_MODEL))
    branches = jnp.stack([o_a, o_b, o_c], axis=2)
    proj = jnp.einsum('btne,ned->btnd', branches, w_branch.reshape(N_BRANCH, BRANCH_WIDTH, D_MODEL))
    merged = jnp.sum(gates * proj, axis=2)
    return merged @ w_out, new_buf, vn, new_kv


def swiglu(xn, w1, w2):
    gate, up = jnp.split(xn @ w1, 2, axis=-1)
    return (jax.nn.silu(gate) * up) @ w2


def run_trunk(x, is_prompt, conv_state, c_cache, params):
    (norm_pre_mix, norm_post_mix, norm_pre_ffn, norm_post_ffn, w_in, a_norm_g, a_norm_b, a_w_s,
     a_b_s, b_conv_w, b_conv_b, b_norm_g, b_norm_b, w_branch, w_out, ffn_w_in, ffn_w_out) = params
    slopes = alibi_slopes()
    bufs, vns, kvs = [], [], []
    for l in range(DEPTH):
        conv_buf = None if is_prompt else conv_state[l]
        caches = None if is_prompt else [(c_cache[2 * g][l], c_cache[2 * g + 1][l]) for g in range(C_GROUPS)]
        y, buf, vn, kv = mixer_block(rms_norm(x, norm_pre_mix[l]), w_in[l], a_norm_g[l], a_norm_b[l],
                                     a_w_s[l], a_b_s[l], b_conv_w[l], b_conv_b[l], b_norm_g[l],
                                     b_norm_b[l], w_branch[l], w_out[l], is_prompt, conv_buf, caches, slopes)
        x = x + rms_norm(y, norm_post_mix[l])
        x = x + rms_norm(swiglu(rms_norm(x, norm_pre_ffn[l]), ffn_w_in[l], ffn_w_out[l]), norm_post_ffn[l])
        bufs.append(buf)
        vns.append(vn)
        kvs.append(kv)
    new_kv = [jnp.stack([kvs[l][g][j] for l in range(DEPTH)], axis=0)
              for g in range(C_GROUPS) for j in range(2)]
    return x, jnp.stack(bufs, axis=0), jnp.stack(vns, axis=0), new_kv


def setup_inputs(seed: int = 0) -> dict:
    key = jax.random.key(seed)
    ks = jax.random.split(key, 32)
    f32 = jnp.float32

    def nrm(k, shape, scale):
        return scale * jax.random.normal(k, shape, f32)

    def gain(k, shape):
        return 1.0 + 0.02 * jax.random.normal(k, shape, f32)

    def cache_len(w):
        return min(w, PAST_LEN)

    cshape = [(DEPTH, DEC_BATCH, cache_len(w), C_HEADS, C_HEAD_DIM) for (w, _) in C_CONFIGS]
    return {
        'x_prompt': nrm(ks[0], (BATCH, SEQ, D_MODEL), 1.0),
        'x_sample': nrm(ks[1], (DEC_BATCH, DEC_SEQ, D_MODEL), 1.0),
        'state_b_conv': nrm(ks[2], (DEPTH, DEC_BATCH, CONV_W - 1, B_WIDTH), 0.5),
        'cache_c0_k': nrm(ks[3], cshape[0], 1.0),
        'cache_c0_v': nrm(ks[4], cshape[0], 1.0),
        'cache_c1_k': nrm(ks[5], cshape[1], 1.0),
        'cache_c1_v': nrm(ks[6], cshape[1], 1.0),
        'cache_c2_k': nrm(ks[7], cshape[2], 1.0),
        'cache_c2_v': nrm(ks[8], cshape[2], 1.0),
        'norm_pre_mix': gain(ks[9], (DEPTH, D_MODEL)),
        'norm_post_mix': gain(ks[10], (DEPTH, D_MODEL)),
        'norm_pre_ffn': gain(ks[11], (DEPTH, D_MODEL)),
        'norm_post_ffn': gain(ks[12], (DEPTH, D_MODEL)),
        'w_in': nrm(ks[13], (DEPTH, D_MODEL, D_IN), D_MODEL ** -0.5),
        'a_norm_g': gain(ks[14], (DEPTH, A_WIDTH)),
        'a_norm_b': nrm(ks[15], (DEPTH, A_WIDTH), 0.02),
        'a_w_s': nrm(ks[16], (DEPTH, A_GROUPS, CHUNK, CHUNK), CHUNK ** -0.5),
        'a_b_s': gain(ks[17], (DEPTH, A_GROUPS, CHUNK)),
        'b_conv_w': nrm(ks[18], (DEPTH, CONV_W, B_WIDTH), CONV_W ** -0.5),
        'b_conv_b': nrm(ks[19], (DEPTH, B_WIDTH), 0.02),
        'b_norm_g': gain(ks[20], (DEPTH, B_WIDTH)),
        'b_norm_b': nrm(ks[21], (DEPTH, B_WIDTH), 0.02),
        'w_branch': nrm(ks[22], (DEPTH, N_BRANCH * BRANCH_WIDTH, D_MODEL), BRANCH_WIDTH ** -0.5),
        'w_out': nrm(ks[23], (DEPTH, D_MODEL, D_MODEL), D_MODEL ** -0.5),
        'ffn_w_in': nrm(ks[24], (DEPTH, D_MODEL, 2 * D_FF), D_MODEL ** -0.5),
        'ffn_w_out': nrm(ks[25], (DEPTH, D_FF, D_MODEL), D_FF ** -0.5),
    }


def reference(x_prompt, x_sample, state_b_conv, cache_c0_k, cache_c0_v, cache_c1_k, cache_c1_v,
              cache_c2_k, cache_c2_v, norm_pre_mix, norm_post_mix, norm_pre_ffn, norm_post_ffn,
              w_in, a_norm_g, a_norm_b, a_w_s, a_b_s, b_conv_w, b_conv_b, b_norm_g, b_norm_b,
              w_branch, w_out, ffn_w_in, ffn_w_out):
    params = (norm_pre_mix, norm_post_mix, norm_pre_ffn, norm_post_ffn, w_in, a_norm_g, a_norm_b,
              a_w_s, a_b_s, b_conv_w, b_conv_b, b_norm_g, b_norm_b, w_branch, w_out, ffn_w_in, ffn_w_out)
    c_cache = (cache_c0_k, cache_c0_v, cache_c1_k, cache_c1_v, cache_c2_k, cache_c2_v)
    y_prompt, new_b_conv_prompt, _, kv_p = run_trunk(x_prompt, True, None, None, params)
    y_sample, new_b_conv_sample, new_a_v_sample, kv_s = run_trunk(x_sample, False, state_b_conv, c_cache, params)
    (new_c0_k_prompt, new_c0_v_prompt, new_c1_k_prompt, new_c1_v_prompt,
     new_c2_k_prompt, new_c2_v_prompt) = kv_p
    (new_c0_k_sample, new_c0_v_sample, new_c1_k_sample, new_c1_v_sample,
     new_c2_k_sample, new_c2_v_sample) = kv_s
    return (y_prompt, y_sample, new_b_conv_prompt, new_b_conv_sample, new_a_v_sample,
            new_c0_k_prompt, new_c0_v_prompt, new_c1_k_prompt, new_c1_v_prompt,
            new_c2_k_prompt, new_c2_v_prompt,
            new_c0_k_sample, new_c0_v_sample, new_c1_k_sample, new_c1_v_sample,
            new_c2_k_sample, new_c2_v_sample)
```

```python
import numpy as np
from concourse.bass_utils import run_bass_kernel_spmd
import concourse.bass as bass
import concourse.mybir as mybir

F32 = mybir.dt.float32
BF16 = mybir.dt.bfloat16
ALU = mybir.AluOpType
ACTF = mybir.ActivationFunctionType
AX = mybir.AxisListType

_DSZ = {F32: 4, BF16: 2, mybir.dt.int32: 4, mybir.dt.float32r: 4}


def _region(ap):
    t = ap.tensor
    name = t.name
    dsz = _DSZ.get(ap.dtype, 4)
    dims = list(ap.ap)
    off = int(ap.offset)
    space = str(ap.space)
    if space in ("SB", "PSUM"):
        pstep, pcnt = dims[0]
        if pstep == 0:
            pstep = 1 << 40
        p0 = off // pstep if pstep < (1 << 40) else 0
        f0 = off - p0 * pstep if pstep < (1 << 40) else off
        p1 = p0 + pcnt
        rest = dims[1:]
    else:
        p0, p1 = 0, 1
        f0 = off
        rest = dims
    lo = f0
    hi = f0
    for st, cn in rest:
        if cn <= 0:
            continue
        d = st * (cn - 1)
        if d < 0:
            lo += d
        else:
            hi += d
    return name, p0, p1, lo * dsz, (hi + 1) * dsz


class Sched:
    ENGS = ("tensor", "vector", "scalar", "gpsimd", "sync")

    def __init__(self, nc, n_dma_sems=24):
        self.nc = nc
        self.ops = []
        self.recs = {}
        self.n_dma_sems = n_dma_sems
        self.dma_count = {e: 0 for e in self.ENGS}
        self.dma_hist = {e: [] for e in self.ENGS}
        self.barrier_deps = {e: set() for e in self.ENGS}
        self.last = {e: None for e in self.ENGS}
        self.all_dmas = []

    def _access(self, ap, opid, is_write, deps):
        if str(ap.space) == "PSUM":
            name = ap.tensor.name
            eng = self.ops[opid]["eng"]
            rec = self.recs.setdefault(name, {})
            for e2, (last_any, last_w) in rec.items():
                if e2 != eng:
                    if last_any is not None and last_any != opid:
                        deps.add(last_any)
                else:
                    if is_write:
                        if last_any is not None and last_any != opid:
                            deps.add(last_any)
                    elif last_w is not None and last_w != opid:
                        deps.add(last_w)
            la, lw = rec.get(eng, (None, None))
            rec[eng] = (opid, opid if is_write else lw)
            return
        name, p0, p1, lo, hi = _region(ap)
        lst = self.recs.setdefault(name, [])
        keep = []
        eng = self.ops[opid]["eng"]
        isdma = self.ops[opid]["dma"]
        for r in lst:
            ov = not (r[1] <= p0 or p1 <= r[0] or r[3] <= lo or hi <= r[2])
            if ov and r[4] != opid:
                if is_write or r[5]:
                    deps.add(r[4])
                if is_write and r[0] >= p0 and r[1] <= p1 and r[2] >= lo and r[3] <= hi:
                    continue
            if (not is_write) and (not r[5]) and (not isdma) and r[4] != opid:
                ro = self.ops[r[4]]
                if ro["eng"] == eng and not ro["dma"] and r[0] == p0 and r[1] == p1 and r[2] == lo and r[3] == hi:
                    continue
            keep.append(r)
        keep.append([p0, p1, lo, hi, opid, is_write])
        self.recs[name] = keep

    def op(self, eng, fn, outs=(), ins=(), dma=False):
        opid = len(self.ops)
        o = {"eng": eng, "fn": fn, "deps": set(), "dma": dma}
        self.ops.append(o)
        deps = o["deps"]
        for a in ins:
            self._access(a, opid, False, deps)
        for a in outs:
            self._access(a, opid, True, deps)
        if self.barrier_deps[eng]:
            deps |= self.barrier_deps[eng]
            self.barrier_deps[eng] = set()
        if dma:
            h = self.dma_hist[eng]
            if len(h) >= self.n_dma_sems:
                deps.add(h[-self.n_dma_sems])
            h.append(opid)
            self.all_dmas.append(opid)
        self.last[eng] = opid
        return opid

    def barrier(self):
        d = set(x for x in self.last.values() if x is not None)
        d |= set(self.all_dmas[-64:])
        for e in self.ENGS:
            self.barrier_deps[e] = set(d)

    def dma(self, out, in_, eng="sync", **kw):
        return self.op(eng, lambda e: e.dma_start(out=out, in_=in_, **kw), [out], [in_], dma=True)

    def mm(self, out, lhsT, rhs, start=True, stop=True, **kw):
        return self.op("tensor", lambda e: e.matmul(out, lhsT, rhs, start=start, stop=stop, **kw),
                       [out], [lhsT, rhs])

    def tr(self, out, in_, ident):
        return self.op("tensor", lambda e: e.transpose(out, in_, ident), [out], [in_, ident])

    def act(self, out, in_, func, bias=None, scale=None, accum_out=None, eng="scalar"):
        kw = {}
        ins = [in_]
        outs = [out]
        if bias is not None:
            kw["bias"] = bias
            if not isinstance(bias, (int, float)):
                ins.append(bias)
        if scale is not None:
            kw["scale"] = scale
            if not isinstance(scale, (int, float)):
                ins.append(scale)
        if accum_out is not None:
            kw["accum_out"] = accum_out
            outs.append(accum_out)
        return self.op(eng, lambda e: e.activation(out, in_, func, **kw), outs, ins)

    def tt(self, out, in0, in1, op, eng="vector"):
        return self.op(eng, lambda e: e.tensor_tensor(out, in0, in1, op), [out], [in0, in1])

    def ts(self, out, in0, s1, s2, op0, op1=None, eng="vector", accum_out=None):
        ins = [in0] + [s for s in (s1, s2) if s is not None and not isinstance(s, (int, float))]
        outs = [out] + ([accum_out] if accum_out is not None else [])
        if op1 is None:
            return self.op(eng, lambda e: e.tensor_scalar(out, in0, s1, s2, op0), outs, ins)
        if accum_out is not None:
            return self.op(eng, lambda e: e.tensor_scalar(out, in0, s1, s2, op0, op1, accum_out), outs, ins)
        return self.op(eng, lambda e: e.tensor_scalar(out, in0, s1, s2, op0, op1), outs, ins)

    def stt(self, out, in0, scalar, in1, op0, op1, eng="vector"):
        ins = [in0, in1] + ([scalar] if not isinstance(scalar, (int, float)) else [])
        return self.op(eng, lambda e: e.scalar_tensor_tensor(out, in0, scalar, in1, op0, op1), [out], ins)

    def copy(self, out, in_, eng="vector"):
        if eng == "scalar":
            return self.op(eng, lambda e: e.copy(out, in_), [out], [in_])
        return self.op(eng, lambda e: e.tensor_copy(out, in_), [out], [in_])

    def memset(self, ap, val, eng="vector"):
        return self.op(eng, lambda e: e.memset(ap, val), [ap], [])

    def reduce(self, out, in_, op, axis=AX.X, eng="vector"):
        return self.op(eng, lambda e: e.tensor_reduce(out, in_, axis, op), [out], [in_])

    def recip(self, out, in_):
        return self.op("vector", lambda e: e.reciprocal(out, in_), [out], [in_])

    def emit(self, stack):
        nc = self.nc
        ops = self.ops
        needed = set()
        for o in ops:
            for d in o["deps"]:
                do = ops[d]
                if o["eng"] == "tensor" and do["eng"] == "tensor" and not do["dma"] and not o["dma"]:
                    continue
                needed.add(d)
        final_dmas = list(self.all_dmas)
        eng_sem = {e: stack.enter_context(nc.semaphore("se_" + e)) for e in self.ENGS}
        dma_sems = {e: [stack.enter_context(nc.semaphore("sd_%s_%d" % (e, i)))
                        for i in range(self.n_dma_sems)]
                    for e in self.ENGS if self.dma_count is not None and any(
                        (o["dma"] and o["eng"] == e) for o in ops)}
        cnt = {e: 0 for e in self.ENGS}
        dcount = {e: 0 for e in self.ENGS}
        dsemcnt = {}
        sig = {}
        per_eng = {e: [] for e in self.ENGS}
        for i, o in enumerate(ops):
            e = o["eng"]
            per_eng[e].append(i)
            if o["dma"]:
                k = dcount[e] % self.n_dma_sems
                dcount[e] += 1
                s = dma_sems[e][k]
                dsemcnt[(e, k)] = dsemcnt.get((e, k), 0) + 16
                sig[i] = (s, dsemcnt[(e, k)])
            elif i in needed:
                cnt[e] += 1
                sig[i] = (eng_sem[e], cnt[e])
        self.sig = sig

        plan = {e: [] for e in self.ENGS}
        for ename in self.ENGS:
            seen = {}
            for i in per_eng[ename]:
                o = ops[i]
                waits = []
                for d in sorted(o["deps"]):
                    do = ops[d]
                    if (not do["dma"]) and do["eng"] == ename and (ename == "tensor"):
                        continue
                    s_, v = sig[d]
                    if seen.get(s_.name, 0) >= v:
                        continue
                    seen[s_.name] = v
                    waits.append((s_.name, v))
                plan[ename].append((i, waits, (sig[i][0].name, 16 if o["dma"] else 1) if i in sig else None))
        semv = {}
        pc = {e: 0 for e in self.ENGS}
        progress = True
        while progress:
            progress = False
            for e in self.ENGS:
                while pc[e] < len(plan[e]):
                    i, waits, sg = plan[e][pc[e]]
                    if all(semv.get(n, 0) >= v for n, v in waits):
                        if sg is not None:
                            semv[sg[0]] = semv.get(sg[0], 0) + sg[1]
                        pc[e] += 1
                        progress = True
                    else:
                        break
        stuck = {e: (pc[e], len(plan[e])) for e in self.ENGS if pc[e] < len(plan[e])}
        if stuck:
            for e in stuck:
                i, waits, sg = plan[e][pc[e]]
                print("DEADLOCK", e, "op", i, "waits", [(n, v, semv.get(n, 0)) for n, v in waits])
            raise RuntimeError("scheduler deadlock: %s" % stuck)
        self.max_sem = dict(semv)

        block = stack.enter_context(nc.Block())

        def make(ename):
            def body(eh):
                seen = {}
                for i in per_eng[ename]:
                    o = ops[i]
                    for d in sorted(o["deps"]):
                        do = ops[d]
                        if (not do["dma"]) and do["eng"] == ename and (ename == "tensor"):
                            continue
                        s, v = sig[d]
                        if seen.get(s.name, 0) >= v:
                            continue
                        seen[s.name] = v
                        eh.wait_ge(s, v)
                    ins = o["fn"](eh)
                    if i in sig:
                        ins.then_inc(sig[i][0], 16 if o["dma"] else 1)
                if ename == "sync":
                    for d in final_dmas:
                        s, v = sig[d]
                        if seen.get(s.name, 0) >= v:
                            continue
                        seen[s.name] = v
                        eh.wait_ge(s, v)
                    for e2 in self.ENGS:
                        if e2 != "sync" and cnt[e2] > 0:
                            eh.wait_ge(eng_sem[e2], cnt[e2])
            return body

        for ename in self.ENGS:
            if per_eng[ename] or ename == "sync":
                getattr(block, ename)(make(ename))

import numpy as np
from contextlib import ExitStack

D = 1024
NT = 2048
DIN = 9728
OFF_AU, OFF_AV, OFF_B, OFF_CQ, OFF_CK, OFF_CV, OFF_G = 0, 512, 1024, 2048, 3584, 5120, 6656
DFF = 2816
DIL = (1, 4, 16)
EPS = 1e-6
SCALE = 0.125
KB = 1024
XF, XH, OT, PL = 0, 32 * KB, 64 * KB, 112 * KB
ARENA = 188 * KB


class _Stop(Exception):
    pass


def build_nc(stop=None, dbg=False, step=None, skip_sample=False, sample_only=False):
    nc = bass.Bass("TRN2", target_bir_lowering=False)
    din = lambda n, s: nc.dram_tensor(n, list(s), F32, kind="ExternalInput").ap()
    dout = lambda n, s: nc.dram_tensor(n, list(s), F32, kind="ExternalOutput").ap()
    xw = din("xw", [6144, D])
    flags_d = din("flags", [128, 4])
    etab_d = din("etab", [12, 128, 512])
    ident_d = din("ident", [128, 128])
    tril_d = din("tril", [128, 128])
    ones2_d = din("ones2", [128, 256])
    g_pre_mix = din("norm_pre_mix", [2, D]); g_post_mix = din("norm_post_mix", [2, D])
    g_pre_ffn = din("norm_pre_ffn", [2, D]); g_post_ffn = din("norm_post_ffn", [2, D])
    w_in = din("w_in", [2, D, DIN])
    a_norm_g = din("a_norm_g", [2, 512]); a_norm_b = din("a_norm_b", [2, 512])
    a_w_s = din("a_w_s", [2, 4, 128, 128]); a_b_s = din("a_b_s", [2, 4, 128])
    b_conv_w = din("b_conv_w", [2, 31, 512]); b_conv_b = din("b_conv_b", [2, 512])
    b_norm_g = din("b_norm_g", [2, 512]); b_norm_b = din("b_norm_b", [2, 512])
    w_branch = din("w_branch", [2, 1536, D]); w_out = din("w_out", [2, D, D])
    ffn_w_in = din("ffn_w_in", [2, D, 2 * DFF]); ffn_w_out = din("ffn_w_out", [2, DFF, D])

    xs_d = din("xs", [32, D])
    st_d = din("st", [2, 4, 30, 512])
    bdm_d = din("bdm", [32, 32])
    c0k = din("c0k", [2, 4, 128, 512]); c0v = din("c0v", [2, 4, 128, 512])
    c1k = din("c1k", [2, 4, 512, 512]); c1v = din("c1v", [2, 4, 512, 512])
    c2k = din("c2k", [2, 4, 2048, 512]); c2v = din("c2v", [2, 4, 2048, 512])
    ys_o = dout("ys_o", [32, D])
    nbs_o = dout("nbs_o", [2, 4, 30, 512])
    nav_o = dout("nav_o", [2, 32, 512])
    ks_o = dout("ks_o", [2, 3, 32, 512])
    vs_o = dout("vs_o", [2, 3, 32, 512])
    if dbg:
        x1s = nc.dram_tensor("x1s", [4096, D], F32, kind="ExternalOutput").ap()
        xmid = nc.dram_tensor("xmid", [NT, D], F32, kind="ExternalOutput").ap()
    else:
        x1s = nc.dram_tensor("x1s", [4096, D], F32).ap()
        xmid = nc.dram_tensor("xmid", [NT, D], F32).ap()

    y_o = dout("y_o", [NT, D])
    k_o = dout("k_o", [2, 3, NT, 512])
    v_o = dout("v_o", [2, 3, NT, 512])
    gt_o = dout("gt_o", [2, 512, 32])
    dbg_o = nc.dram_tensor("dbg_o", [128, ARENA // 2], BF16, kind="ExternalOutput").ap() if dbg else None

    with ExitStack() as st:
        sbt = lambda n, s, d: st.enter_context(nc.sbuf_tensor(n, list(s), d))
        arena = sbt("arena", [128, ARENA // 2], BF16)
        identb = sbt("identb", [128, 128], BF16)
        identf = sbt("identf", [128, 128], F32)
        trilf = sbt("trilf", [128, 128], F32)
        ones2 = sbt("ones2s", [128, 2, 128], BF16)
        etab = sbt("etabs", [128, 12, 512], BF16)
        flags = sbt("flagss", [128, 4], F32)
        stat = sbt("stat", [128, 256], F32)
        bs_sb = sbt("bs_sb", [128, 4], F32)
        bs32 = sbt("bs32", [32, 4], F32)
        bdm = sbt("bdm_s", [32, 32], F32)
        xs_res = sbt("xs_res", [32, D], F32)
        psb = [st.enter_context(nc.psum_tensor("psb%d" % i, [128, 512], F32)) for i in range(8)]
        S = Sched(nc)
        state = {"ps": 0, "st": 0, "alt": 0}

        def ps():
            state["ps"] = (state["ps"] + 1) % 8
            return psb[state["ps"]]

        def stc(n=1):
            i = state["st"]
            if i + n > 256:
                i = 0
            state["st"] = i + n
            return stat[:, i:i + n]

        def alt(a="vector", b="scalar"):
            state["alt"] ^= 1
            return a if state["alt"] else b

        def Vw(off, shape, dt=BF16):
            n = 1
            for s_ in shape:
                n *= s_
            dsz = 2 if dt == BF16 else 4
            v = arena[:, off // 2: off // 2 + n * dsz // 2]
            if dt != BF16:
                v = v.bitcast(dt)
            if len(shape) == 2:
                v = v.rearrange("p (a b) -> p a b", a=shape[0])
            elif len(shape) == 3:
                v = v.rearrange("p (a b c) -> p a b c", a=shape[0], b=shape[1])
            return v

        def dstep(n):
            if step == n and stop is not None and state.get("pass") == stop.split(":")[0]:
                raise _Stop()

        def ckpt(name):
            if stop == name:
                raise _Stop()

        def bcast_load(dst, row):
            S.dma(dst, row.partition_broadcast(128))

        def evac(out, in_, eng=None):
            eng = eng or alt()
            S.copy(out, in_, eng=eng)

        S.dma(identf[:], ident_d)
        S.dma(trilf[:], tril_d)
        S.dma(flags[:], flags_d)
        S.dma(ones2[:].rearrange("p a b -> p (a b)"), ones2_d, eng="gpsimd")
        S.dma(etab[:], etab_d.rearrange("n p c -> p n c"), eng="gpsimd")
        S.copy(identb[:], identf[:])
        S.dma(bdm[:], bdm_d)

        if step == 777:
            S.dma(v_o[0, 0][0:128, 0:128], identf[:])
        try:
            ckpt("const")
        except _Stop:
            S.barrier()
            S.dma(dbg_o, arena[:])
            S.emit(st)
            return nc

        def lockstep(gens):
            gens = list(gens)
            while gens:
                nxt = []
                for g_ in gens:
                    try:
                        next(g_)
                        nxt.append(g_)
                    except StopIteration:
                        pass
                gens = nxt

        def run1(gen):
            for _ in gen:
                pass

        def g_rstd(ss, n, eps=EPS, P=slice(0, 128)):
            m = stc()[P, :]
            S.ts(m, ss, 1.0 / n, eps, ALU.mult, ALU.add)
            yield
            S.act(m, m, ACTF.Sqrt)
            yield
            r = stc()[P, :]
            S.recip(r, m)
            yield
            return r

        def g_sumsq(src, junk, P=slice(0, 128)):
            ss = stc()[P, :]
            S.memset(ss, 0.0)
            yield
            S.act(junk, src, ACTF.Square, accum_out=ss)
            yield
            return ss

        def rstd_from_ss(ss, n, eps=EPS, P=slice(0, 128)):
            g_ = g_rstd(ss, n, eps, P)
            try:
                while True:
                    next(g_)
            except StopIteration as e_:
                return e_.value

        def sumsq(src, junk, P=slice(0, 128)):
            g_ = g_sumsq(src, junk, P)
            try:
                while True:
                    next(g_)
            except StopIteration as e_:
                return e_.value

        def g_norm_to_T(xt, gtile, dstT, tile, junk, xnb):
            ss = yield from g_sumsq(xt, junk)
            r = yield from g_rstd(ss, D)
            S.stt(xnb, xt, r, gtile, ALU.mult, ALU.mult)
            yield
            pt = ps()[:].bitcast(BF16)
            for k in range(8):
                S.tr(pt[:, k * 128:(k + 1) * 128], xnb[:, k * 128:(k + 1) * 128], identb[:])
            yield
            evac(dstT[:, :, tile * 128:(tile + 1) * 128], pt[:, 0:1024].rearrange("p (k c) -> p k c", k=8))
            yield

        def phase_norm(src, grow, dstT, ntiles, tile0=0):
            gtile = Vw(PL + 20 * KB, [1024], F32)
            bcast_load(gtile, grow)

            def body(i, sl_):
                xt = Vw(PL + sl_ * 4 * KB, [1024], F32)
                S.dma(xt, src[i * 128:(i + 1) * 128, :])
                yield
                yield from g_norm_to_T(xt, gtile, dstT, tile0 + i, Vw(PL + 8 * KB + sl_ * 4 * KB, [1024], F32),
                                       Vw(PL + 16 * KB + sl_ * 2 * KB, [1024]))
            for i0 in range(0, ntiles, 2):
                lockstep([body(i0, 0), body(i0 + 1, 1)])

        def g_layernorm512(src, gt, bt, out, toff):
            junk = Vw(toff, [512], F32)
            tmp = Vw(toff + 2 * KB, [512], F32)
            sm = stc()
            S.reduce(sm, src, ALU.add)
            yield
            sq = yield from g_sumsq(src, junk)
            mean = stc()
            S.ts(mean, sm, 1.0 / 512, 0.0, ALU.mult, ALU.add)
            yield
            msq = stc()
            S.tt(msq, mean, mean, ALU.mult)
            yield
            var = stc()
            S.stt(var, sq, 1.0 / 512, msq, ALU.mult, ALU.subtract)
            yield
            S.ts(var, var, 1.0, EPS, ALU.mult, ALU.add)
            yield
            S.act(var, var, ACTF.Sqrt)
            yield
            r = stc()
            S.recip(r, var)
            yield
            S.ts(tmp, src, mean, r, ALU.subtract, ALU.mult)
            yield
            S.tt(tmp, tmp, gt, ALU.mult)
            yield
            S.tt(out, tmp, bt, ALU.add)
            yield

        def to_featT(src_bf, dstT, tile):
            pt = ps()[:].bitcast(BF16)
            for c in range(4):
                S.tr(pt[:, c * 128:(c + 1) * 128], src_bf[:, c * 128:(c + 1) * 128], identb[:])
            evac(dstT[:, :, tile * 128:(tile + 1) * 128],
                 pt[:, 0:512].rearrange("p (c t) -> p c t", c=4))

        def wload(dst, src2d, kc):
            S.dma(dst, src2d.rearrange("(k p) c -> p k c", p=128), eng="gpsimd")

        def run_pass(pname, l, xsrc, hsrc, xdst, fcol, out_l):
            state["pass"] = pname
            xnT_f = Vw(XF, [8, NT])
            xnT_h = Vw(XH, [8, NT])
            mergedT = xnT_h
            oT = [Vw(OT + n * 16 * KB, [4, NT]) for n in range(3)]
            S.barrier()
            phase_norm(hsrc, g_pre_mix[l:l + 1, :], xnT_h, 16)
            phase_norm(xsrc, g_pre_mix[l:l + 1, :], xnT_f, 16)

            ckpt("%s:N" % pname)
            WA = Vw(PL, [8, 1024])
            wload(WA, w_in[l][:, OFF_AU:OFF_AU + 1024], 8)
            agt = Vw(PL + 16 * KB, [512], F32); abt = Vw(PL + 18 * KB, [512], F32)
            bcast_load(agt, a_norm_g[l:l + 1, :]); bcast_load(abt, a_norm_b[l:l + 1, :])
            WsT = Vw(PL + 20 * KB, [4, 128])
            wtmp = Vw(PL + 21 * KB, [4, 128], F32)
            wtmpb = Vw(PL + 23 * KB, [4, 128])
            S.dma(wtmp, a_w_s[l].rearrange("g t s -> t g s"))
            S.dma(bs_sb[:], a_b_s[l].rearrange("g t -> t g"), allow_slow_non_contiguous=True)
            for g in range(4):
                S.tt(wtmpb[:, g, :], wtmp[:, g, :], trilf[:], ALU.mult)
            pt = ps()[:].bitcast(BF16)
            for g in range(4):
                S.tr(pt[:, g * 128:(g + 1) * 128], wtmpb[:, g, :], identb[:])
            evac(WsT, pt[:, 0:512].rearrange("p (g t) -> p g t", g=4))
            def bodyA(i, sl_):
                TA = PL + 24 * KB + sl_ * 14 * KB
                u_sb = Vw(TA, [512], F32)
                v_sb = Vw(TA + 2 * KB, [512], F32)
                vnb = Vw(TA + 4 * KB, [512])
                oab = Vw(TA + 5 * KB, [512])
                pm_sb = Vw(TA + 10 * KB, [512], F32)
                pu = ps(); pv = ps()
                for k in range(8):
                    S.mm(pu[:], xnT_f[:, k, i * 128:(i + 1) * 128], WA[:, k, 0:512], start=(k == 0), stop=(k == 7))
                for k in range(8):
                    S.mm(pv[:], xnT_f[:, k, i * 128:(i + 1) * 128], WA[:, k, 512:1024], start=(k == 0), stop=(k == 7))
                yield
                S.copy(u_sb, pu[:], eng="scalar")
                S.copy(v_sb, pv[:], eng="vector")
                yield
                yield from g_layernorm512(v_sb, agt, abt, vnb, TA + 6 * KB)
                pm = ps()
                for g in range(4):
                    S.mm(pm[:, g * 128:(g + 1) * 128], WsT[:, g, :], vnb[:, g * 128:(g + 1) * 128])
                yield
                S.copy(pm_sb, pm[:], eng="scalar")
                yield
                for g in range(4):
                    S.stt(oab[:, g * 128:(g + 1) * 128], pm_sb[:, g * 128:(g + 1) * 128], bs_sb[:, g:g + 1],
                          u_sb[:, g * 128:(g + 1) * 128], ALU.add, ALU.mult)
                yield
                pt = ps()[:].bitcast(BF16)
                for c in range(4):
                    S.tr(pt[:, c * 128:(c + 1) * 128], oab[:, c * 128:(c + 1) * 128], identb[:])
                yield
                evac(oT[0][:, :, i * 128:(i + 1) * 128], pt[:, 0:512].rearrange("p (c t) -> p c t", c=4))
                yield
            for i0 in range(0, 16, 2):
                lockstep([bodyA(i0, 0), bodyA(i0 + 1, 1)])

            ckpt("%s:A" % pname)
            S.barrier()
            WB = Vw(PL, [8, 1024])
            wload(WB, w_in[l][:, OFF_B:OFF_B + 1024], 8)
            gluT = Vw(PL + 16 * KB, [4, 2176])
            diag = Vw(PL + 33 * KB, [124, 128])
            TB = OT + 32 * KB
            bgt = Vw(TB, [512], F32); bbt = Vw(TB + 2 * KB, [512], F32); cbt = Vw(TB + 4 * KB, [512], F32)
            bcast_load(bgt, b_norm_g[l:l + 1, :]); bcast_load(bbt, b_norm_b[l:l + 1, :]); bcast_load(cbt, b_conv_b[l:l + 1, :])
            ysb = Vw(TB + 6 * KB, [512], F32)
            sig = Vw(TB + 8 * KB, [512], F32)
            obb = Vw(TB + 10 * KB, [512])
            cw = Vw(TB + 11 * KB, [512], F32)
            cwT = Vw(TB + 13 * KB, [4, 32], F32)
            gt32 = Vw(TB + 13 * KB + 512, [4, 32], F32)
            lnout = Vw(TB + 14 * KB, [512], F32)
            for j in range(31):
                S.dma(cwT[:, :, j], b_conv_w[l, j].rearrange("(c p) -> p c", p=128), allow_slow_non_contiguous=True)
            for c in range(4):
                for j in range(31):
                    S.ts(diag[:, c * 31 + j, :], identb[:], cwT[:, c, j:j + 1], 1.0, ALU.mult, ALU.mult,
                         eng=("vector" if j % 2 == 0 else "gpsimd"))
            def glu_block(rhs_of_k, n, dst_cols, tail=None):
                for c in range(4):
                    pa = ps(); pg = ps()
                    for k in range(8):
                        S.mm(pa[:, 0:n], WB[:, k, c * 128:(c + 1) * 128], rhs_of_k(k), start=(k == 0), stop=(k == 7))
                    for k in range(8):
                        S.mm(pg[:, 0:n], WB[:, k, 512 + c * 128:512 + (c + 1) * 128], rhs_of_k(k), start=(k == 0), stop=(k == 7))
                    S.act(sig[:, 0:n], pg[:, 0:n], ACTF.Sigmoid)
                    S.tt(gluT[:, c, dst_cols:dst_cols + n], pa[:, 0:n], sig[:, 0:n], ALU.mult)
                    if tail is not None:
                        S.tt(gt32[:, c, :], pa[:, n - 32:n], sig[:, n - 32:n], ALU.mult)
            glu_block(lambda k: xnT_h[:, k, NT - 128:NT], 128, 0)
            for c in range(4):
                S.ts(gluT[:, c, 0:128], gluT[:, c, 0:128], flags[:, fcol:fcol + 1], 1.0, ALU.mult, ALU.mult)
            for w in range(4):
                glu_block(lambda k, w=w: xnT_f[:, k, w * 512:(w + 1) * 512], 512, 128 + w * 512,
                          tail=(out_l is not None and w == 3) or None)
            if out_l is not None:
                S.dma(gt_o[out_l].rearrange("(c p) t -> p c t", p=128), gt32)
            def bodyB(i, sl_):
                ysb_ = ysb if sl_ == 0 else Vw(TB + 11 * KB, [512], F32)
                lnout_ = lnout if sl_ == 0 else Vw(PL + 72 * KB, [512], F32)
                obb_ = obb if sl_ == 0 else Vw(PL + 74 * KB, [512])
                pc = ps()
                for c in range(4):
                    for j in range(31):
                        s0 = 128 + i * 128 - 30 + j
                        S.mm(pc[:, c * 128:(c + 1) * 128], gluT[:, c, s0:s0 + 128], diag[:, c * 31 + j, :],
                             start=(j == 0), stop=(j == 30))
                yield
                S.tt(ysb_, pc[:], cbt, ALU.add)
                yield
                yield from g_layernorm512(ysb_, bgt, bbt, lnout_, PL + 64 * KB + sl_ * 4 * KB)
                S.act(obb_, lnout_, ACTF.Silu)
                yield
                pt = ps()[:].bitcast(BF16)
                for c in range(4):
                    S.tr(pt[:, c * 128:(c + 1) * 128], obb_[:, c * 128:(c + 1) * 128], identb[:])
                yield
                evac(oT[1][:, :, i * 128:(i + 1) * 128], pt[:, 0:512].rearrange("p (c t) -> p c t", c=4))
                yield
            for i0 in range(0, 16, 2):
                lockstep([bodyB(i0, 0), bodyB(i0 + 1, 1)])

            ckpt("%s:B" % pname)
            S.barrier()
            WC = Vw(PL, [9, 8, 128])
            QT = Vw(PL + 18 * KB, [2, NT])
            KTb = Vw(PL + 26 * KB, [1, 4096])[:, 0, :]
            Vt = Vw(PL + 34 * KB, [32, 2, 128])
            acc = Vw(PL + 50 * KB, [2, NT], F32)
            Pt2 = [Vw(PL + 66 * KB + q * KB, [512]) for q in range(2)]
            kst = Vw(PL + 68 * KB, [512], F32)
            import os
            vst = Vw(PL + (68 if os.environ.get("VST68") else 70) * KB, [512], F32)
            Eh = Vw(PL + 72 * KB, [512])
            S.memset(Vt.rearrange("p a b c -> p (a b c)"), 0.0, eng="gpsimd")
            S.memset(QT[64:128, 0, :], 0.0, eng="gpsimd")
            S.memset(QT[0:64, 1, :], 0.0, eng="gpsimd")
            for c in range(4):
                for g in range(3):
                    for j, off in enumerate((OFF_CQ, OFF_CK, OFF_CV)):
                        wload(WC[:, g * 3 + j, :, :], w_in[l][:, off + g * 512 + c * 128: off + g * 512 + (c + 1) * 128], 8)
                for g in range(3):
                    d = DIL[g]
                    Lh = 128 * d
                    nb = 16 // d
                    Wq, Wk, Wv = WC[:, g * 3 + 0], WC[:, g * 3 + 1], WC[:, g * 3 + 2]
                    E = etab[:, g * 4 + c, :]
                    for hh in range(2):
                        S.ts(Eh[:, hh * 256:hh * 256 + 128], E[:, hh * 256:hh * 256 + 128], flags[:, fcol:fcol + 1], 1.0,
                             ALU.mult, ALU.mult)
                        S.copy(Eh[:, hh * 256 + 128:hh * 256 + 256], E[:, hh * 256 + 128:hh * 256 + 256], eng="gpsimd")
                    for w in range(4):
                        pq = ps(); pk = ps()
                        for k in range(8):
                            S.mm(pq[:], Wq[:, k, :], xnT_f[:, k, w * 512:(w + 1) * 512], start=(k == 0), stop=(k == 7))
                        for k in range(8):
                            S.mm(pk[:], Wk[:, k, :], xnT_f[:, k, w * 512:(w + 1) * 512], start=(k == 0), stop=(k == 7))
                        S.copy(QT[0:64, 0, w * 512:(w + 1) * 512], pq[0:64, :], eng="vector")
                        S.copy(QT[64:128, 1, w * 512:(w + 1) * 512], pq[64:128, :], eng="scalar")
                        evac(KTb[:, Lh + w * 512:Lh + (w + 1) * 512], pk[:])
                    hw = min(512, Lh)
                    for w in range(Lh // hw):
                        pk = ps()
                        c0 = NT - Lh + w * hw
                        for k in range(8):
                            S.mm(pk[:, 0:hw], Wk[:, k, :], xnT_h[:, k, c0:c0 + hw], start=(k == 0), stop=(k == 7))
                        evac(KTb[:, w * hw:(w + 1) * hw], pk[:, 0:hw])
                    htiles = [("h", r, 0) for r in range(d)]
                    ftiles = [("f", r, jb) for r in range(d) for jb in range(nb)]
                    groups = [(t0, htiles[t0:t0 + 4]) for t0 in range(0, d, 4)] + \
                             [(d + t0, ftiles[t0:t0 + 4]) for t0 in range(0, 16, 4)]
                    vo2 = v_o[out_l, g] if out_l is not None else None
                    for (t0, grp) in groups:
                        pv = ps()
                        for q, (kind, r, jb) in enumerate(grp):
                            if kind == "h":
                                srcT = xnT_h; s0 = NT - Lh + r
                            else:
                                srcT = xnT_f; s0 = r + d * 128 * jb
                            for k in range(8):
                                S.mm(pv[:, q * 128:(q + 1) * 128], srcT[:, k, s0:s0 + 127 * d + 1:d],
                                     Wv[:, k, :], start=(k == 0), stop=(k == 7))
                        n = len(grp)
                        pv3 = pv[:, 0:n * 128].rearrange("p (t e) -> p t e", t=n)
                        S.copy(Vt[:, t0:t0 + n, 0, 0:64], pv3[:, :, 0:64], eng="vector")
                        S.copy(Vt[:, t0:t0 + n, 1, 64:128], pv3[:, :, 64:128], eng="scalar")
                        if out_l is not None and grp[0][0] == "f":
                            S.copy(vst, pv[:], eng="scalar")
                            cs = slice(c * 128, (c + 1) * 128)
                            _, r0, jb0 = grp[0]
                            if g == 0:
                                dst = vo2.rearrange("(q p) e -> p q e", p=128)[:, jb0:jb0 + 4, cs]
                            elif g == 1:
                                dst = vo2.rearrange("(q p dd) e -> p q dd e", p=128, dd=4)[:, :, r0, cs]
                            else:
                                dst = vo2.rearrange("(p dd) e -> p dd e", dd=16)[:, r0:r0 + 4, cs]
                            S.dma(dst, vst.rearrange("p (t e) -> p t e", t=4))
                    import os
                    if out_l is not None and not os.environ.get("NOKOUT"):
                        for t0 in range(0, 16, 4):
                            pk = ps()
                            for q in range(4):
                                i = t0 + q
                                for k in range(8):
                                    S.mm(pk[:, q * 128:(q + 1) * 128], xnT_f[:, k, i * 128:(i + 1) * 128], Wk[:, k, :],
                                         start=(k == 0), stop=(k == 7))
                            S.copy(kst, pk[:], eng="scalar")
                            S.dma(k_o[out_l, g][t0 * 128:(t0 + 4) * 128, c * 128:(c + 1) * 128].rearrange("(t p) e -> p t e", p=128),
                                  kst.rearrange("p (t e) -> p t e", t=4))
                    bi = 0
                    for r in range(d):
                        for jb in range(nb):
                            qs = r + d * 128 * jb
                            sl = lambda s_: slice(s_, s_ + 127 * d + 1, d)
                            pss = ps()
                            for hh in range(2):
                                S.mm(pss[:, hh * 256:hh * 256 + 128], KTb[:, sl(Lh + qs - 128 * d)], QT[:, hh, sl(qs)])
                                S.mm(pss[:, hh * 256 + 128:hh * 256 + 256], KTb[:, sl(Lh + qs)], QT[:, hh, sl(qs)])
                            Pt = Pt2[bi % 2]; bi += 1
                            S.act(Pt, pss[:], ACTF.Exp, scale=SCALE)
                            S.tt(Pt, Pt, (Eh if jb == 0 else E), ALU.mult, eng="gpsimd")
                            t1 = d + r * nb + jb
                            th0 = r if jb == 0 else t1 - 1
                            pso = ps()
                            seq = [(hh, half) for hh in range(2) for half in range(2)]
                            for n_, (hh, half) in enumerate(seq):
                                S.mm(pso[:, 0:128], Vt[:, (th0 if half == 0 else t1), hh, :],
                                     Pt[:, hh * 256 + half * 128:hh * 256 + (half + 1) * 128], start=(n_ == 0), stop=(n_ == 3))
                            for n_, (hh, half) in enumerate(seq):
                                S.mm(pso[:, 128:256], ones2[:, hh, :],
                                     Pt[:, hh * 256 + half * 128:hh * 256 + (half + 1) * 128], start=(n_ == 0), stop=(n_ == 3))
                            dst = acc[:, :, sl(qs)]
                            src = pso[:, 0:256].rearrange("p (a q) -> p a q", a=2)
                            if g == 0:
                                S.copy(dst, src, eng="vector")
                            else:
                                S.tt(dst, dst, src, ALU.add)
                S.recip(acc[:, 1, :], acc[:, 1, :])
                S.tt(oT[2][:, c, :], acc[:, 0, :], acc[:, 1, :], ALU.mult)

            ckpt("%s:C" % pname)
            S.barrier()
            Wg = Vw(PL, [3, 8, 128])
            Wb = Vw(PL + 6 * KB, [3, 4, 128])
            macc = Vw(PL + 10 * KB, [512], F32)
            sgm = Vw(PL + 12 * KB, [512], F32)
            mtmp = Vw(PL + 14 * KB, [512], F32)
            Wo = Vw(PL + 16 * KB, [8, 1024])
            gpost = Vw(PL + 32 * KB, [1024], F32)
            wload(Wo, w_out[l], 8)
            bcast_load(gpost, g_post_mix[l:l + 1, :])
            for dc in range(8):
                for n in range(3):
                    wload(Wg[:, n, :, :], w_in[l][:, OFF_G + n * 1024 + dc * 128: OFF_G + n * 1024 + (dc + 1) * 128], 8)
                    wload(Wb[:, n, :, :], w_branch[l][n * 512:(n + 1) * 512, dc * 128:(dc + 1) * 128], 4)
                for w in range(4):
                    ws = slice(w * 512, (w + 1) * 512)
                    for n in range(3):
                        pg = ps(); pp = ps()
                        for k in range(8):
                            S.mm(pg[:], Wg[:, n, k, :], xnT_f[:, k, ws], start=(k == 0), stop=(k == 7))
                        for k in range(4):
                            S.mm(pp[:], Wb[:, n, k, :], oT[n][:, k, ws], start=(k == 0), stop=(k == 3))
                        S.act(sgm, pg[:], ACTF.Sigmoid)
                        if n == 0:
                            S.tt(macc, pp[:], sgm, ALU.mult)
                        else:
                            S.tt(mtmp, pp[:], sgm, ALU.mult)
                            if n == 1:
                                S.tt(macc, macc, mtmp, ALU.add, eng="gpsimd")
                            else:
                                S.tt(mergedT[:, dc, ws], macc, mtmp, ALU.add, eng="gpsimd")
            gpf = Vw(PL + 72 * KB, [1024], F32)
            bcast_load(gpf, g_pre_ffn[l:l + 1, :])

            def bodyM(i, sl_):
                TM = PL + 36 * KB + sl_ * 18 * KB
                ysb2 = Vw(TM, [1024], F32)
                junk2 = Vw(TM + 4 * KB, [1024], F32)
                xt = Vw(TM + 8 * KB, [1024], F32)
                njunk = Vw(TM + 12 * KB, [1024], F32)
                nxnb = Vw(TM + 16 * KB, [1024])
                S.dma(xt, xsrc[i * 128:(i + 1) * 128, :])
                py = [ps(), ps()]
                for cb in range(2):
                    for k in range(8):
                        S.mm(py[cb][:], mergedT[:, k, i * 128:(i + 1) * 128], Wo[:, k, cb * 512:(cb + 1) * 512],
                             start=(k == 0), stop=(k == 7))
                yield
                S.copy(ysb2[:, 0:512], py[0][:], eng="scalar")
                S.copy(ysb2[:, 512:1024], py[1][:], eng="vector")
                yield
                ss = yield from g_sumsq(ysb2, junk2)
                r = yield from g_rstd(ss, D)
                S.stt(ysb2, ysb2, r, gpost, ALU.mult, ALU.mult)
                yield
                S.tt(xt, xt, ysb2, ALU.add)
                yield
                S.dma(xmid[i * 128:(i + 1) * 128, :], xt)
                yield from g_norm_to_T(xt, gpf, xnT_f, i, njunk, nxnb)
            for i0 in range(0, 16, 2):
                lockstep([bodyM(i0, 0), bodyM(i0 + 1, 1)])

            ckpt("%s:M" % pname)
            S.barrier()
            actT = Vw(OT, [22, 1024])
            W2 = Vw(PL, [22, 1024])
            wload(W2, ffn_w_out[l], 22)
            W1 = [Vw(PL + 44 * KB + q * 4 * KB, [2, 8, 128]) for q in range(2)]
            sg = Vw(PL + 52 * KB, [512], F32)
            gpost2 = Vw(PL + 54 * KB, [1024], F32)
            bcast_load(gpost2, g_post_ffn[l:l + 1, :])
            ysb2 = Vw(PL + 58 * KB, [1024], F32)
            junk2 = Vw(PL + 62 * KB, [1024], F32)
            for hf in range(2):
                for fc in range(22):
                    Wc = W1[fc % 2]
                    wload(Wc[:, 0, :, :], ffn_w_in[l][:, fc * 128:(fc + 1) * 128], 8)
                    wload(Wc[:, 1, :, :], ffn_w_in[l][:, DFF + fc * 128:DFF + (fc + 1) * 128], 8)
                    for w in range(2):
                        ws = slice(hf * 1024 + w * 512, hf * 1024 + (w + 1) * 512)
                        pg = ps(); pu = ps()
                        for k in range(8):
                            S.mm(pg[:], Wc[:, 0, k, :], xnT_f[:, k, ws], start=(k == 0), stop=(k == 7))
                        for k in range(8):
                            S.mm(pu[:], Wc[:, 1, k, :], xnT_f[:, k, ws], start=(k == 0), stop=(k == 7))
                        S.act(sg, pg[:], ACTF.Silu)
                        S.tt(actT[:, fc, w * 512:(w + 1) * 512], pu[:], sg, ALU.mult)
                def bodyF(i8, sl_, hf=hf):
                    i = hf * 8 + i8
                    ysb_ = Vw(PL + 58 * KB + sl_ * 8 * KB, [1024], F32)
                    xt = Vw(PL + 62 * KB + sl_ * 8 * KB, [1024], F32)
                    junk_ = Vw(PL + 74 * KB, [1024])
                    S.dma(xt, xmid[i * 128:(i + 1) * 128, :])
                    py = [ps(), ps()]
                    for cb in range(2):
                        for k in range(22):
                            S.mm(py[cb][:], actT[:, k, i8 * 128:(i8 + 1) * 128], W2[:, k, cb * 512:(cb + 1) * 512],
                                 start=(k == 0), stop=(k == 21))
                    yield
                    S.copy(ysb_[:, 0:512], py[0][:], eng="scalar")
                    S.copy(ysb_[:, 512:1024], py[1][:], eng="vector")
                    yield
                    ss = yield from g_sumsq(ysb_, junk_)
                    r = yield from g_rstd(ss, D)
                    S.stt(ysb_, ysb_, r, gpost2, ALU.mult, ALU.mult)
                    yield
                    S.tt(xt, xt, ysb_, ALU.add)
                    yield
                    S.dma(xdst[i * 128:(i + 1) * 128, :], xt)
                    yield
                for i0 in range(0, 8, 2):
                    lockstep([bodyF(i0, 0), bodyF(i0 + 1, 1)])

        def run_sample(l):
            state["pass"] = "S%d" % l
            S.barrier()
            A0 = 0
            xnTs = Vw(A0, [8, 32])
            oTs = [Vw(A0 + 1 * KB + n * 256, [4, 32]) for n in range(3)]
            mergedTs = Vw(A0 + 2 * KB, [8, 32])
            actTs = Vw(A0 + 3 * KB, [22, 32])
            T0 = 8 * KB
            junk = Vw(T0, [1024], F32)
            ysb = Vw(T0 + 4 * KB, [1024], F32)
            gt_a = Vw(T0 + 8 * KB, [1024], F32)
            gt_b = Vw(T0 + 12 * KB, [1024], F32)
            xnb = Vw(T0 + 16 * KB, [1024])
            W0 = 64 * KB

            def norm32(src, gtile):
                ss = sumsq(src[0:32, :], junk[0:32, :], P=slice(0, 32))
                r = rstd_from_ss(ss, D, P=slice(0, 32))
                S.stt(xnb[0:32, :], src[0:32, :], r, gtile[0:32, :], ALU.mult, ALU.mult)
                pt = ps()[:].bitcast(BF16)
                for k in range(8):
                    S.tr(pt[:, k * 32:(k + 1) * 32], xnb[0:32, k * 128:(k + 1) * 128], identb[0:32, 0:32])
                S.copy(xnTs, pt[:, 0:256].rearrange("p (k c) -> p k c", k=8), eng="vector")

            def ln32(src, gt, bt, out, toff):
                jk = Vw(toff, [512], F32)
                tmp = Vw(toff + 2 * KB, [512], F32)
                P = slice(0, 32)
                sm = stc(); S.reduce(sm[P, :], src[P, :], ALU.add)
                sq = stc(); S.memset(sq[P, :], 0.0); S.act(jk[P, :], src[P, :], ACTF.Square, accum_out=sq[P, :])
                mean = stc(); S.ts(mean[P, :], sm[P, :], 1.0 / 512, 0.0, ALU.mult, ALU.add)
                msq = stc(); S.tt(msq[P, :], mean[P, :], mean[P, :], ALU.mult)
                var = stc(); S.stt(var[P, :], sq[P, :], 1.0 / 512, msq[P, :], ALU.mult, ALU.subtract)
                S.ts(var[P, :], var[P, :], 1.0, EPS, ALU.mult, ALU.add)
                S.act(var[P, :], var[P, :], ACTF.Sqrt)
                r = stc(); S.recip(r[P, :], var[P, :])
                S.ts(tmp[P, :], src[P, :], mean[P, :], r[P, :], ALU.subtract, ALU.mult)
                S.tt(tmp[P, :], tmp[P, :], gt[P, :], ALU.mult)
                S.tt(out[P, :], tmp[P, :], bt[P, :], ALU.add)

            def featT32(src_bf, dstT):
                pt = ps()[:].bitcast(BF16)
                for c in range(4):
                    S.tr(pt[:, c * 32:(c + 1) * 32], src_bf[0:32, c * 128:(c + 1) * 128], identb[0:32, 0:32])
                S.copy(dstT, pt[:, 0:128].rearrange("p (c t) -> p c t", c=4), eng="vector")

            P32 = slice(0, 32)
            bcast_load(gt_a, g_pre_mix[l:l + 1, :])
            if l == 0:
                S.dma(xs_res[:], xs_d)
            norm32(xs_res, gt_a)

            WA = Vw(W0, [8, 1024])
            wload(WA, w_in[l][:, OFF_AU:OFF_AU + 1024], 8)
            agt = Vw(T0 + 18 * KB, [512], F32); abt = Vw(T0 + 20 * KB, [512], F32)
            bcast_load(agt, a_norm_g[l:l + 1, :]); bcast_load(abt, a_norm_b[l:l + 1, :])
            BDf = Vw(T0 + 22 * KB, [4, 32], F32)
            BDb = Vw(T0 + 23 * KB, [4, 32])
            S.memset(BDf[P32], 0.0)
            for b in range(4):
                for g in range(4):
                    S.dma(BDf[b * 8:(b + 1) * 8, g, b * 8:(b + 1) * 8], a_w_s[l, g, 0:8, 0:8].rearrange("t s -> s t"),
                          allow_slow_non_contiguous=True)
                S.dma(bs32[b * 8:(b + 1) * 8, :], a_b_s[l][:, 0:8].rearrange("g t -> t g"), allow_slow_non_contiguous=True)
            for g in range(4):
                S.tt(BDb[P32, g, :], BDf[P32, g, :], bdm[:], ALU.mult)
            u_sb = Vw(T0 + 24 * KB, [512], F32); v_sb = Vw(T0 + 26 * KB, [512], F32)
            vn32 = Vw(T0 + 28 * KB, [512], F32); vnb = Vw(T0 + 30 * KB, [512]); oab = Vw(T0 + 31 * KB, [512])
            pm_sb = Vw(T0 + 32 * KB, [512], F32)
            pu = ps(); pv = ps()
            for k in range(8):
                S.mm(pu[P32, :], xnTs[:, k, :], WA[:, k, 0:512], start=(k == 0), stop=(k == 7))
            for k in range(8):
                S.mm(pv[P32, :], xnTs[:, k, :], WA[:, k, 512:1024], start=(k == 0), stop=(k == 7))
            S.copy(u_sb[P32], pu[P32, :], eng="scalar")
            S.copy(v_sb[P32], pv[P32, :], eng="vector")
            ln32(v_sb, agt, abt, vn32, T0 + 34 * KB)
            S.dma(nav_o[l], vn32[P32])
            S.copy(vnb[P32], vn32[P32], eng="vector")
            pm = ps()
            for g in range(4):
                S.mm(pm[P32, g * 128:(g + 1) * 128], BDb[P32, g, :], vnb[P32, g * 128:(g + 1) * 128])
            S.copy(pm_sb[P32], pm[P32, :], eng="scalar")
            for g in range(4):
                S.stt(oab[P32, g * 128:(g + 1) * 128], pm_sb[P32, g * 128:(g + 1) * 128], bs32[:, g:g + 1],
                      u_sb[P32, g * 128:(g + 1) * 128], ALU.add, ALU.mult)
            featT32(oab, oTs[0])

            S.barrier()
            WB = Vw(W0, [8, 1024])
            wload(WB, w_in[l][:, OFF_B:OFF_B + 1024], 8)
            diag = Vw(W0 + 16 * KB, [124, 128])
            cwT = Vw(T0 + 18 * KB, [4, 32], F32)
            for j in range(31):
                S.dma(cwT[:, :, j], b_conv_w[l, j].rearrange("(c p) -> p c", p=128), allow_slow_non_contiguous=True)
            for c in range(4):
                for j in range(31):
                    S.ts(diag[:, c * 31 + j, :], identb[:], cwT[:, c, j:j + 1], 1.0, ALU.mult, ALU.mult,
                         eng=("vector" if j % 2 == 0 else "gpsimd"))
            bgt = Vw(T0 + 20 * KB, [512], F32); bbt = Vw(T0 + 22 * KB, [512], F32)
            bcast_load(bgt, b_norm_g[l:l + 1, :]); bcast_load(bbt, b_norm_b[l:l + 1, :])
            cbT = Vw(T0 + 19 * KB, [4], F32)
            S.dma(cbT, b_conv_b[l].rearrange("(c p) -> p c", p=128), allow_slow_non_contiguous=True)
            pa = ps(); pg = ps()
            for k in range(8):
                S.mm(pa[P32, :], xnTs[:, k, :], WB[:, k, 0:512], start=(k == 0), stop=(k == 7))
            for k in range(8):
                S.mm(pg[P32, :], xnTs[:, k, :], WB[:, k, 512:1024], start=(k == 0), stop=(k == 7))
            sig = Vw(T0 + 24 * KB, [512], F32); glut = Vw(T0 + 26 * KB, [512], F32)
            S.act(sig[P32], pg[P32, :], ACTF.Sigmoid)
            S.tt(glut[P32], pa[P32, :], sig[P32], ALU.mult)
            for b in range(4):
                S.dma(nbs_o[l][b, 22:30, :], glut[b * 8:(b + 1) * 8, :])
            S.dma(nbs_o[l][:, 0:22, :], st_d[l][:, 8:30, :])
            padT = Vw(T0 + 28 * KB, [4, 4, 38])
            gl_f = Vw(T0 + 30 * KB, [4, 32], F32)
            sgf = Vw(T0 + 31 * KB, [32], F32)
            for c in range(4):
                pa = ps(); pg = ps()
                for k in range(8):
                    S.mm(pa[:, 0:32], WB[:, k, c * 128:(c + 1) * 128], xnTs[:, k, :], start=(k == 0), stop=(k == 7))
                for k in range(8):
                    S.mm(pg[:, 0:32], WB[:, k, 512 + c * 128:512 + (c + 1) * 128], xnTs[:, k, :], start=(k == 0), stop=(k == 7))
                S.act(sgf, pg[:, 0:32], ACTF.Sigmoid)
                S.tt(padT[:, c, :, 30:38], pa[:, 0:32].rearrange("p (b t) -> p b t", b=4),
                     sgf.rearrange("p (b t) -> p b t", b=4), ALU.mult)
            stf = Vw(T0 + 32 * KB, [4, 512], F32)
            stb = Vw(T0 + 40 * KB, [4, 512])
            S.dma(stf[0:30], st_d[l].rearrange("b j c -> j b c"))
            S.copy(stb[0:30], stf[0:30], eng="vector")
            pt = ps()[:].bitcast(BF16)
            for b in range(4):
                for c in range(4):
                    S.tr(pt[:, (b * 4 + c) * 32:(b * 4 + c) * 32 + 30], stb[0:30, b, c * 128:(c + 1) * 128], identb[0:30, 0:30])
            S.copy(padT[:, :, :, 0:30], pt[:, 0:512].rearrange("p (b c j) -> p c b j", b=4, c=4)[:, :, :, 0:30], eng="vector")
            pc = ps()
            for c in range(4):
                for j in range(31):
                    S.mm(pc[:, c * 32:(c + 1) * 32], diag[:, c * 31 + j, :], padT[:, c, :, j:j + 8],
                         start=(j == 0), stop=(j == 30))
            ycT = Vw(T0 + 44 * KB, [4, 32])
            for c in range(4):
                S.copy(sgf, pc[:, c * 32:(c + 1) * 32], eng="scalar")
                S.ts(ycT[:, c, :], sgf, cbT[:, c:c + 1], 1.0, ALU.add, ALU.mult)
            pt = ps()[:].bitcast(BF16)
            for c in range(4):
                S.tr(pt[P32, c * 128:(c + 1) * 128], ycT[:, c, :], identb[:])
            yc = Vw(T0 + 24 * KB, [512], F32)
            S.copy(yc[P32], pt[P32, 0:512], eng="vector")
            lnout = Vw(T0 + 26 * KB, [512], F32)
            ln32(yc, bgt, bbt, lnout, T0 + 34 * KB)
            obb = Vw(T0 + 45 * KB, [512])
            S.act(obb[P32], lnout[P32], ACTF.Silu)
            featT32(obb, oTs[1])

            S.barrier()
            WQ = Vw(W0, [8, 512]); WK = Vw(W0 + 8 * KB, [8, 512]); WV = Vw(W0 + 16 * KB, [8, 512])
            QTs = Vw(T0 + 18 * KB, [4, 2, 32])
            KTs = Vw(T0 + 19 * KB, [4, 32])
            accS = Vw(T0 + 20 * KB, [4, 2, 32], F32)
            Kc = Vw(T0 + 22 * KB, [512]); Vc = Vw(T0 + 23 * KB, [512])
            KcT = Vw(T0 + 24 * KB, [4, 128])
            Vc2 = Vw(T0 + 25 * KB, [4, 2, 128])
            Vn2 = Vw(T0 + 27 * KB, [4, 2, 128])
            P0 = Vw(T0 + 29 * KB, [16]); P1 = Vw(T0 + 29 * KB + 64, [16])
            kvst = Vw(T0 + 30 * KB, [512], F32)
            S.memset(Vc2.rearrange("p a b c -> p (a b c)"), 0.0, eng="gpsimd")
            S.memset(Vn2.rearrange("p a b c -> p (a b c)"), 0.0, eng="gpsimd")
            S.memset(QTs.rearrange("p a b c -> p (a b c)"), 0.0, eng="gpsimd")
            caches = ((c0k, c0v), (c1k, c1v), (c2k, c2v))
            for g in range(3):
                d = DIL[g]
                wload(WQ, w_in[l][:, OFF_CQ + g * 512:OFF_CQ + (g + 1) * 512], 8)
                wload(WK, w_in[l][:, OFF_CK + g * 512:OFF_CK + (g + 1) * 512], 8)
                wload(WV, w_in[l][:, OFF_CV + g * 512:OFF_CV + (g + 1) * 512], 8)
                for W_, dst_o in ((WK, ks_o), (WV, vs_o)):
                    pk = ps()
                    for k in range(8):
                        S.mm(pk[P32, :], xnTs[:, k, :], W_[:, k, :], start=(k == 0), stop=(k == 7))
                    S.copy(kvst[P32], pk[P32, :], eng="scalar")
                    S.dma(dst_o[l, g], kvst[P32])
                for c in range(4):
                    pq = ps()
                    for k in range(8):
                        S.mm(pq[:, 0:32], WQ[:, k, c * 128:(c + 1) * 128], xnTs[:, k, :], start=(k == 0), stop=(k == 7))
                    for k in range(8):
                        S.mm(pq[:, 32:64], WK[:, k, c * 128:(c + 1) * 128], xnTs[:, k, :], start=(k == 0), stop=(k == 7))
                    S.copy(QTs[0:64, c, 0, :], pq[0:64, 0:32], eng="vector")
                    S.copy(QTs[64:128, c, 1, :], pq[64:128, 0:32], eng="vector")
                    S.copy(KTs[:, c, :], pq[:, 32:64], eng="vector")
                for b in range(4):
                    for rho in range(min(d, 8)):
                        toks = list(range(rho, 8, d))
                        nq = len(toks)
                        tsl = slice(b * 8 + rho, b * 8 + rho + (nq - 1) * d + 1, d)
                        ck, cv = caches[g]
                        S.dma(Kc, ck[l, b][rho:rho + 127 * d + 1:d, :], eng="gpsimd")
                        S.dma(Vc, cv[l, b][rho:rho + 127 * d + 1:d, :], eng="gpsimd")
                        pt = ps()[:].bitcast(BF16)
                        for c in range(4):
                            S.tr(pt[:, c * 128:(c + 1) * 128], Kc[:, c * 128:(c + 1) * 128], identb[:])
                        S.copy(KcT, pt[:, 0:512].rearrange("p (c k) -> p c k", c=4), eng="vector")
                        Vc3 = Vc.rearrange("p (c e) -> p c e", c=4)
                        S.copy(Vc2[:, :, 0, 0:64], Vc3[:, :, 0:64], eng="vector")
                        S.copy(Vc2[:, :, 1, 64:128], Vc3[:, :, 64:128], eng="gpsimd")
                        pvn = ps()
                        for k in range(8):
                            S.mm(pvn[0:nq, :], xnTs[:, k, tsl], WV[:, k, :], start=(k == 0), stop=(k == 7))
                        pv3 = pvn[0:nq, :].rearrange("p (c e) -> p c e", c=4)
                        S.copy(Vn2[0:nq, :, 0, 0:64], pv3[:, :, 0:64], eng="vector")
                        S.copy(Vn2[0:nq, :, 1, 64:128], pv3[:, :, 64:128], eng="vector")
                        for c in range(4):
                            E = etab[:, g * 4 + c, :]
                            pss = ps()
                            for hh in range(2):
                                S.mm(pss[:, hh * 8:hh * 8 + nq], KcT[:, c, :], QTs[:, c, hh, tsl])
                            for hh in range(2):
                                S.mm(pss[0:nq, 16 + hh * 8:16 + hh * 8 + nq], KTs[:, c, tsl], QTs[:, c, hh, tsl])
                            S.act(P0, pss[:, 0:16], ACTF.Exp, scale=SCALE)
                            S.act(P1[0:nq], pss[0:nq, 16:32], ACTF.Exp, scale=SCALE)
                            for hh in range(2):
                                S.tt(P0[:, hh * 8:hh * 8 + nq], P0[:, hh * 8:hh * 8 + nq], E[:, hh * 256:hh * 256 + nq], ALU.mult)
                                S.tt(P1[0:nq, hh * 8:hh * 8 + nq], P1[0:nq, hh * 8:hh * 8 + nq],
                                     E[0:nq, hh * 256 + 128:hh * 256 + 128 + nq], ALU.mult)
                            pso = ps()
                            for which, col in ((0, 0), (1, 8)):
                                n_ = 0
                                for hh in range(2):
                                    lh0 = Vc2[:, c, hh, :] if which == 0 else ones2[:, hh, :]
                                    lh1 = Vn2[0:nq, c, hh, :] if which == 0 else ones2[0:nq, hh, :]
                                    S.mm(pso[:, col:col + nq], lh0, P0[:, hh * 8:hh * 8 + nq], start=(n_ == 0), stop=False)
                                    n_ += 1
                                    S.mm(pso[:, col:col + nq], lh1, P1[0:nq, hh * 8:hh * 8 + nq], start=False, stop=(hh == 1))
                            dst = accS[:, c, :, tsl]
                            src = pso[:, 0:16].rearrange("p (a q) -> p a q", a=2)[:, :, 0:nq]
                            if g == 0:
                                S.copy(dst, src, eng="vector")
                            else:
                                S.tt(dst, dst, src, ALU.add)
            S.recip(accS[:, :, 1, :], accS[:, :, 1, :])
            S.tt(oTs[2], accS[:, :, 0, :], accS[:, :, 1, :], ALU.mult)

            S.barrier()
            Wg = Vw(W0, [3, 8, 128]); Wb = Vw(W0 + 6 * KB, [3, 4, 128])
            Wo = Vw(W0 + 16 * KB, [8, 1024])
            macc = Vw(T0 + 18 * KB, [32], F32); sgm = Vw(T0 + 18 * KB + 128, [32], F32); mtmp = Vw(T0 + 18 * KB + 256, [32], F32)
            wload(Wo, w_out[l], 8)
            bcast_load(gt_a, g_post_mix[l:l + 1, :])
            bcast_load(gt_b, g_pre_ffn[l:l + 1, :])
            for dc in range(8):
                for n in range(3):
                    wload(Wg[:, n, :, :], w_in[l][:, OFF_G + n * 1024 + dc * 128: OFF_G + n * 1024 + (dc + 1) * 128], 8)
                    wload(Wb[:, n, :, :], w_branch[l][n * 512:(n + 1) * 512, dc * 128:(dc + 1) * 128], 4)
                for n in range(3):
                    pg = ps(); pp = ps()
                    for k in range(8):
                        S.mm(pg[:, 0:32], Wg[:, n, k, :], xnTs[:, k, :], start=(k == 0), stop=(k == 7))
                    for k in range(4):
                        S.mm(pp[:, 0:32], Wb[:, n, k, :], oTs[n][:, k, :], start=(k == 0), stop=(k == 3))
                    S.act(sgm, pg[:, 0:32], ACTF.Sigmoid)
                    if n == 0:
                        S.tt(macc, pp[:, 0:32], sgm, ALU.mult)
                    else:
                        S.tt(mtmp, pp[:, 0:32], sgm, ALU.mult)
                        if n == 1:
                            S.tt(macc, macc, mtmp, ALU.add)
                        else:
                            S.tt(mergedTs[:, dc, :], macc, mtmp, ALU.add)

            def post32(py, gp):
                S.copy(ysb[P32, 0:512], py[0][P32, :], eng="scalar")
                S.copy(ysb[P32, 512:1024], py[1][P32, :], eng="vector")
                ss = sumsq(ysb[P32], junk[P32], P=P32)
                r = rstd_from_ss(ss, D, P=P32)
                S.stt(ysb[P32], ysb[P32], r, gp[P32], ALU.mult, ALU.mult)
                S.tt(xs_res[:], xs_res[:], ysb[P32], ALU.add)

            py = [ps(), ps()]
            for cb in range(2):
                for k in range(8):
                    S.mm(py[cb][P32, :], mergedTs[:, k, :], Wo[:, k, cb * 512:(cb + 1) * 512], start=(k == 0), stop=(k == 7))
            post32(py, gt_a)
            norm32(xs_res, gt_b)

            S.barrier()
            W2 = Vw(W0, [22, 1024])
            wload(W2, ffn_w_out[l], 22)
            W1 = [Vw(W0 + 44 * KB + q * 4 * KB, [2, 8, 128]) for q in range(2)]
            bcast_load(gt_a, g_post_ffn[l:l + 1, :])
            sg = Vw(T0 + 18 * KB, [32], F32)
            for fc in range(22):
                Wc = W1[fc % 2]
                wload(Wc[:, 0, :, :], ffn_w_in[l][:, fc * 128:(fc + 1) * 128], 8)
                wload(Wc[:, 1, :, :], ffn_w_in[l][:, DFF + fc * 128:DFF + (fc + 1) * 128], 8)
                pg = ps(); pu = ps()
                for k in range(8):
                    S.mm(pg[:, 0:32], Wc[:, 0, k, :], xnTs[:, k, :], start=(k == 0), stop=(k == 7))
                for k in range(8):
                    S.mm(pu[:, 0:32], Wc[:, 1, k, :], xnTs[:, k, :], start=(k == 0), stop=(k == 7))
                S.act(sg, pg[:, 0:32], ACTF.Silu)
                S.tt(actTs[:, fc, :], pu[:, 0:32], sg, ALU.mult)
            py = [ps(), ps()]
            for cb in range(2):
                for k in range(22):
                    S.mm(py[cb][P32, :], actTs[:, k, :], W2[:, k, cb * 512:(cb + 1) * 512], start=(k == 0), stop=(k == 21))
            post32(py, gt_a)
            if l == 1:
                S.dma(ys_o, xs_res[:])


        try:
            if not skip_sample:
                run_sample(0)
                ckpt("S0")
                run_sample(1)
                ckpt("S1")
            if sample_only:
                raise _Stop()
            run_pass("A", 0, xw[2048:4096, :], xw[0:2048, :], x1s[0:2048, :], 0, None)
            ckpt("A:F")
            run_pass("B", 0, xw[4096:6144, :], xw[2048:4096, :], x1s[2048:4096, :], 1, 0)
            ckpt("B:F")
            run_pass("C", 1, x1s[2048:4096, :], x1s[0:2048, :], y_o, 2, 1)
        except _Stop:
            pass
        if dbg:
            S.barrier()
            S.dma(dbg_o, arena[:])
        S.emit(st)
    return nc


def _etab():
    e = np.zeros((12, 128, 512), np.float32)
    kk = np.arange(128)[:, None].astype(np.float64)
    qq = np.arange(128)[None, :].astype(np.float64)
    for g in range(3):
        for c in range(4):
            for hh in range(2):
                j = 2 * c + hh
                slope = 2.0 ** (-8.0 * (j * 3 + g + 1.0) / 24.0)
                for half in range(2):
                    step = qq + 128 - kk if half == 0 else qq - kk
                    val = np.exp(-slope * DIL[g] * step)
                    val = np.where((step >= 0) & (step <= 128), val, 0.0)
                    e[g * 4 + c, :, hh * 256 + half * 128: hh * 256 + (half + 1) * 128] = val
    return e


_NC_CACHE = {}


def kernel(**inp):
    f = lambda k: np.ascontiguousarray(np.asarray(inp[k], dtype=np.float32))
    xp = f("x_prompt")
    if "nc" not in _NC_CACHE:
        _NC_CACHE["nc"] = build_nc()
    nc = _NC_CACHE["nc"]
    wnames = ["norm_pre_mix", "norm_post_mix", "norm_pre_ffn", "norm_post_ffn", "w_in", "a_norm_g", "a_norm_b",
              "a_w_s", "a_b_s", "b_conv_w", "b_conv_b", "b_norm_g", "b_norm_b", "w_branch", "w_out", "ffn_w_in",
              "ffn_w_out"]
    shared = {k: f(k) for k in wnames}
    shared["etab"] = _etab()
    shared["ident"] = np.eye(128, dtype=np.float32)
    shared["tril"] = np.tril(np.ones((128, 128), np.float32))
    o2 = np.zeros((128, 2, 128), np.float32)
    o2[:, 0, 0:64] = 1.0
    o2[:, 1, 64:128] = 1.0
    shared["ones2"] = o2.reshape(128, 256)
    bdm = np.zeros((32, 32), np.float32)
    for b_ in range(4):
        for s_ in range(8):
            for t_ in range(s_, 8):
                bdm[b_ * 8 + s_, b_ * 8 + t_] = 1.0
    shared["bdm"] = bdm
    xs_all = f("x_sample"); st_all = f("state_b_conv")
    cch = [f(k) for k in ("cache_c0_k", "cache_c0_v", "cache_c1_k", "cache_c1_v", "cache_c2_k", "cache_c2_v")]
    in_maps = []
    for c in range(8):
        b, seg = c // 4, (c % 4) * 2048
        xw = np.zeros((6144, 1024), np.float32)
        lo = seg - 4096
        s0 = max(lo, 0)
        xw[s0 - lo:] = xp[b, s0:seg + 2048]
        fl = np.zeros((128, 4), np.float32)
        fl[:, 0] = 1.0 if seg >= 4096 else 0.0
        fl[:, 1] = 1.0 if seg >= 2048 else 0.0
        fl[:, 2] = 1.0 if seg >= 2048 else 0.0
        m = dict(shared)
        m["xw"] = xw
        m["flags"] = fl
        bs = slice(c * 4, (c + 1) * 4)
        m["xs"] = np.ascontiguousarray(xs_all[bs].reshape(32, 1024))
        m["st"] = np.ascontiguousarray(st_all[:, bs])
        for nm, arr in zip(("c0k", "c0v", "c1k", "c1v", "c2k", "c2v"), cch):
            m[nm] = np.ascontiguousarray(arr[:, bs].reshape(2, 4, arr.shape[2], 512))
        in_maps.append(m)
    res = run_bass_kernel_spmd(nc, in_maps, core_ids=list(range(8)))
    R = res.results
    y_prompt = np.stack([np.concatenate([R[b * 4 + i]["y_o"] for i in range(4)], axis=0) for b in range(2)], 0)
    gt = np.stack([R[b * 4 + 3]["gt_o"] for b in range(2)], 1)
    new_b_conv_prompt = np.ascontiguousarray(np.transpose(gt, (0, 1, 3, 2))[:, :, 2:, :])
    ko = np.stack([R[b * 4 + 3]["k_o"] for b in range(2)], 2)
    vo = np.stack([R[b * 4 + 3]["v_o"] for b in range(2)], 2)
    kvp = []
    for g, wlen in enumerate((128, 512, 2048)):
        kvp.append(np.ascontiguousarray(ko[:, g, :, NT - wlen:, :]).reshape(2, 2, wlen, 8, 64))
        kvp.append(np.ascontiguousarray(vo[:, g, :, NT - wlen:, :]).reshape(2, 2, wlen, 8, 64))
    y_sample = np.concatenate([R[c]["ys_o"].reshape(4, 8, 1024) for c in range(8)], 0)
    new_b_conv_sample = np.concatenate([R[c]["nbs_o"] for c in range(8)], 1)
    new_a_v_sample = np.concatenate([R[c]["nav_o"].reshape(2, 4, 8, 512) for c in range(8)], 1)
    kvs = []
    for g in range(3):
        kvs.append(np.concatenate([R[c]["ks_o"][:, g].reshape(2, 4, 8, 8, 64) for c in range(8)], 1))
        kvs.append(np.concatenate([R[c]["vs_o"][:, g].reshape(2, 4, 8, 8, 64) for c in range(8)], 1))
    return (y_prompt, y_sample, new_b_conv_prompt, new_b_conv_sample, new_a_v_sample, *kvp, *kvs)
```

```python
import numpy as np
from concourse.bass_utils import run_bass_kernel_spmd
import concourse.bass as bass
import concourse.mybir as mybir

F32 = mybir.dt.float32
BF16 = mybir.dt.bfloat16
ALU = mybir.AluOpType
ACTF = mybir.ActivationFunctionType
AX = mybir.AxisListType

_DSZ = {F32: 4, BF16: 2, mybir.dt.int32: 4, mybir.dt.float32r: 4}


def _region(ap):
    t = ap.tensor
    name = t.name
    dsz = _DSZ.get(ap.dtype, 4)
    dims = list(ap.ap)
    off = int(ap.offset)
    space = str(ap.space)
    if space in ("SB", "PSUM"):
        pstep, pcnt = dims[0]
        if pstep == 0:
            pstep = 1 << 40
        p0 = off // pstep if pstep < (1 << 40) else 0
        f0 = off - p0 * pstep if pstep < (1 << 40) else off
        p1 = p0 + pcnt
        rest = dims[1:]
    else:
        p0, p1 = 0, 1
        f0 = off
        rest = dims
    lo = f0
    hi = f0
    for st, cn in rest:
        if cn <= 0:
            continue
        d = st * (cn - 1)
        if d < 0:
            lo += d
        else:
            hi += d
    return name, p0, p1, lo * dsz, (hi + 1) * dsz


class Sched:
    ENGS = ("tensor", "vector", "scalar", "gpsimd", "sync")

    def __init__(self, nc, n_dma_sems=24):
        self.nc = nc
        self.ops = []
        self.recs = {}
        self.n_dma_sems = n_dma_sems
        self.dma_count = {e: 0 for e in self.ENGS}
        self.dma_hist = {e: [] for e in self.ENGS}
        self.barrier_deps = {e: set() for e in self.ENGS}
        self.last = {e: None for e in self.ENGS}
        self.all_dmas = []

    def _access(self, ap, opid, is_write, deps):
        if str(ap.space) == "PSUM":
            name = ap.tensor.name
            eng = self.ops[opid]["eng"]
            rec = self.recs.setdefault(name, {})
            for e2, (last_any, last_w) in rec.items():
                if e2 != eng:
                    if last_any is not None and last_any != opid:
                        deps.add(last_any)
                else:
                    if is_write:
                        if last_any is not None and last_any != opid:
                            deps.add(last_any)
                    elif last_w is not None and last_w != opid:
                        deps.add(last_w)
            la, lw = rec.get(eng, (None, None))
            rec[eng] = (opid, opid if is_write else lw)
            return
        name, p0, p1, lo, hi = _region(ap)
        lst = self.recs.setdefault(name, [])
        keep = []
        eng = self.ops[opid]["eng"]
        isdma = self.ops[opid]["dma"]
        for r in lst:
            ov = not (r[1] <= p0 or p1 <= r[0] or r[3] <= lo or hi <= r[2])
            if ov and r[4] != opid:
                if is_write or r[5]:
                    deps.add(r[4])
                if is_write and r[0] >= p0 and r[1] <= p1 and r[2] >= lo and r[3] <= hi:
                    continue
            if (not is_write) and (not r[5]) and (not isdma) and r[4] != opid:
                ro = self.ops[r[4]]
                if ro["eng"] == eng and not ro["dma"] and r[0] == p0 and r[1] == p1 and r[2] == lo and r[3] == hi:
                    continue
            keep.append(r)
        keep.append([p0, p1, lo, hi, opid, is_write])
        self.recs[name] = keep

    def op(self, eng, fn, outs=(), ins=(), dma=False):
        opid = len(self.ops)
        o = {"eng": eng, "fn": fn, "deps": set(), "dma": dma}
        self.ops.append(o)
        deps = o["deps"]
        for a in ins:
            self._access(a, opid, False, deps)
        for a in outs:
            self._access(a, opid, True, deps)
        if self.barrier_deps[eng]:
            deps |= self.barrier_deps[eng]
            self.barrier_deps[eng] = set()
        if dma:
            h = self.dma_hist[eng]
            if len(h) >= self.n_dma_sems:
                deps.add(h[-self.n_dma_sems])
            h.append(opid)
            self.all_dmas.append(opid)
        self.last[eng] = opid
        return opid

    def barrier(self):
        d = set(x for x in self.last.values() if x is not None)
        d |= set(self.all_dmas[-64:])
        for e in self.ENGS:
            self.barrier_deps[e] = set(d)

    def dma(self, out, in_, eng="sync", **kw):
        return self.op(eng, lambda e: e.dma_start(out=out, in_=in_, **kw), [out], [in_], dma=True)

    def mm(self, out, lhsT, rhs, start=True, stop=True, **kw):
        return self.op("tensor", lambda e: e.matmul(out, lhsT, rhs, start=start, stop=stop, **kw),
                       [out], [lhsT, rhs])

    def tr(self, out, in_, ident):
        return self.op("tensor", lambda e: e.transpose(out, in_, ident), [out], [in_, ident])

    def act(self, out, in_, func, bias=None, scale=None, accum_out=None, eng="scalar"):
        kw = {}
        ins = [in_]
        outs = [out]
        if bias is not None:
            kw["bias"] = bias
            if not isinstance(bias, (int, float)):
                ins.append(bias)
        if scale is not None:
            kw["scale"] = scale
            if not isinstance(scale, (int, float)):
                ins.append(scale)
        if accum_out is not None:
            kw["accum_out"] = accum_out
            outs.append(accum_out)
        return self.op(eng, lambda e: e.activation(out, in_, func, **kw), outs, ins)

    def tt(self, out, in0, in1, op, eng="vector"):
        return self.op(eng, lambda e: e.tensor_tensor(out, in0, in1, op), [out], [in0, in1])

    def ts(self, out, in0, s1, s2, op0, op1=None, eng="vector", accum_out=None):
        ins = [in0] + [s for s in (s1, s2) if s is not None and not isinstance(s, (int, float))]
        outs = [out] + ([accum_out] if accum_out is not None else [])
        if op1 is None:
            return self.op(eng, lambda e: e.tensor_scalar(out, in0, s1, s2, op0), outs, ins)
        if accum_out is not None:
            return self.op(eng, lambda e: e.tensor_scalar(out, in0, s1, s2, op0, op1, accum_out), outs, ins)
        return self.op(eng, lambda e: e.tensor_scalar(out, in0, s1, s2, op0, op1), outs, ins)

    def stt(self, out, in0, scalar, in1, op0, op1, eng="vector"):
        ins = [in0, in1] + ([scalar] if not isinstance(scalar, (int, float)) else [])
        return self.op(eng, lambda e: e.scalar_tensor_tensor(out, in0, scalar, in1, op0, op1), [out], ins)

    def copy(self, out, in_, eng="vector"):
        if eng == "scalar":
            return self.op(eng, lambda e: e.copy(out, in_), [out], [in_])
        return self.op(eng, lambda e: e.tensor_copy(out, in_), [out], [in_])

    def memset(self, ap, val, eng="vector"):
        return self.op(eng, lambda e: e.memset(ap, val), [ap], [])

    def reduce(self, out, in_, op, axis=AX.X, eng="vector"):
        return self.op(eng, lambda e: e.tensor_reduce(out, in_, axis, op), [out], [in_])

    def recip(self, out, in_):
        return self.op("vector", lambda e: e.reciprocal(out, in_), [out], [in_])

    def emit(self, stack):
        nc = self.nc
        ops = self.ops
        needed = set()
        for o in ops:
            for d in o["deps"]:
                do = ops[d]
                if o["eng"] == "tensor" and do["eng"] == "tensor" and not do["dma"] and not o["dma"]:
                    continue
                needed.add(d)
        final_dmas = list(self.all_dmas)
        eng_sem = {e: stack.enter_context(nc.semaphore("se_" + e)) for e in self.ENGS}
        dma_sems = {e: [stack.enter_context(nc.semaphore("sd_%s_%d" % (e, i)))
                        for i in range(self.n_dma_sems)]
                    for e in self.ENGS if self.dma_count is not None and any(
                        (o["dma"] and o["eng"] == e) for o in ops)}
        cnt = {e: 0 for e in self.ENGS}
        dcount = {e: 0 for e in self.ENGS}
        dsemcnt = {}
        sig = {}
        per_eng = {e: [] for e in self.ENGS}
        for i, o in enumerate(ops):
            e = o["eng"]
            per_eng[e].append(i)
            if o["dma"]:
                k = dcount[e] % self.n_dma_sems
                dcount[e] += 1
                s = dma_sems[e][k]
                dsemcnt[(e, k)] = dsemcnt.get((e, k), 0) + 16
                sig[i] = (s, dsemcnt[(e, k)])
            elif i in needed:
                cnt[e] += 1
                sig[i] = (eng_sem[e], cnt[e])
        self.sig = sig

        plan = {e: [] for e in self.ENGS}
        for ename in self.ENGS:
            seen = {}
            for i in per_eng[ename]:
                o = ops[i]
                waits = []
                for d in sorted(o["deps"]):
                    do = ops[d]
                    if (not do["dma"]) and do["eng"] == ename and (ename == "tensor"):
                        continue
                    s_, v = sig[d]
                    if seen.get(s_.name, 0) >= v:
                        continue
                    seen[s_.name] = v
                    waits.append((s_.name, v))
                plan[ename].append((i, waits, (sig[i][0].name, 16 if o["dma"] else 1) if i in sig else None))
        semv = {}
        pc = {e: 0 for e in self.ENGS}
        progress = True
        while progress:
            progress = False
            for e in self.ENGS:
                while pc[e] < len(plan[e]):
                    i, waits, sg = plan[e][pc[e]]
                    if all(semv.get(n, 0) >= v for n, v in waits):
                        if sg is not None:
                            semv[sg[0]] = semv.get(sg[0], 0) + sg[1]
                        pc[e] += 1
                        progress = True
                    else:
                        break
        stuck = {e: (pc[e], len(plan[e])) for e in self.ENGS if pc[e] < len(plan[e])}
        if stuck:
            for e in stuck:
                i, waits, sg = plan[e][pc[e]]
                print("DEADLOCK", e, "op", i, "waits", [(n, v, semv.get(n, 0)) for n, v in waits])
            raise RuntimeError("scheduler deadlock: %s" % stuck)
        self.max_sem = dict(semv)

        block = stack.enter_context(nc.Block())

        def make(ename):
            def body(eh):
                seen = {}
                for i in per_eng[ename]:
                    o = ops[i]
                    for d in sorted(o["deps"]):
                        do = ops[d]
                        if (not do["dma"]) and do["eng"] == ename and (ename == "tensor"):
                            continue
                        s, v = sig[d]
                        if seen.get(s.name, 0) >= v:
                            continue
                        seen[s.name] = v
                        eh.wait_ge(s, v)
                    ins = o["fn"](eh)
                    if i in sig:
                        ins.then_inc(sig[i][0], 16 if o["dma"] else 1)
                if ename == "sync":
                    for d in final_dmas:
                        s, v = sig[d]
                        if seen.get(s.name, 0) >= v:
                            continue
                        seen[s.name] = v
                        eh.wait_ge(s, v)
                    for e2 in self.ENGS:
                        if e2 != "sync" and cnt[e2] > 0:
                            eh.wait_ge(eng_sem[e2], cnt[e2])
            return body

        for ename in self.ENGS:
            if per_eng[ename] or ename == "sync":
                getattr(block, ename)(make(ename))

import numpy as np
from contextlib import ExitStack

D = 1024
NT = 2048
DIN = 9728
OFF_AU, OFF_AV, OFF_B, OFF_CQ, OFF_CK, OFF_CV, OFF_G = 0, 512, 1024, 2048, 3584, 5120, 6656
DFF = 2816
DIL = (1, 4, 16)
EPS = 1e-6
SCALE = 0.125
KB = 1024
XF, XH, OT, PL = 0, 32 * KB, 64 * KB, 112 * KB
ARENA = 188 * KB


class _Stop(Exception):
    pass


def build_nc(stop=None, dbg=False, step=None, skip_sample=False, sample_only=False):
    nc = bass.Bass("TRN2", target_bir_lowering=False)
    din = lambda n, s: nc.dram_tensor(n, list(s), F32, kind="ExternalInput").ap()
    dout = lambda n, s: nc.dram_tensor(n, list(s), F32, kind="ExternalOutput").ap()
    xw = din("xw", [6144, D])
    flags_d = din("flags", [128, 4])
    etab_d = din("etab", [12, 128, 512])
    ident_d = din("ident", [128, 128])
    tril_d = din("tril", [128, 128])
    ones2_d = din("ones2", [128, 256])
    g_pre_mix = din("norm_pre_mix", [2, D]); g_post_mix = din("norm_post_mix", [2, D])
    g_pre_ffn = din("norm_pre_ffn", [2, D]); g_post_ffn = din("norm_post_ffn", [2, D])
    w_in = din("w_in", [2, D, DIN])
    a_norm_g = din("a_norm_g", [2, 512]); a_norm_b = din("a_norm_b", [2, 512])
    a_w_s = din("a_w_s", [2, 4, 128, 128]); a_b_s = din("a_b_s", [2, 4, 128])
    b_conv_w = din("b_conv_w", [2, 31, 512]); b_conv_b = din("b_conv_b", [2, 512])
    b_norm_g = din("b_norm_g", [2, 512]); b_norm_b = din("b_norm_b", [2, 512])
    w_branch = din("w_branch", [2, 1536, D]); w_out = din("w_out", [2, D, D])
    ffn_w_in = din("ffn_w_in", [2, D, 2 * DFF]); ffn_w_out = din("ffn_w_out", [2, DFF, D])

    xs_d = din("xs", [32, D])
    st_d = din("st", [2, 4, 30, 512])
    bdm_d = din("bdm", [32, 32])
    c0k = din("c0k", [2, 4, 128, 512]); c0v = din("c0v", [2, 4, 128, 512])
    c1k = din("c1k", [2, 4, 512, 512]); c1v = din("c1v", [2, 4, 512, 512])
    c2k = din("c2k", [2, 4, 2048, 512]); c2v = din("c2v", [2, 4, 2048, 512])
    ys_o = dout("ys_o", [32, D])
    nbs_o = dout("nbs_o", [2, 4, 30, 512])
    nav_o = dout("nav_o", [2, 32, 512])
    ks_o = dout("ks_o", [2, 3, 32, 512])
    vs_o = dout("vs_o", [2, 3, 32, 512])
    if dbg:
        x1s = nc.dram_tensor("x1s", [4096, D], F32, kind="ExternalOutput").ap()
        xmid = nc.dram_tensor("xmid", [NT, D], F32, kind="ExternalOutput").ap()
    else:
        x1s = nc.dram_tensor("x1s", [4096, D], F32).ap()
        xmid = nc.dram_tensor("xmid", [NT, D], F32).ap()

    y_o = dout("y_o", [NT, D])
    k_o = dout("k_o", [2, 3, NT, 512])
    v_o = dout("v_o", [2, 3, NT, 512])
    gt_o = dout("gt_o", [2, 512, 32])
    dbg_o = nc.dram_tensor("dbg_o", [128, ARENA // 2], BF16, kind="ExternalOutput").ap() if dbg else None

    with ExitStack() as st:
        sbt = lambda n, s, d: st.enter_context(nc.sbuf_tensor(n, list(s), d))
        arena = sbt("arena", [128, ARENA // 2], BF16)
        identb = sbt("identb", [128, 128], BF16)
        identf = sbt("identf", [128, 128], F32)
        trilf = sbt("trilf", [128, 128], F32)
        ones2 = sbt("ones2s", [128, 2, 128], BF16)
        etab = sbt("etabs", [128, 12, 512], BF16)
        flags = sbt("flagss", [128, 4], F32)
        stat = sbt("stat", [128, 256], F32)
        bs_sb = sbt("bs_sb", [128, 4], F32)
        bs32 = sbt("bs32", [32, 4], F32)
        bdm = sbt("bdm_s", [32, 32], F32)
        xs_res = sbt("xs_res", [32, D], F32)
        psb = [st.enter_context(nc.psum_tensor("psb%d" % i, [128, 512], F32)) for i in range(8)]
        S = Sched(nc)
        state = {"ps": 0, "st": 0, "alt": 0}

        def ps():
            state["ps"] = (state["ps"] + 1) % 8
            return psb[state["ps"]]

        def stc(n=1):
            i = state["st"]
            if i + n > 256:
                i = 0
            state["st"] = i + n
            return stat[:, i:i + n]

        def alt(a="vector", b="scalar"):
            state["alt"] ^= 1
            return a if state["alt"] else b

        def Vw(off, shape, dt=BF16):
            n = 1
            for s_ in shape:
                n *= s_
            dsz = 2 if dt == BF16 else 4
            v = arena[:, off // 2: off // 2 + n * dsz // 2]
            if dt != BF16:
                v = v.bitcast(dt)
            if len(shape) == 2:
                v = v.rearrange("p (a b) -> p a b", a=shape[0])
            elif len(shape) == 3:
                v = v.rearrange("p (a b c) -> p a b c", a=shape[0], b=shape[1])
            return v

        def dstep(n):
            if step == n and stop is not None and state.get("pass") == stop.split(":")[0]:
                raise _Stop()

        def ckpt(name):
            if stop == name:
                raise _Stop()

        def bcast_load(dst, row):
            S.dma(dst, row.partition_broadcast(128))

        def evac(out, in_, eng=None):
            eng = eng or alt()
            S.copy(out, in_, eng=eng)

        S.dma(identf[:], ident_d)
        S.dma(trilf[:], tril_d)
        S.dma(flags[:], flags_d)
        S.dma(ones2[:].rearrange("p a b -> p (a b)"), ones2_d, eng="gpsimd")
        S.dma(etab[:], etab_d.rearrange("n p c -> p n c"), eng="gpsimd")
        S.copy(identb[:], identf[:])
        S.dma(bdm[:], bdm_d)

        if step == 777:
            S.dma(v_o[0, 0][0:128, 0:128], identf[:])
        try:
            ckpt("const")
        except _Stop:
            S.barrier()
            S.dma(dbg_o, arena[:])
            S.emit(st)
            return nc

        def lockstep(gens):
            gens = list(gens)
            while gens:
                nxt = []
                for g_ in gens:
                    try:
                        next(g_)
                        nxt.append(g_)
                    except StopIteration:
                        pass
                gens = nxt

        def run1(gen):
            for _ in gen:
                pass

        def g_rstd(ss, n, eps=EPS, P=slice(0, 128)):
            m = stc()[P, :]
            S.ts(m, ss, 1.0 / n, eps, ALU.mult, ALU.add)
            yield
            S.act(m, m, ACTF.Sqrt)
            yield
            r = stc()[P, :]
            S.recip(r, m)
            yield
            return r

        def g_sumsq(src, junk, P=slice(0, 128)):
            ss = stc()[P, :]
            S.memset(ss, 0.0)
            yield
            S.act(junk, src, ACTF.Square, accum_out=ss)
            yield
            return ss

        def rstd_from_ss(ss, n, eps=EPS, P=slice(0, 128)):
            g_ = g_rstd(ss, n, eps, P)
            try:
                while True:
                    next(g_)
            except StopIteration as e_:
                return e_.value

        def sumsq(src, junk, P=slice(0, 128)):
            g_ = g_sumsq(src, junk, P)
            try:
                while True:
                    next(g_)
            except StopIteration as e_:
                return e_.value

        def g_norm_to_T(xt, gtile, dstT, tile, junk, xnb):
            ss = yield from g_sumsq(xt, junk)
            r = yield from g_rstd(ss, D)
            S.stt(xnb, xt, r, gtile, ALU.mult, ALU.mult)
            yield
            pt = ps()[:].bitcast(BF16)
            for k in range(8):
                S.tr(pt[:, k * 128:(k + 1) * 128], xnb[:, k * 128:(k + 1) * 128], identb[:])
            yield
            evac(dstT[:, :, tile * 128:(tile + 1) * 128], pt[:, 0:1024].rearrange("p (k c) -> p k c", k=8))
            yield

        def phase_norm(src, grow, dstT, ntiles, tile0=0):
            gtile = Vw(PL + 20 * KB, [1024], F32)
            bcast_load(gtile, grow)

            def body(i, sl_):
                xt = Vw(PL + sl_ * 4 * KB, [1024], F32)
                S.dma(xt, src[i * 128:(i + 1) * 128, :])
                yield
                yield from g_norm_to_T(xt, gtile, dstT, tile0 + i, Vw(PL + 8 * KB + sl_ * 4 * KB, [1024], F32),
                                       Vw(PL + 16 * KB + sl_ * 2 * KB, [1024]))
            for i0 in range(0, ntiles, 2):
                lockstep([body(i0, 0), body(i0 + 1, 1)])

        def g_layernorm512(src, gt, bt, out, toff):
            junk = Vw(toff, [512], F32)
            tmp = Vw(toff + 2 * KB, [512], F32)
            sm = stc()
            S.reduce(sm, src, ALU.add)
            yield
            sq = yield from g_sumsq(src, junk)
            mean = stc()
            S.ts(mean, sm, 1.0 / 512, 0.0, ALU.mult, ALU.add)
            yield
            msq = stc()
            S.tt(msq, mean, mean, ALU.mult)
            yield
            var = stc()
            S.stt(var, sq, 1.0 / 512, msq, ALU.mult, ALU.subtract)
            yield
            S.ts(var, var, 1.0, EPS, ALU.mult, ALU.add)
            yield
            S.act(var, var, ACTF.Sqrt)
            yield
            r = stc()
            S.recip(r, var)
            yield
            S.ts(tmp, src, mean, r, ALU.subtract, ALU.mult)
            yield
            S.tt(tmp, tmp, gt, ALU.mult)
            yield
            S.tt(out, tmp, bt, ALU.add)
            yield

        def to_featT(src_bf, dstT, tile):
            pt = ps()[:].bitcast(BF16)
            for c in range(4):
                S.tr(pt[:, c * 128:(c + 1) * 128], src_bf[:, c * 128:(c + 1) * 128], identb[:])
            evac(dstT[:, :, tile * 128:(tile + 1) * 128],
                 pt[:, 0:512].rearrange("p (c t) -> p c t", c=4))

        def wload(dst, src2d, kc):
            S.dma(dst, src2d.rearrange("(k p) c -> p k c", p=128), eng="gpsimd")

        def run_pass(pname, l, xsrc, hsrc, xdst, fcol, out_l):
            state["pass"] = pname
            xnT_f = Vw(XF, [8, NT])
            xnT_h = Vw(XH, [8, NT])
            mergedT = xnT_h
            oT = [Vw(OT + n * 16 * KB, [4, NT]) for n in range(3)]
            S.barrier()
            phase_norm(hsrc, g_pre_mix[l:l + 1, :], xnT_h, 16)
            phase_norm(xsrc, g_pre_mix[l:l + 1, :], xnT_f, 16)

            ckpt("%s:N" % pname)
            WA = Vw(PL, [8, 1024])
            wload(WA, w_in[l][:, OFF_AU:OFF_AU + 1024], 8)
            agt = Vw(PL + 16 * KB, [512], F32); abt = Vw(PL + 18 * KB, [512], F32)
            bcast_load(agt, a_norm_g[l:l + 1, :]); bcast_load(abt, a_norm_b[l:l + 1, :])
            WsT = Vw(PL + 20 * KB, [4, 128])
            wtmp = Vw(PL + 21 * KB, [4, 128], F32)
            wtmpb = Vw(PL + 23 * KB, [4, 128])
            S.dma(wtmp, a_w_s[l].rearrange("g t s -> t g s"))
            S.dma(bs_sb[:], a_b_s[l].rearrange("g t -> t g"), allow_slow_non_contiguous=True)
            for g in range(4):
                S.tt(wtmpb[:, g, :], wtmp[:, g, :], trilf[:], ALU.mult)
            pt = ps()[:].bitcast(BF16)
            for g in range(4):
                S.tr(pt[:, g * 128:(g + 1) * 128], wtmpb[:, g, :], identb[:])
            evac(WsT, pt[:, 0:512].rearrange("p (g t) -> p g t", g=4))
            def bodyA(i, sl_):
                TA = PL + 24 * KB + sl_ * 14 * KB
                u_sb = Vw(TA, [512], F32)
                v_sb = Vw(TA + 2 * KB, [512], F32)
                vnb = Vw(TA + 4 * KB, [512])
                oab = Vw(TA + 5 * KB, [512])
                pm_sb = Vw(TA + 10 * KB, [512], F32)
                pu = ps(); pv = ps()
                for k in range(8):
                    S.mm(pu[:], xnT_f[:, k, i * 128:(i + 1) * 128], WA[:, k, 0:512], start=(k == 0), stop=(k == 7))
                for k in range(8):
                    S.mm(pv[:], xnT_f[:, k, i * 128:(i + 1) * 128], WA[:, k, 512:1024], start=(k == 0), stop=(k == 7))
                yield
                S.copy(u_sb, pu[:], eng="scalar")
                S.copy(v_sb, pv[:], eng="vector")
                yield
                yield from g_layernorm512(v_sb, agt, abt, vnb, TA + 6 * KB)
                pm = ps()
                for g in range(4):
                    S.mm(pm[:, g * 128:(g + 1) * 128], WsT[:, g, :], vnb[:, g * 128:(g + 1) * 128])
                yield
                S.copy(pm_sb, pm[:], eng="scalar")
                yield
                for g in range(4):
                    S.stt(oab[:, g * 128:(g + 1) * 128], pm_sb[:, g * 128:(g + 1) * 128], bs_sb[:, g:g + 1],
                          u_sb[:, g * 128:(g + 1) * 128], ALU.add, ALU.mult)
                yield
                pt = ps()[:].bitcast(BF16)
                for c in range(4):
                    S.tr(pt[:, c * 128:(c + 1) * 128], oab[:, c * 128:(c + 1) * 128], identb[:])
                yield
                evac(oT[0][:, :, i * 128:(i + 1) * 128], pt[:, 0:512].rearrange("p (c t) -> p c t", c=4))
                yield
            for i0 in range(0, 16, 2):
                lockstep([bodyA(i0, 0), bodyA(i0 + 1, 1)])

            ckpt("%s:A" % pname)
            S.barrier()
            WB = Vw(PL, [8, 1024])
            wload(WB, w_in[l][:, OFF_B:OFF_B + 1024], 8)
            gluT = Vw(PL + 16 * KB, [4, 2176])
            diag = Vw(PL + 33 * KB, [124, 128])
            TB = OT + 32 * KB
            bgt = Vw(TB, [512], F32); bbt = Vw(TB + 2 * KB, [512], F32); cbt = Vw(TB + 4 * KB, [512], F32)
            bcast_load(bgt, b_norm_g[l:l + 1, :]); bcast_load(bbt, b_norm_b[l:l + 1, :]); bcast_load(cbt, b_conv_b[l:l + 1, :])
            ysb = Vw(TB + 6 * KB, [512], F32)
            sig = Vw(TB + 8 * KB, [512], F32)
            obb = Vw(TB + 10 * KB, [512])
            cw = Vw(TB + 11 * KB, [512], F32)
            cwT = Vw(TB + 13 * KB, [4, 32], F32)
            gt32 = Vw(TB + 13 * KB + 512, [4, 32], F32)
            lnout = Vw(TB + 14 * KB, [512], F32)
            for j in range(31):
                S.dma(cwT[:, :, j], b_conv_w[l, j].rearrange("(c p) -> p c", p=128), allow_slow_non_contiguous=True)
            for c in range(4):
                for j in range(31):
                    S.ts(diag[:, c * 31 + j, :], identb[:], cwT[:, c, j:j + 1], 1.0, ALU.mult, ALU.mult,
                         eng=("vector" if j % 2 == 0 else "gpsimd"))
            def glu_block(rhs_of_k, n, dst_cols, tail=None):
                for c in range(4):
                    pa = ps(); pg = ps()
                    for k in range(8):
                        S.mm(pa[:, 0:n], WB[:, k, c * 128:(c + 1) * 128], rhs_of_k(k), start=(k == 0), stop=(k == 7))
                    for k in range(8):
                        S.mm(pg[:, 0:n], WB[:, k, 512 + c * 128:512 + (c + 1) * 128], rhs_of_k(k), start=(k == 0), stop=(k == 7))
                    S.act(sig[:, 0:n], pg[:, 0:n], ACTF.Sigmoid)
                    S.tt(gluT[:, c, dst_cols:dst_cols + n], pa[:, 0:n], sig[:, 0:n], ALU.mult)
                    if tail is not None:
                        S.tt(gt32[:, c, :], pa[:, n - 32:n], sig[:, n - 32:n], ALU.mult)
            glu_block(lambda k: xnT_h[:, k, NT - 128:NT], 128, 0)
            for c in range(4):
                S.ts(gluT[:, c, 0:128], gluT[:, c, 0:128], flags[:, fcol:fcol + 1], 1.0, ALU.mult, ALU.mult)
            for w in range(4):
                glu_block(lambda k, w=w: xnT_f[:, k, w * 512:(w + 1) * 512], 512, 128 + w * 512,
                          tail=(out_l is not None and w == 3) or None)
            if out_l is not None:
                S.dma(gt_o[out_l].rearrange("(c p) t -> p c t", p=128), gt32)
            def bodyB(i, sl_):
                ysb_ = ysb if sl_ == 0 else Vw(TB + 11 * KB, [512], F32)
                lnout_ = lnout if sl_ == 0 else Vw(PL + 72 * KB, [512], F32)
                obb_ = obb if sl_ == 0 else Vw(PL + 74 * KB, [512])
                pc = ps()
                for c in range(4):
                    for j in range(31):
                        s0 = 128 + i * 128 - 30 + j
                        S.mm(pc[:, c * 128:(c + 1) * 128], gluT[:, c, s0:s0 + 128], diag[:, c * 31 + j, :],
                             start=(j == 0), stop=(j == 30))
                yield
                S.tt(ysb_, pc[:], cbt, ALU.add)
                yield
                yield from g_layernorm512(ysb_, bgt, bbt, lnout_, PL + 64 * KB + sl_ * 4 * KB)
                S.act(obb_, lnout_, ACTF.Silu)
                yield
                pt = ps()[:].bitcast(BF16)
                for c in range(4):
                    S.tr(pt[:, c * 128:(c + 1) * 128], obb_[:, c * 128:(c + 1) * 128], identb[:])
                yield
                evac(oT[1][:, :, i * 128:(i + 1) * 128], pt[:, 0:512].rearrange("p (c t) -> p c t", c=4))
                yield
            for i0 in range(0, 16, 2):
                lockstep([bodyB(i0, 0), bodyB(i0 + 1, 1)])

            ckpt("%s:B" % pname)
            S.barrier()
            WC = Vw(PL, [9, 8, 128])
            QT = Vw(PL + 18 * KB, [2, NT])
            KTb = Vw(PL + 26 * KB, [1, 4096])[:, 0, :]
            Vt = Vw(PL + 34 * KB, [32, 2, 128])
            acc = Vw(PL + 50 * KB, [2, NT], F32)
            Pt2 = [Vw(PL + 66 * KB + q * KB, [512]) for q in range(2)]
            kst = Vw(PL + 68 * KB, [512], F32)
            import os
            vst = Vw(PL + (68 if os.environ.get("VST68") else 70) * KB, [512], F32)
            Eh = Vw(PL + 72 * KB, [512])
            S.memset(Vt.rearrange("p a b c -> p (a b c)"), 0.0, eng="gpsimd")
            S.memset(QT[64:128, 0, :], 0.0, eng="gpsimd")
            S.memset(QT[0:64, 1, :], 0.0, eng="gpsimd")
            for c in range(4):
                for g in range(3):
                    for j, off in enumerate((OFF_CQ, OFF_CK, OFF_CV)):
                        wload(WC[:, g * 3 + j, :, :], w_in[l][:, off + g * 512 + c * 128: off + g * 512 + (c + 1) * 128], 8)
                for g in range(3):
                    d = DIL[g]
                    Lh = 128 * d
                    nb = 16 // d
                    Wq, Wk, Wv = WC[:, g * 3 + 0], WC[:, g * 3 + 1], WC[:, g * 3 + 2]
                    E = etab[:, g * 4 + c, :]
                    for hh in range(2):
                        S.ts(Eh[:, hh * 256:hh * 256 + 128], E[:, hh * 256:hh * 256 + 128], flags[:, fcol:fcol + 1], 1.0,
                             ALU.mult, ALU.mult)
                        S.copy(Eh[:, hh * 256 + 128:hh * 256 + 256], E[:, hh * 256 + 128:hh * 256 + 256], eng="gpsimd")
                    for w in range(4):
                        pq = ps(); pk = ps()
                        for k in range(8):
                            S.mm(pq[:], Wq[:, k, :], xnT_f[:, k, w * 512:(w + 1) * 512], start=(k == 0), stop=(k == 7))
                        for k in range(8):
                            S.mm(pk[:], Wk[:, k, :], xnT_f[:, k, w * 512:(w + 1) * 512], start=(k == 0), stop=(k == 7))
                        S.copy(QT[0:64, 0, w * 512:(w + 1) * 512], pq[0:64, :], eng="vector")
                        S.copy(QT[64:128, 1, w * 512:(w + 1) * 512], pq[64:128, :], eng="scalar")
                        evac(KTb[:, Lh + w * 512:Lh + (w + 1) * 512], pk[:])
                    hw = min(512, Lh)
                    for w in range(Lh // hw):
                        pk = ps()
                        c0 = NT - Lh + w * hw
                        for k in range(8):
                            S.mm(pk[:, 0:hw], Wk[:, k, :], xnT_h[:, k, c0:c0 + hw], start=(k == 0), stop=(k == 7))
                        evac(KTb[:, w * hw:(w + 1) * hw], pk[:, 0:hw])
                    htiles = [("h", r, 0) for r in range(d)]
                    ftiles = [("f", r, jb) for r in range(d) for jb in range(nb)]
                    groups = [(t0, htiles[t0:t0 + 4]) for t0 in range(0, d, 4)] + \
                             [(d + t0, ftiles[t0:t0 + 4]) for t0 in range(0, 16, 4)]
                    vo2 = v_o[out_l, g] if out_l is not None else None
                    for (t0, grp) in groups:
                        pv = ps()
                        for q, (kind, r, jb) in enumerate(grp):
                            if kind == "h":
                                srcT = xnT_h; s0 = NT - Lh + r
                            else:
                                srcT = xnT_f; s0 = r + d * 128 * jb
                            for k in range(8):
                                S.mm(pv[:, q * 128:(q + 1) * 128], srcT[:, k, s0:s0 + 127 * d + 1:d],
                                     Wv[:, k, :], start=(k == 0), stop=(k == 7))
                        n = len(grp)
                        pv3 = pv[:, 0:n * 128].rearrange("p (t e) -> p t e", t=n)
                        S.copy(Vt[:, t0:t0 + n, 0, 0:64], pv3[:, :, 0:64], eng="vector")
                        S.copy(Vt[:, t0:t0 + n, 1, 64:128], pv3[:, :, 64:128], eng="scalar")
                        if out_l is not None and grp[0][0] == "f":
                            S.copy(vst, pv[:], eng="scalar")
                            cs = slice(c * 128, (c + 1) * 128)
                            _, r0, jb0 = grp[0]
                            if g == 0:
                                dst = vo2.rearrange("(q p) e -> p q e", p=128)[:, jb0:jb0 + 4, cs]
                            elif g == 1:
                                dst = vo2.rearrange("(q p dd) e -> p q dd e", p=128, dd=4)[:, :, r0, cs]
                            else:
                                dst = vo2.rearrange("(p dd) e -> p dd e", dd=16)[:, r0:r0 + 4, cs]
                            S.dma(dst, vst.rearrange("p (t e) -> p t e", t=4))
                    import os
                    if out_l is not None and not os.environ.get("NOKOUT"):
                        for t0 in range(0, 16, 4):
                            pk = ps()
                            for q in range(4):
                                i = t0 + q
                                for k in range(8):
                                    S.mm(pk[:, q * 128:(q + 1) * 128], xnT_f[:, k, i * 128:(i + 1) * 128], Wk[:, k, :],
                                         start=(k == 0), stop=(k == 7))
                            S.copy(kst, pk[:], eng="scalar")
                            S.dma(k_o[out_l, g][t0 * 128:(t0 + 4) * 128, c * 128:(c + 1) * 128].rearrange("(t p) e -> p t e", p=128),
                                  kst.rearrange("p (t e) -> p t e", t=4))
                    def blockC(r, jb, Pt, g=g, d=d, Lh=Lh, nb=nb, E=E):
                        qs = r + d * 128 * jb
                        sl = lambda s_: slice(s_, s_ + 127 * d + 1, d)
                        pss = ps()
                        for hh in range(2):
                            S.mm(pss[:, hh * 256:hh * 256 + 128], KTb[:, sl(Lh + qs - 128 * d)], QT[:, hh, sl(qs)])
                            S.mm(pss[:, hh * 256 + 128:hh * 256 + 256], KTb[:, sl(Lh + qs)], QT[:, hh, sl(qs)])
                        yield
                        S.act(Pt, pss[:], ACTF.Exp, scale=SCALE)
                        yield
                        S.tt(Pt, Pt, (Eh if jb == 0 else E), ALU.mult, eng="gpsimd")
                        yield
                        t1 = d + r * nb + jb
                        th0 = r if jb == 0 else t1 - 1
                        pso = ps()
                        seq = [(hh, half) for hh in range(2) for half in range(2)]
                        for n_, (hh, half) in enumerate(seq):
                            S.mm(pso[:, 0:128], Vt[:, (th0 if half == 0 else t1), hh, :],
                                 Pt[:, hh * 256 + half * 128:hh * 256 + (half + 1) * 128], start=(n_ == 0), stop=(n_ == 3))
                        for n_, (hh, half) in enumerate(seq):
                            S.mm(pso[:, 128:256], ones2[:, hh, :],
                                 Pt[:, hh * 256 + half * 128:hh * 256 + (half + 1) * 128], start=(n_ == 0), stop=(n_ == 3))
                        yield
                        dst = acc[:, :, sl(qs)]
                        src = pso[:, 0:256].rearrange("p (a q) -> p a q", a=2)
                        if g == 0:
                            S.copy(dst, src, eng="vector")
                        else:
                            S.tt(dst, dst, src, ALU.add)
                        yield
                    blks = [(r, jb) for r in range(d) for jb in range(nb)]
                    for b0 in range(0, 16, 2):
                        lockstep([blockC(blks[b0][0], blks[b0][1], Pt2[0]), blockC(blks[b0 + 1][0], blks[b0 + 1][1], Pt2[1])])
                S.recip(acc[:, 1, :], acc[:, 1, :])
                S.tt(oT[2][:, c, :], acc[:, 0, :], acc[:, 1, :], ALU.mult)

            ckpt("%s:C" % pname)
            S.barrier()
            Wg = Vw(PL, [3, 8, 128])
            Wb = Vw(PL + 6 * KB, [3, 4, 128])
            macc = Vw(PL + 10 * KB, [512], F32)
            sgm = Vw(PL + 12 * KB, [512], F32)
            mtmp = Vw(PL + 14 * KB, [512], F32)
            Wo = Vw(PL + 16 * KB, [8, 1024])
            gpost = Vw(PL + 32 * KB, [1024], F32)
            wload(Wo, w_out[l], 8)
            bcast_load(gpost, g_post_mix[l:l + 1, :])
            for dc in range(8):
                for n in range(3):
                    wload(Wg[:, n, :, :], w_in[l][:, OFF_G + n * 1024 + dc * 128: OFF_G + n * 1024 + (dc + 1) * 128], 8)
                    wload(Wb[:, n, :, :], w_branch[l][n * 512:(n + 1) * 512, dc * 128:(dc + 1) * 128], 4)
                for w in range(4):
                    ws = slice(w * 512, (w + 1) * 512)
                    for n in range(3):
                        pg = ps(); pp = ps()
                        for k in range(8):
                            S.mm(pg[:], Wg[:, n, k, :], xnT_f[:, k, ws], start=(k == 0), stop=(k == 7))
                        for k in range(4):
                            S.mm(pp[:], Wb[:, n, k, :], oT[n][:, k, ws], start=(k == 0), stop=(k == 3))
                        S.act(sgm, pg[:], ACTF.Sigmoid)
                        if n == 0:
                            S.tt(macc, pp[:], sgm, ALU.mult)
                        else:
                            S.tt(mtmp, pp[:], sgm, ALU.mult)
                            if n == 1:
                                S.tt(macc, macc, mtmp, ALU.add, eng="gpsimd")
                            else:
                                S.tt(mergedT[:, dc, ws], macc, mtmp, ALU.add, eng="gpsimd")
            gpf = Vw(PL + 72 * KB, [1024], F32)
            bcast_load(gpf, g_pre_ffn[l:l + 1, :])

            def bodyM(i, sl_):
                TM = PL + 36 * KB + sl_ * 18 * KB
                ysb2 = Vw(TM, [1024], F32)
                junk2 = Vw(TM + 4 * KB, [1024], F32)
                xt = Vw(TM + 8 * KB, [1024], F32)
                njunk = Vw(TM + 12 * KB, [1024], F32)
                nxnb = Vw(TM + 16 * KB, [1024])
                S.dma(xt, xsrc[i * 128:(i + 1) * 128, :])
                py = [ps(), ps()]
                for cb in range(2):
                    for k in range(8):
                        S.mm(py[cb][:], mergedT[:, k, i * 128:(i + 1) * 128], Wo[:, k, cb * 512:(cb + 1) * 512],
                             start=(k == 0), stop=(k == 7))
                yield
                S.copy(ysb2[:, 0:512], py[0][:], eng="scalar")
                S.copy(ysb2[:, 512:1024], py[1][:], eng="vector")
                yield
                ss = yield from g_sumsq(ysb2, junk2)
                r = yield from g_rstd(ss, D)
                S.stt(ysb2, ysb2, r, gpost, ALU.mult, ALU.mult)
                yield
                S.tt(xt, xt, ysb2, ALU.add)
                yield
                S.dma(xmid[i * 128:(i + 1) * 128, :], xt)
                yield from g_norm_to_T(xt, gpf, xnT_f, i, njunk, nxnb)
            for i0 in range(0, 16, 2):
                lockstep([bodyM(i0, 0), bodyM(i0 + 1, 1)])

            ckpt("%s:M" % pname)
            S.barrier()
            actT = Vw(OT, [22, 1024])
            W2 = Vw(PL, [22, 1024])
            wload(W2, ffn_w_out[l], 22)
            W1 = [Vw(PL + 44 * KB + q * 4 * KB, [2, 8, 128]) for q in range(2)]
            sg = Vw(PL + 52 * KB, [512], F32)
            gpost2 = Vw(PL + 54 * KB, [1024], F32)
            bcast_load(gpost2, g_post_ffn[l:l + 1, :])
            ysb2 = Vw(PL + 58 * KB, [1024], F32)
            junk2 = Vw(PL + 62 * KB, [1024], F32)
            for hf in range(2):
                for fc in range(22):
                    Wc = W1[fc % 2]
                    wload(Wc[:, 0, :, :], ffn_w_in[l][:, fc * 128:(fc + 1) * 128], 8)
                    wload(Wc[:, 1, :, :], ffn_w_in[l][:, DFF + fc * 128:DFF + (fc + 1) * 128], 8)
                    for w in range(2):
                        ws = slice(hf * 1024 + w * 512, hf * 1024 + (w + 1) * 512)
                        pg = ps(); pu = ps()
                        for k in range(8):
                            S.mm(pg[:], Wc[:, 0, k, :], xnT_f[:, k, ws], start=(k == 0), stop=(k == 7))
                        for k in range(8):
                            S.mm(pu[:], Wc[:, 1, k, :], xnT_f[:, k, ws], start=(k == 0), stop=(k == 7))
                        S.act(sg, pg[:], ACTF.Silu)
                        S.tt(actT[:, fc, w * 512:(w + 1) * 512], pu[:], sg, ALU.mult)
                def bodyF(i8, sl_, hf=hf):
                    i = hf * 8 + i8
                    ysb_ = Vw(PL + 58 * KB + sl_ * 8 * KB, [1024], F32)
                    xt = Vw(PL + 62 * KB + sl_ * 8 * KB, [1024], F32)
                    junk_ = Vw(PL + 74 * KB, [1024])
                    S.dma(xt, xmid[i * 128:(i + 1) * 128, :])
                    py = [ps(), ps()]
                    for cb in range(2):
                        for k in range(22):
                            S.mm(py[cb][:], actT[:, k, i8 * 128:(i8 + 1) * 128], W2[:, k, cb * 512:(cb + 1) * 512],
                                 start=(k == 0), stop=(k == 21))
                    yield
                    S.copy(ysb_[:, 0:512], py[0][:], eng="scalar")
                    S.copy(ysb_[:, 512:1024], py[1][:], eng="vector")
                    yield
                    ss = yield from g_sumsq(ysb_, junk_)
                    r = yield from g_rstd(ss, D)
                    S.stt(ysb_, ysb_, r, gpost2, ALU.mult, ALU.mult)
                    yield
                    S.tt(xt, xt, ysb_, ALU.add)
                    yield
                    S.dma(xdst[i * 128:(i + 1) * 128, :], xt)
                    yield
                for i0 in range(0, 8, 2):
                    lockstep([bodyF(i0, 0), bodyF(i0 + 1, 1)])

        def run_sample(l):
            state["pass"] = "S%d" % l
            S.barrier()
            A0 = 0
            xnTs = Vw(A0, [8, 32])
            oTs = [Vw(A0 + 1 * KB + n * 256, [4, 32]) for n in range(3)]
            mergedTs = Vw(A0 + 2 * KB, [8, 32])
            actTs = Vw(A0 + 3 * KB, [22, 32])
            T0 = 8 * KB
            junk = Vw(T0, [1024], F32)
            ysb = Vw(T0 + 4 * KB, [1024], F32)
            gt_a = Vw(T0 + 8 * KB, [1024], F32)
            gt_b = Vw(T0 + 12 * KB, [1024], F32)
            xnb = Vw(T0 + 16 * KB, [1024])
            W0 = 56 * KB

            def norm32(src, gtile):
                ss = sumsq(src[0:32, :], junk[0:32, :], P=slice(0, 32))
                r = rstd_from_ss(ss, D, P=slice(0, 32))
                S.stt(xnb[0:32, :], src[0:32, :], r, gtile[0:32, :], ALU.mult, ALU.mult)
                pt = ps()[:].bitcast(BF16)
                for k in range(8):
                    S.tr(pt[:, k * 32:(k + 1) * 32], xnb[0:32, k * 128:(k + 1) * 128], identb[0:32, 0:32])
                S.copy(xnTs, pt[:, 0:256].rearrange("p (k c) -> p k c", k=8), eng="vector")

            def ln32(src, gt, bt, out, toff):
                jk = Vw(toff, [512], F32)
                tmp = Vw(toff + 2 * KB, [512], F32)
                P = slice(0, 32)
                sm = stc(); S.reduce(sm[P, :], src[P, :], ALU.add)
                sq = stc(); S.memset(sq[P, :], 0.0); S.act(jk[P, :], src[P, :], ACTF.Square, accum_out=sq[P, :])
                mean = stc(); S.ts(mean[P, :], sm[P, :], 1.0 / 512, 0.0, ALU.mult, ALU.add)
                msq = stc(); S.tt(msq[P, :], mean[P, :], mean[P, :], ALU.mult)
                var = stc(); S.stt(var[P, :], sq[P, :], 1.0 / 512, msq[P, :], ALU.mult, ALU.subtract)
                S.ts(var[P, :], var[P, :], 1.0, EPS, ALU.mult, ALU.add)
                S.act(var[P, :], var[P, :], ACTF.Sqrt)
                r = stc(); S.recip(r[P, :], var[P, :])
                S.ts(tmp[P, :], src[P, :], mean[P, :], r[P, :], ALU.subtract, ALU.mult)
                S.tt(tmp[P, :], tmp[P, :], gt[P, :], ALU.mult)
                S.tt(out[P, :], tmp[P, :], bt[P, :], ALU.add)

            def featT32(src_bf, dstT):
                pt = ps()[:].bitcast(BF16)
                for c in range(4):
                    S.tr(pt[:, c * 32:(c + 1) * 32], src_bf[0:32, c * 128:(c + 1) * 128], identb[0:32, 0:32])
                S.copy(dstT, pt[:, 0:128].rearrange("p (c t) -> p c t", c=4), eng="vector")

            P32 = slice(0, 32)
            bcast_load(gt_a, g_pre_mix[l:l + 1, :])
            if l == 0:
                S.dma(xs_res[:], xs_d)
            norm32(xs_res, gt_a)

            WA = Vw(W0, [8, 1024])
            wload(WA, w_in[l][:, OFF_AU:OFF_AU + 1024], 8)
            agt = Vw(T0 + 18 * KB, [512], F32); abt = Vw(T0 + 20 * KB, [512], F32)
            bcast_load(agt, a_norm_g[l:l + 1, :]); bcast_load(abt, a_norm_b[l:l + 1, :])
            BDf = Vw(T0 + 22 * KB, [4, 32], F32)
            BDb = Vw(T0 + 23 * KB, [4, 32])
            S.memset(BDf[P32], 0.0)
            for b in range(4):
                for g in range(4):
                    S.dma(BDf[b * 8:(b + 1) * 8, g, b * 8:(b + 1) * 8], a_w_s[l, g, 0:8, 0:8].rearrange("t s -> s t"),
                          allow_slow_non_contiguous=True)
                S.dma(bs32[b * 8:(b + 1) * 8, :], a_b_s[l][:, 0:8].rearrange("g t -> t g"), allow_slow_non_contiguous=True)
            for g in range(4):
                S.tt(BDb[P32, g, :], BDf[P32, g, :], bdm[:], ALU.mult)
            u_sb = Vw(T0 + 24 * KB, [512], F32); v_sb = Vw(T0 + 26 * KB, [512], F32)
            vn32 = Vw(T0 + 28 * KB, [512], F32); vnb = Vw(T0 + 30 * KB, [512]); oab = Vw(T0 + 31 * KB, [512])
            pm_sb = Vw(T0 + 32 * KB, [512], F32)
            pu = ps(); pv = ps()
            for k in range(8):
                S.mm(pu[P32, :], xnTs[:, k, :], WA[:, k, 0:512], start=(k == 0), stop=(k == 7))
            for k in range(8):
                S.mm(pv[P32, :], xnTs[:, k, :], WA[:, k, 512:1024], start=(k == 0), stop=(k == 7))
            S.copy(u_sb[P32], pu[P32, :], eng="scalar")
            S.copy(v_sb[P32], pv[P32, :], eng="vector")
            ln32(v_sb, agt, abt, vn32, T0 + 34 * KB)
            S.dma(nav_o[l], vn32[P32])
            S.copy(vnb[P32], vn32[P32], eng="vector")
            pm = ps()
            for g in range(4):
                S.mm(pm[P32, g * 128:(g + 1) * 128], BDb[P32, g, :], vnb[P32, g * 128:(g + 1) * 128])
            S.copy(pm_sb[P32], pm[P32, :], eng="scalar")
            for g in range(4):
                S.stt(oab[P32, g * 128:(g + 1) * 128], pm_sb[P32, g * 128:(g + 1) * 128], bs32[:, g:g + 1],
                      u_sb[P32, g * 128:(g + 1) * 128], ALU.add, ALU.mult)
            featT32(oab, oTs[0])

            S.barrier()
            WB = Vw(W0 + 16 * KB, [8, 1024])
            wload(WB, w_in[l][:, OFF_B:OFF_B + 1024], 8)
            diag = Vw(W0 + 32 * KB, [124, 128])
            cwT = Vw(T0 + 18 * KB, [4, 32], F32)
            for j in range(31):
                S.dma(cwT[:, :, j], b_conv_w[l, j].rearrange("(c p) -> p c", p=128), allow_slow_non_contiguous=True)
            for c in range(4):
                for j in range(31):
                    S.ts(diag[:, c * 31 + j, :], identb[:], cwT[:, c, j:j + 1], 1.0, ALU.mult, ALU.mult,
                         eng=("vector" if j % 2 == 0 else "gpsimd"))
            bgt = Vw(T0 + 20 * KB, [512], F32); bbt = Vw(T0 + 22 * KB, [512], F32)
            bcast_load(bgt, b_norm_g[l:l + 1, :]); bcast_load(bbt, b_norm_b[l:l + 1, :])
            cbT = Vw(T0 + 19 * KB, [4], F32)
            S.dma(cbT, b_conv_b[l].rearrange("(c p) -> p c", p=128), allow_slow_non_contiguous=True)
            pa = ps(); pg = ps()
            for k in range(8):
                S.mm(pa[P32, :], xnTs[:, k, :], WB[:, k, 0:512], start=(k == 0), stop=(k == 7))
            for k in range(8):
                S.mm(pg[P32, :], xnTs[:, k, :], WB[:, k, 512:1024], start=(k == 0), stop=(k == 7))
            sig = Vw(T0 + 24 * KB, [512], F32); glut = Vw(T0 + 26 * KB, [512], F32)
            S.act(sig[P32], pg[P32, :], ACTF.Sigmoid)
            S.tt(glut[P32], pa[P32, :], sig[P32], ALU.mult)
            for b in range(4):
                S.dma(nbs_o[l][b, 22:30, :], glut[b * 8:(b + 1) * 8, :])
            S.dma(nbs_o[l][:, 0:22, :], st_d[l][:, 8:30, :])
            padT = Vw(T0 + 28 * KB, [4, 4, 38])
            gl_f = Vw(T0 + 30 * KB, [4, 32], F32)
            sgf = Vw(T0 + 31 * KB, [32], F32)
            for c in range(4):
                pa = ps(); pg = ps()
                for k in range(8):
                    S.mm(pa[:, 0:32], WB[:, k, c * 128:(c + 1) * 128], xnTs[:, k, :], start=(k == 0), stop=(k == 7))
                for k in range(8):
                    S.mm(pg[:, 0:32], WB[:, k, 512 + c * 128:512 + (c + 1) * 128], xnTs[:, k, :], start=(k == 0), stop=(k == 7))
                S.act(sgf, pg[:, 0:32], ACTF.Sigmoid)
                S.tt(padT[:, c, :, 30:38], pa[:, 0:32].rearrange("p (b t) -> p b t", b=4),
                     sgf.rearrange("p (b t) -> p b t", b=4), ALU.mult)
            stf = Vw(T0 + 32 * KB, [4, 512], F32)
            stb = Vw(T0 + 40 * KB, [4, 512])
            S.dma(stf[0:30], st_d[l].rearrange("b j c -> j b c"))
            S.copy(stb[0:30], stf[0:30], eng="vector")
            pt = ps()[:].bitcast(BF16)
            for b in range(4):
                for c in range(4):
                    S.tr(pt[:, (b * 4 + c) * 32:(b * 4 + c) * 32 + 30], stb[0:30, b, c * 128:(c + 1) * 128], identb[0:30, 0:30])
            S.copy(padT[:, :, :, 0:30], pt[:, 0:512].rearrange("p (b c j) -> p c b j", b=4, c=4)[:, :, :, 0:30], eng="vector")
            pc = ps()
            for c in range(4):
                for j in range(31):
                    S.mm(pc[:, c * 32:(c + 1) * 32], diag[:, c * 31 + j, :], padT[:, c, :, j:j + 8],
                         start=(j == 0), stop=(j == 30))
            ycT = Vw(T0 + 44 * KB, [4, 32])
            for c in range(4):
                S.copy(sgf, pc[:, c * 32:(c + 1) * 32], eng="scalar")
                S.ts(ycT[:, c, :], sgf, cbT[:, c:c + 1], 1.0, ALU.add, ALU.mult)
            pt = ps()[:].bitcast(BF16)
            for c in range(4):
                S.tr(pt[P32, c * 128:(c + 1) * 128], ycT[:, c, :], identb[:])
            yc = Vw(T0 + 24 * KB, [512], F32)
            S.copy(yc[P32], pt[P32, 0:512], eng="vector")
            lnout = Vw(T0 + 26 * KB, [512], F32)
            ln32(yc, bgt, bbt, lnout, T0 + 34 * KB)
            obb = Vw(T0 + 45 * KB, [512])
            S.act(obb[P32], lnout[P32], ACTF.Silu)
            featT32(obb, oTs[1])

            S.barrier()
            QTs = Vw(T0 + 18 * KB, [4, 2, 32])
            KTs = Vw(T0 + 19 * KB, [4, 32])
            accS = Vw(T0 + 20 * KB, [4, 2, 32], F32)
            def cslot(q):
                o = T0 + 22 * KB + q * 12 * KB
                return (Vw(o, [512]), Vw(o + 1 * KB, [512]), Vw(o + 2 * KB, [4, 128]), Vw(o + 3 * KB, [4, 2, 128]),
                        Vw(o + 5 * KB, [4, 2, 128]), Vw(o + 7 * KB, [16]), Vw(o + 7 * KB + 64, [16]))
            cslots = [cslot(0), cslot(1)]
            kvst = Vw(T0 + 30 * KB, [512], F32)
            for q in range(2):
                S.memset(cslots[q][3].rearrange("p a b c -> p (a b c)"), 0.0, eng="gpsimd")
                S.memset(cslots[q][4].rearrange("p a b c -> p (a b c)"), 0.0, eng="gpsimd")
            cctr = [0]
            S.memset(QTs.rearrange("p a b c -> p (a b c)"), 0.0, eng="gpsimd")
            caches = ((c0k, c0v), (c1k, c1v), (c2k, c2v))
            for g in range(3):
                d = DIL[g]
                WS = W0 + 64 * KB + (g % 2) * 24 * KB
                WQ = Vw(WS, [8, 512]); WK = Vw(WS + 8 * KB, [8, 512]); WV = Vw(WS + 16 * KB, [8, 512])
                wload(WQ, w_in[l][:, OFF_CQ + g * 512:OFF_CQ + (g + 1) * 512], 8)
                wload(WK, w_in[l][:, OFF_CK + g * 512:OFF_CK + (g + 1) * 512], 8)
                wload(WV, w_in[l][:, OFF_CV + g * 512:OFF_CV + (g + 1) * 512], 8)
                for W_, dst_o in ((WK, ks_o), (WV, vs_o)):
                    pk = ps()
                    for k in range(8):
                        S.mm(pk[P32, :], xnTs[:, k, :], W_[:, k, :], start=(k == 0), stop=(k == 7))
                    S.copy(kvst[P32], pk[P32, :], eng="scalar")
                    S.dma(dst_o[l, g], kvst[P32])
                for c in range(4):
                    pq = ps()
                    for k in range(8):
                        S.mm(pq[:, 0:32], WQ[:, k, c * 128:(c + 1) * 128], xnTs[:, k, :], start=(k == 0), stop=(k == 7))
                    for k in range(8):
                        S.mm(pq[:, 32:64], WK[:, k, c * 128:(c + 1) * 128], xnTs[:, k, :], start=(k == 0), stop=(k == 7))
                    S.copy(QTs[0:64, c, 0, :], pq[0:64, 0:32], eng="vector")
                    S.copy(QTs[64:128, c, 1, :], pq[64:128, 0:32], eng="vector")
                    S.copy(KTs[:, c, :], pq[:, 32:64], eng="vector")
                for b in range(4):
                    for rho in range(min(d, 8)):
                        toks = list(range(rho, 8, d))
                        nq = len(toks)
                        tsl = slice(b * 8 + rho, b * 8 + rho + (nq - 1) * d + 1, d)
                        ck, cv = caches[g]
                        Kc, Vc, KcT, Vc2, Vn2, P0, P1 = cslots[cctr[0] % 2]
                        cctr[0] += 1
                        S.dma(Kc, ck[l, b][rho:rho + 127 * d + 1:d, :], eng="gpsimd")
                        S.dma(Vc, cv[l, b][rho:rho + 127 * d + 1:d, :], eng="gpsimd")
                        pt = ps()[:].bitcast(BF16)
                        for c in range(4):
                            S.tr(pt[:, c * 128:(c + 1) * 128], Kc[:, c * 128:(c + 1) * 128], identb[:])
                        S.copy(KcT, pt[:, 0:512].rearrange("p (c k) -> p c k", c=4), eng="vector")
                        Vc3 = Vc.rearrange("p (c e) -> p c e", c=4)
                        S.copy(Vc2[:, :, 0, 0:64], Vc3[:, :, 0:64], eng="vector")
                        S.copy(Vc2[:, :, 1, 64:128], Vc3[:, :, 64:128], eng="gpsimd")
                        pvn = ps()
                        for k in range(8):
                            S.mm(pvn[0:nq, :], xnTs[:, k, tsl], WV[:, k, :], start=(k == 0), stop=(k == 7))
                        pv3 = pvn[0:nq, :].rearrange("p (c e) -> p c e", c=4)
                        S.copy(Vn2[0:nq, :, 0, 0:64], pv3[:, :, 0:64], eng="vector")
                        S.copy(Vn2[0:nq, :, 1, 64:128], pv3[:, :, 64:128], eng="vector")
                        for c in range(4):
                            E = etab[:, g * 4 + c, :]
                            pss = ps()
                            for hh in range(2):
                                S.mm(pss[:, hh * 8:hh * 8 + nq], KcT[:, c, :], QTs[:, c, hh, tsl])
                            for hh in range(2):
                                S.mm(pss[0:nq, 16 + hh * 8:16 + hh * 8 + nq], KTs[:, c, tsl], QTs[:, c, hh, tsl])
                            S.act(P0, pss[:, 0:16], ACTF.Exp, scale=SCALE)
                            S.act(P1[0:nq], pss[0:nq, 16:32], ACTF.Exp, scale=SCALE)
                            for hh in range(2):
                                S.tt(P0[:, hh * 8:hh * 8 + nq], P0[:, hh * 8:hh * 8 + nq], E[:, hh * 256:hh * 256 + nq], ALU.mult)
                                S.tt(P1[0:nq, hh * 8:hh * 8 + nq], P1[0:nq, hh * 8:hh * 8 + nq],
                                     E[0:nq, hh * 256 + 128:hh * 256 + 128 + nq], ALU.mult)
                            pso = ps()
                            for which, col in ((0, 0), (1, 8)):
                                n_ = 0
                                for hh in range(2):
                                    lh0 = Vc2[:, c, hh, :] if which == 0 else ones2[:, hh, :]
                                    lh1 = Vn2[0:nq, c, hh, :] if which == 0 else ones2[0:nq, hh, :]
                                    S.mm(pso[:, col:col + nq], lh0, P0[:, hh * 8:hh * 8 + nq], start=(n_ == 0), stop=False)
                                    n_ += 1
                                    S.mm(pso[:, col:col + nq], lh1, P1[0:nq, hh * 8:hh * 8 + nq], start=False, stop=(hh == 1))
                            dst = accS[:, c, :, tsl]
                            src = pso[:, 0:16].rearrange("p (a q) -> p a q", a=2)[:, :, 0:nq]
                            if g == 0:
                                S.copy(dst, src, eng="vector")
                            else:
                                S.tt(dst, dst, src, ALU.add)
            S.recip(accS[:, :, 1, :], accS[:, :, 1, :])
            S.tt(oTs[2], accS[:, :, 0, :], accS[:, :, 1, :], ALU.mult)

            S.barrier()
            Wo = Vw(W0 + 72 * KB, [8, 1024])
            macc = Vw(T0 + 18 * KB, [32], F32); sgm = Vw(T0 + 18 * KB + 128, [32], F32); mtmp = Vw(T0 + 18 * KB + 256, [32], F32)
            wload(Wo, w_out[l], 8)
            bcast_load(gt_a, g_post_mix[l:l + 1, :])
            bcast_load(gt_b, g_pre_ffn[l:l + 1, :])
            for dc in range(8):
                Wg = Vw(W0 + dc * 6 * KB, [3, 8, 128]); Wb = Vw(W0 + 48 * KB + dc * 3 * KB, [3, 4, 128])
                for n in range(3):
                    wload(Wg[:, n, :, :], w_in[l][:, OFF_G + n * 1024 + dc * 128: OFF_G + n * 1024 + (dc + 1) * 128], 8)
                    wload(Wb[:, n, :, :], w_branch[l][n * 512:(n + 1) * 512, dc * 128:(dc + 1) * 128], 4)
                for n in range(3):
                    pg = ps(); pp = ps()
                    for k in range(8):
                        S.mm(pg[:, 0:32], Wg[:, n, k, :], xnTs[:, k, :], start=(k == 0), stop=(k == 7))
                    for k in range(4):
                        S.mm(pp[:, 0:32], Wb[:, n, k, :], oTs[n][:, k, :], start=(k == 0), stop=(k == 3))
                    S.act(sgm, pg[:, 0:32], ACTF.Sigmoid)
                    if n == 0:
                        S.tt(macc, pp[:, 0:32], sgm, ALU.mult)
                    else:
                        S.tt(mtmp, pp[:, 0:32], sgm, ALU.mult)
                        if n == 1:
                            S.tt(macc, macc, mtmp, ALU.add)
                        else:
                            S.tt(mergedTs[:, dc, :], macc, mtmp, ALU.add)

            def post32(py, gp):
                S.copy(ysb[P32, 0:512], py[0][P32, :], eng="scalar")
                S.copy(ysb[P32, 512:1024], py[1][P32, :], eng="vector")
                ss = sumsq(ysb[P32], junk[P32], P=P32)
                r = rstd_from_ss(ss, D, P=P32)
                S.stt(ysb[P32], ysb[P32], r, gp[P32], ALU.mult, ALU.mult)
                S.tt(xs_res[:], xs_res[:], ysb[P32], ALU.add)

            py = [ps(), ps()]
            for cb in range(2):
                for k in range(8):
                    S.mm(py[cb][P32, :], mergedTs[:, k, :], Wo[:, k, cb * 512:(cb + 1) * 512], start=(k == 0), stop=(k == 7))
            post32(py, gt_a)
            norm32(xs_res, gt_b)

            S.barrier()
            W2 = Vw(W0, [22, 1024])
            wload(W2, ffn_w_out[l], 22)
            W1 = [Vw(W0 + 44 * KB + q * 4 * KB, [2, 8, 128]) for q in range(22)]
            bcast_load(gt_a, g_post_ffn[l:l + 1, :])
            sg = Vw(T0 + 18 * KB, [32], F32)
            for fc in range(22):
                Wc = W1[fc]
                wload(Wc[:, 0, :, :], ffn_w_in[l][:, fc * 128:(fc + 1) * 128], 8)
                wload(Wc[:, 1, :, :], ffn_w_in[l][:, DFF + fc * 128:DFF + (fc + 1) * 128], 8)
                pg = ps(); pu = ps()
                for k in range(8):
                    S.mm(pg[:, 0:32], Wc[:, 0, k, :], xnTs[:, k, :], start=(k == 0), stop=(k == 7))
                for k in range(8):
                    S.mm(pu[:, 0:32], Wc[:, 1, k, :], xnTs[:, k, :], start=(k == 0), stop=(k == 7))
                S.act(sg, pg[:, 0:32], ACTF.Silu)
                S.tt(actTs[:, fc, :], pu[:, 0:32], sg, ALU.mult)
            py = [ps(), ps()]
            for cb in range(2):
                for k in range(22):
                    S.mm(py[cb][P32, :], actTs[:, k, :], W2[:, k, cb * 512:(cb + 1) * 512], start=(k == 0), stop=(k == 21))
            post32(py, gt_a)
            if l == 1:
                S.dma(ys_o, xs_res[:])


        try:
            if not skip_sample:
                run_sample(0)
                ckpt("S0")
                run_sample(1)
                ckpt("S1")
            if sample_only:
                raise _Stop()
            run_pass("A", 0, xw[2048:4096, :], xw[0:2048, :], x1s[0:2048, :], 0, None)
            ckpt("A:F")
            run_pass("B", 0, xw[4096:6144, :], xw[2048:4096, :], x1s[2048:4096, :], 1, 0)
            ckpt("B:F")
            run_pass("C", 1, x1s[2048:4096, :], x1s[0:2048, :], y_o, 2, 1)
        except _Stop:
            pass
        if dbg:
            S.barrier()
            S.dma(dbg_o, arena[:])
        S.emit(st)
    return nc


def _etab():
    e = np.zeros((12, 128, 512), np.float32)
    kk = np.arange(128)[:, None].astype(np.float64)
    qq = np.arange(128)[None, :].astype(np.float64)
    for g in range(3):
        for c in range(4):
            for hh in range(2):
                j = 2 * c + hh
                slope = 2.0 ** (-8.0 * (j * 3 + g + 1.0) / 24.0)
                for half in range(2):
                    step = qq + 128 - kk if half == 0 else qq - kk
                    val = np.exp(-slope * DIL[g] * step)
                    val = np.where((step >= 0) & (step <= 128), val, 0.0)
                    e[g * 4 + c, :, hh * 256 + half * 128: hh * 256 + (half + 1) * 128] = val
    return e


_NC_CACHE = {}


def kernel(**inp):
    f = lambda k: np.ascontiguousarray(np.asarray(inp[k], dtype=np.float32))
    xp = f("x_prompt")
    if "nc" not in _NC_CACHE:
        _NC_CACHE["nc"] = build_nc()
    nc = _NC_CACHE["nc"]
    wnames = ["norm_pre_mix", "norm_post_mix", "norm_pre_ffn", "norm_post_ffn", "w_in", "a_norm_g", "a_norm_b",
              "a_w_s", "a_b_s", "b_conv_w", "b_conv_b", "b_norm_g", "b_norm_b", "w_branch", "w_out", "ffn_w_in",
              "ffn_w_out"]
    shared = {k: f(k) for k in wnames}
    shared["etab"] = _etab()
    shared["ident"] = np.eye(128, dtype=np.float32)
    shared["tril"] = np.tril(np.ones((128, 128), np.float32))
    o2 = np.zeros((128, 2, 128), np.float32)
    o2[:, 0, 0:64] = 1.0
    o2[:, 1, 64:128] = 1.0
    shared["ones2"] = o2.reshape(128, 256)
    bdm = np.zeros((32, 32), np.float32)
    for b_ in range(4):
        for s_ in range(8):
            for t_ in range(s_, 8):
                bdm[b_ * 8 + s_, b_ * 8 + t_] = 1.0
    shared["bdm"] = bdm
    xs_all = f("x_sample"); st_all = f("state_b_conv")
    cch = [f(k) for k in ("cache_c0_k", "cache_c0_v", "cache_c1_k", "cache_c1_v", "cache_c2_k", "cache_c2_v")]
    in_maps = []
    for c in range(8):
        b, seg = c // 4, (c % 4) * 2048
        xw = np.zeros((6144, 1024), np.float32)
        lo = seg - 4096
        s0 = max(lo, 0)
        xw[s0 - lo:] = xp[b, s0:seg + 2048]
        fl = np.zeros((128, 4), np.float32)
        fl[:, 0] = 1.0 if seg >= 4096 else 0.0
        fl[:, 1] = 1.0 if seg >= 2048 else 0.0
        fl[:, 2] = 1.0 if seg >= 2048 else 0.0
        m = dict(shared)
        m["xw"] = xw
        m["flags"] = fl
        bs = slice(c * 4, (c + 1) * 4)
        m["xs"] = np.ascontiguousarray(xs_all[bs].reshape(32, 1024))
        m["st"] = np.ascontiguousarray(st_all[:, bs])
        for nm, arr in zip(("c0k", "c0v", "c1k", "c1v", "c2k", "c2v"), cch):
            m[nm] = np.ascontiguousarray(arr[:, bs].reshape(2, 4, arr.shape[2], 512))
        in_maps.append(m)
    res = run_bass_kernel_spmd(nc, in_maps, core_ids=list(range(8)))
    R = res.results
    y_prompt = np.stack([np.concatenate([R[b * 4 + i]["y_o"] for i in range(4)], axis=0) for b in range(2)], 0)
    gt = np.stack([R[b * 4 + 3]["gt_o"] for b in range(2)], 1)
    new_b_conv_prompt = np.ascontiguousarray(np.transpose(gt, (0, 1, 3, 2))[:, :, 2:, :])
    ko = np.stack([R[b * 4 + 3]["k_o"] for b in range(2)], 2)
    vo = np.stack([R[b * 4 + 3]["v_o"] for b in range(2)], 2)
    kvp = []
    for g, wlen in enumerate((128, 512, 2048)):
        kvp.append(np.ascontiguousarray(ko[:, g, :, NT - wlen:, :]).reshape(2, 2, wlen, 8, 64))
        kvp.append(np.ascontiguousarray(vo[:, g, :, NT - wlen:, :]).reshape(2, 2, wlen, 8, 64))
    y_sample = np.concatenate([R[c]["ys_o"].reshape(4, 8, 1024) for c in range(8)], 0)
    new_b_conv_sample = np.concatenate([R[c]["nbs_o"] for c in range(8)], 1)
    new_a_v_sample = np.concatenate([R[c]["nav_o"].reshape(2, 4, 8, 512) for c in range(8)], 1)
    kvs = []
    for g in range(3):
        kvs.append(np.concatenate([R[c]["ks_o"][:, g].reshape(2, 4, 8, 8, 64) for c in range(8)], 1))
        kvs.append(np.concatenate([R[c]["vs_o"][:, g].reshape(2, 4, 8, 8, 64) for c in range(8)], 1))
    return (y_prompt, y_sample, new_b_conv_prompt, new_b_conv_sample, new_a_v_sample, *kvp, *kvs)
```

```python
import numpy as np
from concourse.bass_utils import run_bass_kernel_spmd
import concourse.bass as bass
import concourse.mybir as mybir

F32 = mybir.dt.float32
BF16 = mybir.dt.bfloat16
ALU = mybir.AluOpType
ACTF = mybir.ActivationFunctionType
AX = mybir.AxisListType

_DSZ = {F32: 4, BF16: 2, mybir.dt.int32: 4, mybir.dt.float32r: 4}


def _region(ap):
    t = ap.tensor
    name = t.name
    dsz = _DSZ.get(ap.dtype, 4)
    dims = list(ap.ap)
    off = int(ap.offset)
    space = str(ap.space)
    if space in ("SB", "PSUM"):
        pstep, pcnt = dims[0]
        if pstep == 0:
            pstep = 1 << 40
        p0 = off // pstep if pstep < (1 << 40) else 0
        f0 = off - p0 * pstep if pstep < (1 << 40) else off
        p1 = p0 + pcnt
        rest = dims[1:]
    else:
        p0, p1 = 0, 1
        f0 = off
        rest = dims
    lo = f0
    hi = f0
    for st, cn in rest:
        if cn <= 0:
            continue
        d = st * (cn - 1)
        if d < 0:
            lo += d
        else:
            hi += d
    return name, p0, p1, lo * dsz, (hi + 1) * dsz


class Sched:
    ENGS = ("tensor", "vector", "scalar", "gpsimd", "sync")

    def __init__(self, nc, n_dma_sems=24):
        self.nc = nc
        self.ops = []
        self.recs = {}
        self.n_dma_sems = n_dma_sems
        self.dma_count = {e: 0 for e in self.ENGS}
        self.dma_hist = {e: [] for e in self.ENGS}
        self.barrier_deps = {e: set() for e in self.ENGS}
        self.last = {e: None for e in self.ENGS}
        self.all_dmas = []

    def _access(self, ap, opid, is_write, deps):
        if str(ap.space) == "PSUM":
            name = ap.tensor.name
            eng = self.ops[opid]["eng"]
            rec = self.recs.setdefault(name, {})
            for e2, (last_any, last_w) in rec.items():
                if e2 != eng:
                    if last_any is not None and last_any != opid:
                        deps.add(last_any)
                else:
                    if is_write:
                        if last_any is not None and last_any != opid:
                            deps.add(last_any)
                    elif last_w is not None and last_w != opid:
                        deps.add(last_w)
            la, lw = rec.get(eng, (None, None))
            rec[eng] = (opid, opid if is_write else lw)
            return
        name, p0, p1, lo, hi = _region(ap)
        lst = self.recs.setdefault(name, [])
        keep = []
        eng = self.ops[opid]["eng"]
        isdma = self.ops[opid]["dma"]
        for r in lst:
            ov = not (r[1] <= p0 or p1 <= r[0] or r[3] <= lo or hi <= r[2])
            if ov and r[4] != opid:
                if is_write or r[5]:
                    deps.add(r[4])
                if is_write and r[0] >= p0 and r[1] <= p1 and r[2] >= lo and r[3] <= hi:
                    continue
            if (not is_write) and (not r[5]) and (not isdma) and r[4] != opid:
                ro = self.ops[r[4]]
                if ro["eng"] == eng and not ro["dma"] and r[0] == p0 and r[1] == p1 and r[2] == lo and r[3] == hi:
                    continue
            keep.append(r)
        keep.append([p0, p1, lo, hi, opid, is_write])
        self.recs[name] = keep

    def op(self, eng, fn, outs=(), ins=(), dma=False):
        opid = len(self.ops)
        o = {"eng": eng, "fn": fn, "deps": set(), "dma": dma}
        self.ops.append(o)
        deps = o["deps"]
        for a in ins:
            self._access(a, opid, False, deps)
        for a in outs:
            self._access(a, opid, True, deps)
        if self.barrier_deps[eng]:
            deps |= self.barrier_deps[eng]
            self.barrier_deps[eng] = set()
        if dma:
            h = self.dma_hist[eng]
            if len(h) >= self.n_dma_sems:
                deps.add(h[-self.n_dma_sems])
            h.append(opid)
            self.all_dmas.append(opid)
        self.last[eng] = opid
        return opid

    def barrier(self):
        d = set(x for x in self.last.values() if x is not None)
        d |= set(self.all_dmas[-64:])
        for e in self.ENGS:
            self.barrier_deps[e] = set(d)

    def dma(self, out, in_, eng="sync", **kw):
        return self.op(eng, lambda e: e.dma_start(out=out, in_=in_, **kw), [out], [in_], dma=True)

    def mm(self, out, lhsT, rhs, start=True, stop=True, **kw):
        return self.op("tensor", lambda e: e.matmul(out, lhsT, rhs, start=start, stop=stop, **kw),
                       [out], [lhsT, rhs])

    def tr(self, out, in_, ident):
        return self.op("tensor", lambda e: e.transpose(out, in_, ident), [out], [in_, ident])

    def act(self, out, in_, func, bias=None, scale=None, accum_out=None, eng="scalar"):
        kw = {}
        ins = [in_]
        outs = [out]
        if bias is not None:
            kw["bias"] = bias
            if not isinstance(bias, (int, float)):
                ins.append(bias)
        if scale is not None:
            kw["scale"] = scale
            if not isinstance(scale, (int, float)):
                ins.append(scale)
        if accum_out is not None:
            kw["accum_out"] = accum_out
            outs.append(accum_out)
        return self.op(eng, lambda e: e.activation(out, in_, func, **kw), outs, ins)

    def tt(self, out, in0, in1, op, eng="vector"):
        return self.op(eng, lambda e: e.tensor_tensor(out, in0, in1, op), [out], [in0, in1])

    def ts(self, out, in0, s1, s2, op0, op1=None, eng="vector", accum_out=None):
        ins = [in0] + [s for s in (s1, s2) if s is not None and not isinstance(s, (int, float))]
        outs = [out] + ([accum_out] if accum_out is not None else [])
        if op1 is None:
            return self.op(eng, lambda e: e.tensor_scalar(out, in0, s1, s2, op0), outs, ins)
        if accum_out is not None:
            return self.op(eng, lambda e: e.tensor_scalar(out, in0, s1, s2, op0, op1, accum_out), outs, ins)
        return self.op(eng, lambda e: e.tensor_scalar(out, in0, s1, s2, op0, op1), outs, ins)

    def stt(self, out, in0, scalar, in1, op0, op1, eng="vector"):
        ins = [in0, in1] + ([scalar] if not isinstance(scalar, (int, float)) else [])
        return self.op(eng, lambda e: e.scalar_tensor_tensor(out, in0, scalar, in1, op0, op1), [out], ins)

    def copy(self, out, in_, eng="vector"):
        if eng == "scalar":
            return self.op(eng, lambda e: e.copy(out, in_), [out], [in_])
        return self.op(eng, lambda e: e.tensor_copy(out, in_), [out], [in_])

    def memset(self, ap, val, eng="vector"):
        return self.op(eng, lambda e: e.memset(ap, val), [ap], [])

    def reduce(self, out, in_, op, axis=AX.X, eng="vector"):
        return self.op(eng, lambda e: e.tensor_reduce(out, in_, axis, op), [out], [in_])

    def recip(self, out, in_):
        return self.op("vector", lambda e: e.reciprocal(out, in_), [out], [in_])

    def emit(self, stack):
        nc = self.nc
        ops = self.ops
        needed = set()
        for o in ops:
            for d in o["deps"]:
                do = ops[d]
                if o["eng"] == "tensor" and do["eng"] == "tensor" and not do["dma"] and not o["dma"]:
                    continue
                needed.add(d)
        final_dmas = list(self.all_dmas)
        eng_sem = {e: stack.enter_context(nc.semaphore("se_" + e)) for e in self.ENGS}
        dma_sems = {e: [stack.enter_context(nc.semaphore("sd_%s_%d" % (e, i)))
                        for i in range(self.n_dma_sems)]
                    for e in self.ENGS if self.dma_count is not None and any(
                        (o["dma"] and o["eng"] == e) for o in ops)}
        cnt = {e: 0 for e in self.ENGS}
        dcount = {e: 0 for e in self.ENGS}
        dsemcnt = {}
        sig = {}
        per_eng = {e: [] for e in self.ENGS}
        for i, o in enumerate(ops):
            e = o["eng"]
            per_eng[e].append(i)
            if o["dma"]:
                k = dcount[e] % self.n_dma_sems
                dcount[e] += 1
                s = dma_sems[e][k]
                dsemcnt[(e, k)] = dsemcnt.get((e, k), 0) + 16
                sig[i] = (s, dsemcnt[(e, k)])
            elif i in needed:
                cnt[e] += 1
                sig[i] = (eng_sem[e], cnt[e])
        self.sig = sig

        plan = {e: [] for e in self.ENGS}
        for ename in self.ENGS:
            seen = {}
            for i in per_eng[ename]:
                o = ops[i]
                waits = []
                for d in sorted(o["deps"]):
                    do = ops[d]
                    if (not do["dma"]) and do["eng"] == ename and (ename == "tensor"):
                        continue
                    s_, v = sig[d]
                    if seen.get(s_.name, 0) >= v:
                        continue
                    seen[s_.name] = v
                    waits.append((s_.name, v))
                plan[ename].append((i, waits, (sig[i][0].name, 16 if o["dma"] else 1) if i in sig else None))
        semv = {}
        pc = {e: 0 for e in self.ENGS}
        progress = True
        while progress:
            progress = False
            for e in self.ENGS:
                while pc[e] < len(plan[e]):
                    i, waits, sg = plan[e][pc[e]]
                    if all(semv.get(n, 0) >= v for n, v in waits):
                        if sg is not None:
                            semv[sg[0]] = semv.get(sg[0], 0) + sg[1]
                        pc[e] += 1
                        progress = True
                    else:
                        break
        stuck = {e: (pc[e], len(plan[e])) for e in self.ENGS if pc[e] < len(plan[e])}
        if stuck:
            for e in stuck:
                i, waits, sg = plan[e][pc[e]]
                print("DEADLOCK", e, "op", i, "waits", [(n, v, semv.get(n, 0)) for n, v in waits])
            raise RuntimeError("scheduler deadlock: %s" % stuck)
        self.max_sem = dict(semv)

        block = stack.enter_context(nc.Block())

        def make(ename):
            def body(eh):
                seen = {}
                for i in per_eng[ename]:
                    o = ops[i]
                    for d in sorted(o["deps"]):
                        do = ops[d]
                        if (not do["dma"]) and do["eng"] == ename and (ename == "tensor"):
                            continue
                        s, v = sig[d]
                        if seen.get(s.name, 0) >= v:
                            continue
                        seen[s.name] = v
                        eh.wait_ge(s, v)
                    ins = o["fn"](eh)
                    if i in sig:
                        ins.then_inc(sig[i][0], 16 if o["dma"] else 1)
                if ename == "sync":
                    for d in final_dmas:
                        s, v = sig[d]
                        if seen.get(s.name, 0) >= v:
                            continue
                        seen[s.name] = v
                        eh.wait_ge(s, v)
                    for e2 in self.ENGS:
                        if e2 != "sync" and cnt[e2] > 0:
                            eh.wait_ge(eng_sem[e2], cnt[e2])
            return body

        for ename in self.ENGS:
            if per_eng[ename] or ename == "sync":
                getattr(block, ename)(make(ename))

import numpy as np
from contextlib import ExitStack

D = 1024
NT = 2048
DIN = 9728
OFF_AU, OFF_AV, OFF_B, OFF_CQ, OFF_CK, OFF_CV, OFF_G = 0, 512, 1024, 2048, 3584, 5120, 6656
DFF = 2816
DIL = (1, 4, 16)
EPS = 1e-6
SCALE = 0.125
KB = 1024
XF, XH, OT, PL = 0, 32 * KB, 64 * KB, 112 * KB
ARENA = 188 * KB


class _Stop(Exception):
    pass


def build_nc(stop=None, dbg=False, step=None, skip_sample=False, sample_only=False):
    nc = bass.Bass("TRN2", target_bir_lowering=False)
    din = lambda n, s: nc.dram_tensor(n, list(s), F32, kind="ExternalInput").ap()
    dout = lambda n, s: nc.dram_tensor(n, list(s), F32, kind="ExternalOutput").ap()
    xw = din("xw", [6144, D])
    flags_d = din("flags", [128, 4])
    etab_d = din("etab", [12, 128, 512])
    ident_d = din("ident", [128, 128])
    tril_d = din("tril", [128, 128])
    ones2_d = din("ones2", [128, 256])
    g_pre_mix = din("norm_pre_mix", [2, D]); g_post_mix = din("norm_post_mix", [2, D])
    g_pre_ffn = din("norm_pre_ffn", [2, D]); g_post_ffn = din("norm_post_ffn", [2, D])
    w_in = din("w_in", [2, D, DIN])
    a_norm_g = din("a_norm_g", [2, 512]); a_norm_b = din("a_norm_b", [2, 512])
    a_w_s = din("a_w_s", [2, 4, 128, 128]); a_b_s = din("a_b_s", [2, 4, 128])
    b_conv_w = din("b_conv_w", [2, 31, 512]); b_conv_b = din("b_conv_b", [2, 512])
    b_norm_g = din("b_norm_g", [2, 512]); b_norm_b = din("b_norm_b", [2, 512])
    w_branch = din("w_branch", [2, 1536, D]); w_out = din("w_out", [2, D, D])
    ffn_w_in = din("ffn_w_in", [2, D, 2 * DFF]); ffn_w_out = din("ffn_w_out", [2, DFF, D])

    xs_d = din("xs", [32, D])
    st_d = din("st", [2, 4, 30, 512])
    bdm_d = din("bdm", [32, 32])
    c0k = din("c0k", [2, 4, 128, 512]); c0v = din("c0v", [2, 4, 128, 512])
    c1k = din("c1k", [2, 4, 512, 512]); c1v = din("c1v", [2, 4, 512, 512])
    c2k = din("c2k", [2, 4, 2048, 512]); c2v = din("c2v", [2, 4, 2048, 512])
    ys_o = dout("ys_o", [32, D])
    nbs_o = dout("nbs_o", [2, 4, 30, 512])
    nav_o = dout("nav_o", [2, 32, 512])
    ks_o = dout("ks_o", [2, 3, 32, 512])
    vs_o = dout("vs_o", [2, 3, 32, 512])
    if dbg:
        x1s = nc.dram_tensor("x1s", [4096, D], F32, kind="ExternalOutput").ap()
        xmid = nc.dram_tensor("xmid", [NT, D], F32, kind="ExternalOutput").ap()
    else:
        x1s = nc.dram_tensor("x1s", [4096, D], F32).ap()
        xmid = nc.dram_tensor("xmid", [NT, D], F32).ap()

    y_o = dout("y_o", [NT, D])
    k_o = dout("k_o", [2, 3, NT, 512])
    v_o = dout("v_o", [2, 3, NT, 512])
    gt_o = dout("gt_o", [2, 512, 32])
    dbg_o = nc.dram_tensor("dbg_o", [128, ARENA // 2], BF16, kind="ExternalOutput").ap() if dbg else None

    with ExitStack() as st:
        sbt = lambda n, s, d: st.enter_context(nc.sbuf_tensor(n, list(s), d))
        arena = sbt("arena", [128, ARENA // 2], BF16)
        identb = sbt("identb", [128, 128], BF16)
        identf = sbt("identf", [128, 128], F32)
        trilf = sbt("trilf", [128, 128], F32)
        ones2 = sbt("ones2s", [128, 2, 128], BF16)
        etab = sbt("etabs", [128, 12, 512], BF16)
        flags = sbt("flagss", [128, 4], F32)
        stat = sbt("stat", [128, 256], F32)
        bs_sb = sbt("bs_sb", [128, 4], F32)
        bs32 = sbt("bs32", [32, 4], F32)
        bdm = sbt("bdm_s", [32, 32], F32)
        xs_res = sbt("xs_res", [32, D], F32)
        psb = [st.enter_context(nc.psum_tensor("psb%d" % i, [128, 512], F32)) for i in range(8)]
        S = Sched(nc)
        state = {"ps": 0, "st": 0, "alt": 0}

        def ps():
            state["ps"] = (state["ps"] + 1) % 8
            return psb[state["ps"]]

        def stc(n=1):
            i = state["st"]
            if i + n > 256:
                i = 0
            state["st"] = i + n
            return stat[:, i:i + n]

        def alt(a="vector", b="scalar"):
            state["alt"] ^= 1
            return a if state["alt"] else b

        def Vw(off, shape, dt=BF16):
            n = 1
            for s_ in shape:
                n *= s_
            dsz = 2 if dt == BF16 else 4
            v = arena[:, off // 2: off // 2 + n * dsz // 2]
            if dt != BF16:
                v = v.bitcast(dt)
            if len(shape) == 2:
                v = v.rearrange("p (a b) -> p a b", a=shape[0])
            elif len(shape) == 3:
                v = v.rearrange("p (a b c) -> p a b c", a=shape[0], b=shape[1])
            return v

        def dstep(n):
            if step == n and stop is not None and state.get("pass") == stop.split(":")[0]:
                raise _Stop()

        def ckpt(name):
            if stop == name:
                raise _Stop()

        def bcast_load(dst, row):
            S.dma(dst, row.partition_broadcast(128))

        def evac(out, in_, eng=None):
            eng = eng or alt()
            S.copy(out, in_, eng=eng)

        S.dma(identf[:], ident_d)
        S.dma(trilf[:], tril_d)
        S.dma(flags[:], flags_d)
        S.dma(ones2[:].rearrange("p a b -> p (a b)"), ones2_d, eng="gpsimd")
        S.dma(etab[:], etab_d.rearrange("n p c -> p n c"), eng="gpsimd")
        S.copy(identb[:], identf[:])
        S.dma(bdm[:], bdm_d)

        if step == 777:
            S.dma(v_o[0, 0][0:128, 0:128], identf[:])
        try:
            ckpt("const")
        except _Stop:
            S.barrier()
            S.dma(dbg_o, arena[:])
            S.emit(st)
            return nc

        def lockstep(gens):
            gens = list(gens)
            while gens:
                nxt = []
                for g_ in gens:
                    try:
                        next(g_)
                        nxt.append(g_)
                    except StopIteration:
                        pass
                gens = nxt

        def run1(gen):
            for _ in gen:
                pass

        def g_rstd(ss, n, eps=EPS, P=slice(0, 128)):
            m = stc()[P, :]
            S.ts(m, ss, 1.0 / n, eps, ALU.mult, ALU.add)
            yield
            S.act(m, m, ACTF.Sqrt)
            yield
            r = stc()[P, :]
            S.recip(r, m)
            yield
            return r

        def g_sumsq(src, junk, P=slice(0, 128)):
            ss = stc()[P, :]
            S.memset(ss, 0.0)
            yield
            S.act(junk, src, ACTF.Square, accum_out=ss)
            yield
            return ss

        def rstd_from_ss(ss, n, eps=EPS, P=slice(0, 128)):
            g_ = g_rstd(ss, n, eps, P)
            try:
                while True:
                    next(g_)
            except StopIteration as e_:
                return e_.value

        def sumsq(src, junk, P=slice(0, 128)):
            g_ = g_sumsq(src, junk, P)
            try:
                while True:
                    next(g_)
            except StopIteration as e_:
                return e_.value

        def g_norm_to_T(xt, gtile, dstT, tile, junk, xnb):
            ss = yield from g_sumsq(xt, junk)
            r = yield from g_rstd(ss, D)
            S.stt(xnb, xt, r, gtile, ALU.mult, ALU.mult)
            yield
            pt = ps()[:].bitcast(BF16)
            for k in range(8):
                S.tr(pt[:, k * 128:(k + 1) * 128], xnb[:, k * 128:(k + 1) * 128], identb[:])
            yield
            evac(dstT[:, :, tile * 128:(tile + 1) * 128], pt[:, 0:1024].rearrange("p (k c) -> p k c", k=8))
            yield

        def phase_norm(src, grow, dstT, ntiles, tile0=0):
            gtile = Vw(PL + 20 * KB, [1024], F32)
            bcast_load(gtile, grow)

            def body(i, sl_):
                xt = Vw(PL + sl_ * 4 * KB, [1024], F32)
                S.dma(xt, src[i * 128:(i + 1) * 128, :])
                yield
                yield from g_norm_to_T(xt, gtile, dstT, tile0 + i, Vw(PL + 8 * KB + sl_ * 4 * KB, [1024], F32),
                                       Vw(PL + 16 * KB + sl_ * 2 * KB, [1024]))
            for i0 in range(0, ntiles, 2):
                lockstep([body(i0, 0), body(i0 + 1, 1)])

        def g_layernorm512(src, gt, bt, out, toff):
            junk = Vw(toff, [512], F32)
            tmp = Vw(toff + 2 * KB, [512], F32)
            sm = stc()
            S.reduce(sm, src, ALU.add)
            yield
            sq = yield from g_sumsq(src, junk)
            mean = stc()
            S.ts(mean, sm, 1.0 / 512, 0.0, ALU.mult, ALU.add)
            yield
            msq = stc()
            S.tt(msq, mean, mean, ALU.mult)
            yield
            var = stc()
            S.stt(var, sq, 1.0 / 512, msq, ALU.mult, ALU.subtract)
            yield
            S.ts(var, var, 1.0, EPS, ALU.mult, ALU.add)
            yield
            S.act(var, var, ACTF.Sqrt)
            yield
            r = stc()
            S.recip(r, var)
            yield
            S.ts(tmp, src, mean, r, ALU.subtract, ALU.mult)
            yield
            S.tt(tmp, tmp, gt, ALU.mult)
            yield
            S.tt(out, tmp, bt, ALU.add)
            yield

        def to_featT(src_bf, dstT, tile):
            pt = ps()[:].bitcast(BF16)
            for c in range(4):
                S.tr(pt[:, c * 128:(c + 1) * 128], src_bf[:, c * 128:(c + 1) * 128], identb[:])
            evac(dstT[:, :, tile * 128:(tile + 1) * 128],
                 pt[:, 0:512].rearrange("p (c t) -> p c t", c=4))

        def wload(dst, src2d, kc):
            S.dma(dst, src2d.rearrange("(k p) c -> p k c", p=128), eng="gpsimd")

        def run_pass(pname, l, xsrc, hsrc, xdst, fcol, out_l):
            state["pass"] = pname
            xnT_f = Vw(XF, [8, NT])
            xnT_h = Vw(XH, [8, NT])
            mergedT = xnT_h
            oT = [Vw(OT + n * 16 * KB, [4, NT]) for n in range(3)]
            S.barrier()
            phase_norm(hsrc, g_pre_mix[l:l + 1, :], xnT_h, 16)
            phase_norm(xsrc, g_pre_mix[l:l + 1, :], xnT_f, 16)

            ckpt("%s:N" % pname)
            WA = Vw(PL, [8, 1024])
            wload(WA, w_in[l][:, OFF_AU:OFF_AU + 1024], 8)
            agt = Vw(PL + 16 * KB, [512], F32); abt = Vw(PL + 18 * KB, [512], F32)
            bcast_load(agt, a_norm_g[l:l + 1, :]); bcast_load(abt, a_norm_b[l:l + 1, :])
            WsT = Vw(PL + 20 * KB, [4, 128])
            wtmp = Vw(PL + 21 * KB, [4, 128], F32)
            wtmpb = Vw(PL + 23 * KB, [4, 128])
            S.dma(wtmp, a_w_s[l].rearrange("g t s -> t g s"))
            S.dma(bs_sb[:], a_b_s[l].rearrange("g t -> t g"), allow_slow_non_contiguous=True)
            for g in range(4):
                S.tt(wtmpb[:, g, :], wtmp[:, g, :], trilf[:], ALU.mult)
            pt = ps()[:].bitcast(BF16)
            for g in range(4):
                S.tr(pt[:, g * 128:(g + 1) * 128], wtmpb[:, g, :], identb[:])
            evac(WsT, pt[:, 0:512].rearrange("p (g t) -> p g t", g=4))
            def bodyA(i, sl_):
                TA = PL + 24 * KB + sl_ * 14 * KB
                u_sb = Vw(TA, [512], F32)
                v_sb = Vw(TA + 2 * KB, [512], F32)
                vnb = Vw(TA + 4 * KB, [512])
                oab = Vw(TA + 5 * KB, [512])
                pm_sb = Vw(TA + 10 * KB, [512], F32)
                pu = ps(); pv = ps()
                for k in range(8):
                    S.mm(pu[:], xnT_f[:, k, i * 128:(i + 1) * 128], WA[:, k, 0:512], start=(k == 0), stop=(k == 7))
                for k in range(8):
                    S.mm(pv[:], xnT_f[:, k, i * 128:(i + 1) * 128], WA[:, k, 512:1024], start=(k == 0), stop=(k == 7))
                yield
                S.copy(u_sb, pu[:], eng="scalar")
                S.copy(v_sb, pv[:], eng="vector")
                yield
                yield from g_layernorm512(v_sb, agt, abt, vnb, TA + 6 * KB)
                pm = ps()
                for g in range(4):
                    S.mm(pm[:, g * 128:(g + 1) * 128], WsT[:, g, :], vnb[:, g * 128:(g + 1) * 128])
                yield
                S.copy(pm_sb, pm[:], eng="scalar")
                yield
                for g in range(4):
                    S.stt(oab[:, g * 128:(g + 1) * 128], pm_sb[:, g * 128:(g + 1) * 128], bs_sb[:, g:g + 1],
                          u_sb[:, g * 128:(g + 1) * 128], ALU.add, ALU.mult)
                yield
                pt = ps()[:].bitcast(BF16)
                for c in range(4):
                    S.tr(pt[:, c * 128:(c + 1) * 128], oab[:, c * 128:(c + 1) * 128], identb[:])
                yield
                evac(oT[0][:, :, i * 128:(i + 1) * 128], pt[:, 0:512].rearrange("p (c t) -> p c t", c=4))
                yield
            for i0 in range(0, 16, 2):
                lockstep([bodyA(i0, 0), bodyA(i0 + 1, 1)])

            ckpt("%s:A" % pname)
            S.barrier()
            WB = Vw(PL, [8, 1024])
            wload(WB, w_in[l][:, OFF_B:OFF_B + 1024], 8)
            gluT = Vw(PL + 16 * KB, [4, 2176])
            diag = Vw(PL + 33 * KB, [124, 128])
            TB = OT + 32 * KB
            bgt = Vw(TB, [512], F32); bbt = Vw(TB + 2 * KB, [512], F32); cbt = Vw(TB + 4 * KB, [512], F32)
            bcast_load(bgt, b_norm_g[l:l + 1, :]); bcast_load(bbt, b_norm_b[l:l + 1, :]); bcast_load(cbt, b_conv_b[l:l + 1, :])
            ysb = Vw(TB + 6 * KB, [512], F32)
            sig = Vw(TB + 8 * KB, [512], F32)
            obb = Vw(TB + 10 * KB, [512])
            cw = Vw(TB + 11 * KB, [512], F32)
            cwT = Vw(TB + 13 * KB, [4, 32], F32)
            gt32 = Vw(TB + 13 * KB + 512, [4, 32], F32)
            lnout = Vw(TB + 14 * KB, [512], F32)
            for j in range(31):
                S.dma(cwT[:, :, j], b_conv_w[l, j].rearrange("(c p) -> p c", p=128), allow_slow_non_contiguous=True)
            for c in range(4):
                for j in range(31):
                    S.ts(diag[:, c * 31 + j, :], identb[:], cwT[:, c, j:j + 1], 1.0, ALU.mult, ALU.mult,
                         eng=("vector" if j % 2 == 0 else "gpsimd"))
            def glu_block(rhs_of_k, n, dst_cols, tail=None):
                for c in range(4):
                    pa = ps(); pg = ps()
                    for k in range(8):
                        S.mm(pa[:, 0:n], WB[:, k, c * 128:(c + 1) * 128], rhs_of_k(k), start=(k == 0), stop=(k == 7))
                    for k in range(8):
                        S.mm(pg[:, 0:n], WB[:, k, 512 + c * 128:512 + (c + 1) * 128], rhs_of_k(k), start=(k == 0), stop=(k == 7))
                    S.act(sig[:, 0:n], pg[:, 0:n], ACTF.Sigmoid)
                    S.tt(gluT[:, c, dst_cols:dst_cols + n], pa[:, 0:n], sig[:, 0:n], ALU.mult)
                    if tail is not None:
                        S.tt(gt32[:, c, :], pa[:, n - 32:n], sig[:, n - 32:n], ALU.mult)
            glu_block(lambda k: xnT_h[:, k, NT - 128:NT], 128, 0)
            for c in range(4):
                S.ts(gluT[:, c, 0:128], gluT[:, c, 0:128], flags[:, fcol:fcol + 1], 1.0, ALU.mult, ALU.mult)
            for w in range(4):
                glu_block(lambda k, w=w: xnT_f[:, k, w * 512:(w + 1) * 512], 512, 128 + w * 512,
                          tail=(out_l is not None and w == 3) or None)
            if out_l is not None:
                S.dma(gt_o[out_l].rearrange("(c p) t -> p c t", p=128), gt32)
            def bodyB(i, sl_):
                ysb_ = ysb if sl_ == 0 else Vw(TB + 11 * KB, [512], F32)
                lnout_ = lnout if sl_ == 0 else Vw(PL + 72 * KB, [512], F32)
                obb_ = obb if sl_ == 0 else Vw(PL + 74 * KB, [512])
                pc = ps()
                for c in range(4):
                    for j in range(31):
                        s0 = 128 + i * 128 - 30 + j
                        S.mm(pc[:, c * 128:(c + 1) * 128], gluT[:, c, s0:s0 + 128], diag[:, c * 31 + j, :],
                             start=(j == 0), stop=(j == 30))
                yield
                S.tt(ysb_, pc[:], cbt, ALU.add)
                yield
                yield from g_layernorm512(ysb_, bgt, bbt, lnout_, PL + 64 * KB + sl_ * 4 * KB)
                S.act(obb_, lnout_, ACTF.Silu)
                yield
                pt = ps()[:].bitcast(BF16)
                for c in range(4):
                    S.tr(pt[:, c * 128:(c + 1) * 128], obb_[:, c * 128:(c + 1) * 128], identb[:])
                yield
                evac(oT[1][:, :, i * 128:(i + 1) * 128], pt[:, 0:512].rearrange("p (c t) -> p c t", c=4))
                yield
            for i0 in range(0, 16, 2):
                lockstep([bodyB(i0, 0), bodyB(i0 + 1, 1)])

            ckpt("%s:B" % pname)
            S.barrier()
            WC = Vw(PL, [9, 8, 128])
            QT = Vw(PL + 18 * KB, [2, NT])
            KTb = Vw(PL + 26 * KB, [1, 4096])[:, 0, :]
            Vt = Vw(PL + 34 * KB, [32, 2, 128])
            acc = Vw(PL + 50 * KB, [2, NT], F32)
            Pt2 = [Vw(PL + 66 * KB, [512]), Vw(PL + 67 * KB, [512]), Vw(PL + 73 * KB, [512]), Vw(PL + 74 * KB, [512])]
            kst = Vw(PL + 68 * KB, [512], F32)
            import os
            vst = Vw(PL + (68 if os.environ.get("VST68") else 70) * KB, [512], F32)
            Eh = Vw(PL + 72 * KB, [512])
            S.memset(Vt.rearrange("p a b c -> p (a b c)"), 0.0, eng="gpsimd")
            S.memset(QT[64:128, 0, :], 0.0, eng="gpsimd")
            S.memset(QT[0:64, 1, :], 0.0, eng="gpsimd")
            for c in range(4):
                for g in range(3):
                    for j, off in enumerate((OFF_CQ, OFF_CK, OFF_CV)):
                        wload(WC[:, g * 3 + j, :, :], w_in[l][:, off + g * 512 + c * 128: off + g * 512 + (c + 1) * 128], 8)
                for g in range(3):
                    d = DIL[g]
                    Lh = 128 * d
                    nb = 16 // d
                    Wq, Wk, Wv = WC[:, g * 3 + 0], WC[:, g * 3 + 1], WC[:, g * 3 + 2]
                    E = etab[:, g * 4 + c, :]
                    for hh in range(2):
                        S.ts(Eh[:, hh * 256:hh * 256 + 128], E[:, hh * 256:hh * 256 + 128], flags[:, fcol:fcol + 1], 1.0,
                             ALU.mult, ALU.mult)
                        S.copy(Eh[:, hh * 256 + 128:hh * 256 + 256], E[:, hh * 256 + 128:hh * 256 + 256], eng="gpsimd")
                    for w in range(4):
                        pq = ps(); pk = ps()
                        for k in range(8):
                            S.mm(pq[:], Wq[:, k, :], xnT_f[:, k, w * 512:(w + 1) * 512], start=(k == 0), stop=(k == 7))
                        for k in range(8):
                            S.mm(pk[:], Wk[:, k, :], xnT_f[:, k, w * 512:(w + 1) * 512], start=(k == 0), stop=(k == 7))
                        S.copy(QT[0:64, 0, w * 512:(w + 1) * 512], pq[0:64, :], eng="vector")
                        S.copy(QT[64:128, 1, w * 512:(w + 1) * 512], pq[64:128, :], eng="scalar")
                        evac(KTb[:, Lh + w * 512:Lh + (w + 1) * 512], pk[:])
                    hw = min(512, Lh)
                    for w in range(Lh // hw):
                        pk = ps()
                        c0 = NT - Lh + w * hw
                        for k in range(8):
                            S.mm(pk[:, 0:hw], Wk[:, k, :], xnT_h[:, k, c0:c0 + hw], start=(k == 0), stop=(k == 7))
                        evac(KTb[:, w * hw:(w + 1) * hw], pk[:, 0:hw])
                    htiles = [("h", r, 0) for r in range(d)]
                    ftiles = [("f", r, jb) for r in range(d) for jb in range(nb)]
                    groups = [(t0, htiles[t0:t0 + 4]) for t0 in range(0, d, 4)] + \
                             [(d + t0, ftiles[t0:t0 + 4]) for t0 in range(0, 16, 4)]
                    vo2 = v_o[out_l, g] if out_l is not None else None
                    for (t0, grp) in groups:
                        pv = ps()
                        for q, (kind, r, jb) in enumerate(grp):
                            if kind == "h":
                                srcT = xnT_h; s0 = NT - Lh + r
                            else:
                                srcT = xnT_f; s0 = r + d * 128 * jb
                            for k in range(8):
                                S.mm(pv[:, q * 128:(q + 1) * 128], srcT[:, k, s0:s0 + 127 * d + 1:d],
                                     Wv[:, k, :], start=(k == 0), stop=(k == 7))
                        n = len(grp)
                        pv3 = pv[:, 0:n * 128].rearrange("p (t e) -> p t e", t=n)
                        S.copy(Vt[:, t0:t0 + n, 0, 0:64], pv3[:, :, 0:64], eng="vector")
                        S.copy(Vt[:, t0:t0 + n, 1, 64:128], pv3[:, :, 64:128], eng="scalar")
                        if out_l is not None and grp[0][0] == "f":
                            S.copy(vst, pv[:], eng="scalar")
                            cs = slice(c * 128, (c + 1) * 128)
                            _, r0, jb0 = grp[0]
                            if g == 0:
                                dst = vo2.rearrange("(q p) e -> p q e", p=128)[:, jb0:jb0 + 4, cs]
                            elif g == 1:
                                dst = vo2.rearrange("(q p dd) e -> p q dd e", p=128, dd=4)[:, :, r0, cs]
                            else:
                                dst = vo2.rearrange("(p dd) e -> p dd e", dd=16)[:, r0:r0 + 4, cs]
                            S.dma(dst, vst.rearrange("p (t e) -> p t e", t=4))
                    import os
                    if out_l is not None and not os.environ.get("NOKOUT"):
                        for t0 in range(0, 16, 4):
                            pk = ps()
                            for q in range(4):
                                i = t0 + q
                                for k in range(8):
                                    S.mm(pk[:, q * 128:(q + 1) * 128], xnT_f[:, k, i * 128:(i + 1) * 128], Wk[:, k, :],
                                         start=(k == 0), stop=(k == 7))
                            S.copy(kst, pk[:], eng="scalar")
                            S.dma(k_o[out_l, g][t0 * 128:(t0 + 4) * 128, c * 128:(c + 1) * 128].rearrange("(t p) e -> p t e", p=128),
                                  kst.rearrange("p (t e) -> p t e", t=4))
                    def blockC(r, jb, Pt, g=g, d=d, Lh=Lh, nb=nb, E=E):
                        qs = r + d * 128 * jb
                        sl = lambda s_: slice(s_, s_ + 127 * d + 1, d)
                        pss = ps()
                        for hh in range(2):
                            S.mm(pss[:, hh * 256:hh * 256 + 128], KTb[:, sl(Lh + qs - 128 * d)], QT[:, hh, sl(qs)])
                            S.mm(pss[:, hh * 256 + 128:hh * 256 + 256], KTb[:, sl(Lh + qs)], QT[:, hh, sl(qs)])
                        yield
                        S.act(Pt, pss[:], ACTF.Exp, scale=SCALE)
                        yield
                        S.tt(Pt, Pt, (Eh if jb == 0 else E), ALU.mult, eng="gpsimd")
                        yield
                        t1 = d + r * nb + jb
                        th0 = r if jb == 0 else t1 - 1
                        pso = ps()
                        seq = [(hh, half) for hh in range(2) for half in range(2)]
                        for n_, (hh, half) in enumerate(seq):
                            S.mm(pso[:, 0:128], Vt[:, (th0 if half == 0 else t1), hh, :],
                                 Pt[:, hh * 256 + half * 128:hh * 256 + (half + 1) * 128], start=(n_ == 0), stop=(n_ == 3))
                        for n_, (hh, half) in enumerate(seq):
                            S.mm(pso[:, 128:256], ones2[:, hh, :],
                                 Pt[:, hh * 256 + half * 128:hh * 256 + (half + 1) * 128], start=(n_ == 0), stop=(n_ == 3))
                        yield
                        dst = acc[:, :, sl(qs)]
                        src = pso[:, 0:256].rearrange("p (a q) -> p a q", a=2)
                        if g == 0:
                            S.copy(dst, src, eng="vector")
                        else:
                            S.tt(dst, dst, src, ALU.add)
                        yield
                    blks = [(r, jb) for r in range(d) for jb in range(nb)]
                    for b0 in range(0, 16, 4):
                        lockstep([blockC(blks[b0 + q][0], blks[b0 + q][1], Pt2[q]) for q in range(4)])
                S.recip(acc[:, 1, :], acc[:, 1, :])
                S.tt(oT[2][:, c, :], acc[:, 0, :], acc[:, 1, :], ALU.mult)

            ckpt("%s:C" % pname)
            S.barrier()
            Wg = Vw(PL, [3, 8, 128])
            Wb = Vw(PL + 6 * KB, [3, 4, 128])
            macc = Vw(PL + 10 * KB, [512], F32)
            sgm = Vw(PL + 12 * KB, [512], F32)
            mtmp = Vw(PL + 14 * KB, [512], F32)
            Wo = Vw(PL + 16 * KB, [8, 1024])
            gpost = Vw(PL + 32 * KB, [1024], F32)
            wload(Wo, w_out[l], 8)
            bcast_load(gpost, g_post_mix[l:l + 1, :])
            for dc in range(8):
                for n in range(3):
                    wload(Wg[:, n, :, :], w_in[l][:, OFF_G + n * 1024 + dc * 128: OFF_G + n * 1024 + (dc + 1) * 128], 8)
                    wload(Wb[:, n, :, :], w_branch[l][n * 512:(n + 1) * 512, dc * 128:(dc + 1) * 128], 4)
                for w in range(4):
                    ws = slice(w * 512, (w + 1) * 512)
                    for n in range(3):
                        pg = ps(); pp = ps()
                        for k in range(8):
                            S.mm(pg[:], Wg[:, n, k, :], xnT_f[:, k, ws], start=(k == 0), stop=(k == 7))
                        for k in range(4):
                            S.mm(pp[:], Wb[:, n, k, :], oT[n][:, k, ws], start=(k == 0), stop=(k == 3))
                        S.act(sgm, pg[:], ACTF.Sigmoid)
                        if n == 0:
                            S.tt(macc, pp[:], sgm, ALU.mult)
                        else:
                            S.tt(mtmp, pp[:], sgm, ALU.mult)
                            if n == 1:
                                S.tt(macc, macc, mtmp, ALU.add, eng="gpsimd")
                            else:
                                S.tt(mergedT[:, dc, ws], macc, mtmp, ALU.add, eng="gpsimd")
            gpf = Vw(PL + 72 * KB, [1024], F32)
            bcast_load(gpf, g_pre_ffn[l:l + 1, :])

            def bodyM(i, sl_):
                TM = PL + 36 * KB + sl_ * 18 * KB
                ysb2 = Vw(TM, [1024], F32)
                junk2 = Vw(TM + 4 * KB, [1024], F32)
                xt = Vw(TM + 8 * KB, [1024], F32)
                njunk = Vw(TM + 12 * KB, [1024], F32)
                nxnb = Vw(TM + 16 * KB, [1024])
                S.dma(xt, xsrc[i * 128:(i + 1) * 128, :])
                py = [ps(), ps()]
                for cb in range(2):
                    for k in range(8):
                        S.mm(py[cb][:], mergedT[:, k, i * 128:(i + 1) * 128], Wo[:, k, cb * 512:(cb + 1) * 512],
                             start=(k == 0), stop=(k == 7))
                yield
                S.copy(ysb2[:, 0:512], py[0][:], eng="scalar")
                S.copy(ysb2[:, 512:1024], py[1][:], eng="vector")
                yield
                ss = yield from g_sumsq(ysb2, junk2)
                r = yield from g_rstd(ss, D)
                S.stt(ysb2, ysb2, r, gpost, ALU.mult, ALU.mult)
                yield
                S.tt(xt, xt, ysb2, ALU.add)
                yield
                S.dma(xmid[i * 128:(i + 1) * 128, :], xt)
                yield from g_norm_to_T(xt, gpf, xnT_f, i, njunk, nxnb)
            for i0 in range(0, 16, 2):
                lockstep([bodyM(i0, 0), bodyM(i0 + 1, 1)])

            ckpt("%s:M" % pname)
            S.barrier()
            actT = Vw(OT, [22, 1024])
            W2 = Vw(PL, [22, 1024])
            wload(W2, ffn_w_out[l], 22)
            W1 = [Vw(PL + 44 * KB + q * 4 * KB, [2, 8, 128]) for q in range(2)]
            sg = Vw(PL + 52 * KB, [512], F32)
            gpost2 = Vw(PL + 54 * KB, [1024], F32)
            bcast_load(gpost2, g_post_ffn[l:l + 1, :])
            ysb2 = Vw(PL + 58 * KB, [1024], F32)
            junk2 = Vw(PL + 62 * KB, [1024], F32)
            for hf in range(2):
                for fc in range(22):
                    Wc = W1[fc % 2]
                    wload(Wc[:, 0, :, :], ffn_w_in[l][:, fc * 128:(fc + 1) * 128], 8)
                    wload(Wc[:, 1, :, :], ffn_w_in[l][:, DFF + fc * 128:DFF + (fc + 1) * 128], 8)
                    for w in range(2):
                        ws = slice(hf * 1024 + w * 512, hf * 1024 + (w + 1) * 512)
                        pg = ps(); pu = ps()
                        for k in range(8):
                            S.mm(pg[:], Wc[:, 0, k, :], xnT_f[:, k, ws], start=(k == 0), stop=(k == 7))
                        for k in range(8):
                            S.mm(pu[:], Wc[:, 1, k, :], xnT_f[:, k, ws], start=(k == 0), stop=(k == 7))
                        S.act(sg, pg[:], ACTF.Silu)
                        S.tt(actT[:, fc, w * 512:(w + 1) * 512], pu[:], sg, ALU.mult)
                def bodyF(i8, sl_, hf=hf):
                    i = hf * 8 + i8
                    ysb_ = Vw(PL + 58 * KB + sl_ * 8 * KB, [1024], F32)
                    xt = Vw(PL + 62 * KB + sl_ * 8 * KB, [1024], F32)
                    junk_ = Vw(PL + 74 * KB, [1024])
                    S.dma(xt, xmid[i * 128:(i + 1) * 128, :])
                    py = [ps(), ps()]
                    for cb in range(2):
                        for k in range(22):
                            S.mm(py[cb][:], actT[:, k, i8 * 128:(i8 + 1) * 128], W2[:, k, cb * 512:(cb + 1) * 512],
                                 start=(k == 0), stop=(k == 21))
                    yield
                    S.copy(ysb_[:, 0:512], py[0][:], eng="scalar")
                    S.copy(ysb_[:, 512:1024], py[1][:], eng="vector")
                    yield
                    ss = yield from g_sumsq(ysb_, junk_)
                    r = yield from g_rstd(ss, D)
                    S.stt(ysb_, ysb_, r, gpost2, ALU.mult, ALU.mult)
                    yield
                    S.tt(xt, xt, ysb_, ALU.add)
                    yield
                    S.dma(xdst[i * 128:(i + 1) * 128, :], xt)
                    yield
                for i0 in range(0, 8, 2):
                    lockstep([bodyF(i0, 0), bodyF(i0 + 1, 1)])

        def run_sample(l):
            state["pass"] = "S%d" % l
            S.barrier()
            A0 = 0
            xnTs = Vw(A0, [8, 32])
            oTs = [Vw(A0 + 1 * KB + n * 256, [4, 32]) for n in range(3)]
            mergedTs = Vw(A0 + 2 * KB, [8, 32])
            actTs = Vw(A0 + 3 * KB, [22, 32])
            T0 = 8 * KB
            junk = Vw(T0, [1024], F32)
            ysb = Vw(T0 + 4 * KB, [1024], F32)
            gt_a = Vw(T0 + 8 * KB, [1024], F32)
            gt_b = Vw(T0 + 12 * KB, [1024], F32)
            xnb = Vw(T0 + 16 * KB, [1024])
            W0 = 56 * KB

            def norm32(src, gtile):
                ss = sumsq(src[0:32, :], junk[0:32, :], P=slice(0, 32))
                r = rstd_from_ss(ss, D, P=slice(0, 32))
                S.stt(xnb[0:32, :], src[0:32, :], r, gtile[0:32, :], ALU.mult, ALU.mult)
                pt = ps()[:].bitcast(BF16)
                for k in range(8):
                    S.tr(pt[:, k * 32:(k + 1) * 32], xnb[0:32, k * 128:(k + 1) * 128], identb[0:32, 0:32])
                S.copy(xnTs, pt[:, 0:256].rearrange("p (k c) -> p k c", k=8), eng="vector")

            def ln32(src, gt, bt, out, toff):
                jk = Vw(toff, [512], F32)
                tmp = Vw(toff + 2 * KB, [512], F32)
                P = slice(0, 32)
                sm = stc(); S.reduce(sm[P, :], src[P, :], ALU.add)
                sq = stc(); S.memset(sq[P, :], 0.0); S.act(jk[P, :], src[P, :], ACTF.Square, accum_out=sq[P, :])
                mean = stc(); S.ts(mean[P, :], sm[P, :], 1.0 / 512, 0.0, ALU.mult, ALU.add)
                msq = stc(); S.tt(msq[P, :], mean[P, :], mean[P, :], ALU.mult)
                var = stc(); S.stt(var[P, :], sq[P, :], 1.0 / 512, msq[P, :], ALU.mult, ALU.subtract)
                S.ts(var[P, :], var[P, :], 1.0, EPS, ALU.mult, ALU.add)
                S.act(var[P, :], var[P, :], ACTF.Sqrt)
                r = stc(); S.recip(r[P, :], var[P, :])
                S.ts(tmp[P, :], src[P, :], mean[P, :], r[P, :], ALU.subtract, ALU.mult)
                S.tt(tmp[P, :], tmp[P, :], gt[P, :], ALU.mult)
                S.tt(out[P, :], tmp[P, :], bt[P, :], ALU.add)

            def featT32(src_bf, dstT):
                pt = ps()[:].bitcast(BF16)
                for c in range(4):
                    S.tr(pt[:, c * 32:(c + 1) * 32], src_bf[0:32, c * 128:(c + 1) * 128], identb[0:32, 0:32])
                S.copy(dstT, pt[:, 0:128].rearrange("p (c t) -> p c t", c=4), eng="vector")

            P32 = slice(0, 32)
            bcast_load(gt_a, g_pre_mix[l:l + 1, :])
            if l == 0:
                S.dma(xs_res[:], xs_d)
            norm32(xs_res, gt_a)

            WA = Vw(W0, [8, 1024])
            wload(WA, w_in[l][:, OFF_AU:OFF_AU + 1024], 8)
            agt = Vw(T0 + 18 * KB, [512], F32); abt = Vw(T0 + 20 * KB, [512], F32)
            bcast_load(agt, a_norm_g[l:l + 1, :]); bcast_load(abt, a_norm_b[l:l + 1, :])
            BDf = Vw(T0 + 22 * KB, [4, 32], F32)
            BDb = Vw(T0 + 23 * KB, [4, 32])
            S.memset(BDf[P32], 0.0)
            for b in range(4):
                for g in range(4):
                    S.dma(BDf[b * 8:(b + 1) * 8, g, b * 8:(b + 1) * 8], a_w_s[l, g, 0:8, 0:8].rearrange("t s -> s t"),
                          allow_slow_non_contiguous=True)
                S.dma(bs32[b * 8:(b + 1) * 8, :], a_b_s[l][:, 0:8].rearrange("g t -> t g"), allow_slow_non_contiguous=True)
            for g in range(4):
                S.tt(BDb[P32, g, :], BDf[P32, g, :], bdm[:], ALU.mult)
            u_sb = Vw(T0 + 24 * KB, [512], F32); v_sb = Vw(T0 + 26 * KB, [512], F32)
            vn32 = Vw(T0 + 28 * KB, [512], F32); vnb = Vw(T0 + 30 * KB, [512]); oab = Vw(T0 + 31 * KB, [512])
            pm_sb = Vw(T0 + 32 * KB, [512], F32)
            pu = ps(); pv = ps()
            for k in range(8):
                S.mm(pu[P32, :], xnTs[:, k, :], WA[:, k, 0:512], start=(k == 0), stop=(k == 7))
            for k in range(8):
                S.mm(pv[P32, :], xnTs[:, k, :], WA[:, k, 512:1024], start=(k == 0), stop=(k == 7))
            S.copy(u_sb[P32], pu[P32, :], eng="scalar")
            S.copy(v_sb[P32], pv[P32, :], eng="vector")
            ln32(v_sb, agt, abt, vn32, T0 + 34 * KB)
            S.dma(nav_o[l], vn32[P32])
            S.copy(vnb[P32], vn32[P32], eng="vector")
            pm = ps()
            for g in range(4):
                S.mm(pm[P32, g * 128:(g + 1) * 128], BDb[P32, g, :], vnb[P32, g * 128:(g + 1) * 128])
            S.copy(pm_sb[P32], pm[P32, :], eng="scalar")
            for g in range(4):
                S.stt(oab[P32, g * 128:(g + 1) * 128], pm_sb[P32, g * 128:(g + 1) * 128], bs32[:, g:g + 1],
                      u_sb[P32, g * 128:(g + 1) * 128], ALU.add, ALU.mult)
            featT32(oab, oTs[0])

            S.barrier()
            WB = Vw(W0 + 16 * KB, [8, 1024])
            wload(WB, w_in[l][:, OFF_B:OFF_B + 1024], 8)
            diag = Vw(W0 + 32 * KB, [124, 128])
            cwT = Vw(T0 + 18 * KB, [4, 32], F32)
            for j in range(31):
                S.dma(cwT[:, :, j], b_conv_w[l, j].rearrange("(c p) -> p c", p=128), allow_slow_non_contiguous=True)
            for c in range(4):
                for j in range(31):
                    S.ts(diag[:, c * 31 + j, :], identb[:], cwT[:, c, j:j + 1], 1.0, ALU.mult, ALU.mult,
                         eng=("vector" if j % 2 == 0 else "gpsimd"))
            bgt = Vw(T0 + 20 * KB, [512], F32); bbt = Vw(T0 + 22 * KB, [512], F32)
            bcast_load(bgt, b_norm_g[l:l + 1, :]); bcast_load(bbt, b_norm_b[l:l + 1, :])
            cbT = Vw(T0 + 19 * KB, [4], F32)
            S.dma(cbT, b_conv_b[l].rearrange("(c p) -> p c", p=128), allow_slow_non_contiguous=True)
            pa = ps(); pg = ps()
            for k in range(8):
                S.mm(pa[P32, :], xnTs[:, k, :], WB[:, k, 0:512], start=(k == 0), stop=(k == 7))
            for k in range(8):
                S.mm(pg[P32, :], xnTs[:, k, :], WB[:, k, 512:1024], start=(k == 0), stop=(k == 7))
            sig = Vw(T0 + 24 * KB, [512], F32); glut = Vw(T0 + 26 * KB, [512], F32)
            S.act(sig[P32], pg[P32, :], ACTF.Sigmoid)
            S.tt(glut[P32], pa[P32, :], sig[P32], ALU.mult)
            for b in range(4):
                S.dma(nbs_o[l][b, 22:30, :], glut[b * 8:(b + 1) * 8, :])
            S.dma(nbs_o[l][:, 0:22, :], st_d[l][:, 8:30, :])
            padT = Vw(T0 + 28 * KB, [4, 4, 38])
            gl_f = Vw(T0 + 30 * KB, [4, 32], F32)
            sgf = Vw(T0 + 31 * KB, [32], F32)
            for c in range(4):
                pa = ps(); pg = ps()
                for k in range(8):
                    S.mm(pa[:, 0:32], WB[:, k, c * 128:(c + 1) * 128], xnTs[:, k, :], start=(k == 0), stop=(k == 7))
                for k in range(8):
                    S.mm(pg[:, 0:32], WB[:, k, 512 + c * 128:512 + (c + 1) * 128], xnTs[:, k, :], start=(k == 0), stop=(k == 7))
                S.act(sgf, pg[:, 0:32], ACTF.Sigmoid)
                S.tt(padT[:, c, :, 30:38], pa[:, 0:32].rearrange("p (b t) -> p b t", b=4),
                     sgf.rearrange("p (b t) -> p b t", b=4), ALU.mult)
            stf = Vw(T0 + 32 * KB, [4, 512], F32)
            stb = Vw(T0 + 40 * KB, [4, 512])
            S.dma(stf[0:30], st_d[l].rearrange("b j c -> j b c"))
            S.copy(stb[0:30], stf[0:30], eng="vector")
            pt = ps()[:].bitcast(BF16)
            for b in range(4):
                for c in range(4):
                    S.tr(pt[:, (b * 4 + c) * 32:(b * 4 + c) * 32 + 30], stb[0:30, b, c * 128:(c + 1) * 128], identb[0:30, 0:30])
            S.copy(padT[:, :, :, 0:30], pt[:, 0:512].rearrange("p (b c j) -> p c b j", b=4, c=4)[:, :, :, 0:30], eng="vector")
            pc = ps()
            for c in range(4):
                for j in range(31):
                    S.mm(pc[:, c * 32:(c + 1) * 32], diag[:, c * 31 + j, :], padT[:, c, :, j:j + 8],
                         start=(j == 0), stop=(j == 30))
            ycT = Vw(T0 + 44 * KB, [4, 32])
            for c in range(4):
                S.copy(sgf, pc[:, c * 32:(c + 1) * 32], eng="scalar")
                S.ts(ycT[:, c, :], sgf, cbT[:, c:c + 1], 1.0, ALU.add, ALU.mult)
            pt = ps()[:].bitcast(BF16)
            for c in range(4):
                S.tr(pt[P32, c * 128:(c + 1) * 128], ycT[:, c, :], identb[:])
            yc = Vw(T0 + 24 * KB, [512], F32)
            S.copy(yc[P32], pt[P32, 0:512], eng="vector")
            lnout = Vw(T0 + 26 * KB, [512], F32)
            ln32(yc, bgt, bbt, lnout, T0 + 34 * KB)
            obb = Vw(T0 + 45 * KB, [512])
            S.act(obb[P32], lnout[P32], ACTF.Silu)
            featT32(obb, oTs[1])

            S.barrier()
            QTs = Vw(T0 + 18 * KB, [4, 2, 32])
            KTs = Vw(T0 + 19 * KB, [4, 32])
            accS = Vw(T0 + 20 * KB, [4, 2, 32], F32)
            def cslot(q):
                o = T0 + 22 * KB + q * 12 * KB
                return (Vw(o, [512]), Vw(o + 1 * KB, [512]), Vw(o + 2 * KB, [4, 128]), Vw(o + 3 * KB, [4, 2, 128]),
                        Vw(o + 5 * KB, [4, 2, 128]), Vw(o + 7 * KB, [16]), Vw(o + 7 * KB + 64, [16]))
            cslots = [cslot(0), cslot(1)]
            kvst = Vw(T0 + 30 * KB, [512], F32)
            for q in range(2):
                S.memset(cslots[q][3].rearrange("p a b c -> p (a b c)"), 0.0, eng="gpsimd")
                S.memset(cslots[q][4].rearrange("p a b c -> p (a b c)"), 0.0, eng="gpsimd")
            cctr = [0]
            S.memset(QTs.rearrange("p a b c -> p (a b c)"), 0.0, eng="gpsimd")
            caches = ((c0k, c0v), (c1k, c1v), (c2k, c2v))
            for g in range(3):
                d = DIL[g]
                WS = W0 + 64 * KB + (g % 2) * 24 * KB
                WQ = Vw(WS, [8, 512]); WK = Vw(WS + 8 * KB, [8, 512]); WV = Vw(WS + 16 * KB, [8, 512])
                wload(WQ, w_in[l][:, OFF_CQ + g * 512:OFF_CQ + (g + 1) * 512], 8)
                wload(WK, w_in[l][:, OFF_CK + g * 512:OFF_CK + (g + 1) * 512], 8)
                wload(WV, w_in[l][:, OFF_CV + g * 512:OFF_CV + (g + 1) * 512], 8)
                for W_, dst_o in ((WK, ks_o), (WV, vs_o)):
                    pk = ps()
                    for k in range(8):
                        S.mm(pk[P32, :], xnTs[:, k, :], W_[:, k, :], start=(k == 0), stop=(k == 7))
                    S.copy(kvst[P32], pk[P32, :], eng="scalar")
                    S.dma(dst_o[l, g], kvst[P32])
                for c in range(4):
                    pq = ps()
                    for k in range(8):
                        S.mm(pq[:, 0:32], WQ[:, k, c * 128:(c + 1) * 128], xnTs[:, k, :], start=(k == 0), stop=(k == 7))
                    for k in range(8):
                        S.mm(pq[:, 32:64], WK[:, k, c * 128:(c + 1) * 128], xnTs[:, k, :], start=(k == 0), stop=(k == 7))
                    S.copy(QTs[0:64, c, 0, :], pq[0:64, 0:32], eng="vector")
                    S.copy(QTs[64:128, c, 1, :], pq[64:128, 0:32], eng="vector")
                    S.copy(KTs[:, c, :], pq[:, 32:64], eng="vector")
                def sblock(b, rho, slot_, g=g, d=d, WV=WV):
                    toks = list(range(rho, 8, d))
                    nq = len(toks)
                    tsl = slice(b * 8 + rho, b * 8 + rho + (nq - 1) * d + 1, d)
                    ck, cv = caches[g]
                    Kc, Vc, KcT, Vc2, Vn2, P0, P1 = cslots[slot_]
                    S.dma(Kc, ck[l, b][rho:rho + 127 * d + 1:d, :], eng="gpsimd")
                    S.dma(Vc, cv[l, b][rho:rho + 127 * d + 1:d, :], eng="gpsimd")
                    yield
                    pt = ps()[:].bitcast(BF16)
                    for c in range(4):
                        S.tr(pt[:, c * 128:(c + 1) * 128], Kc[:, c * 128:(c + 1) * 128], identb[:])
                    pvn = ps()
                    for k in range(8):
                        S.mm(pvn[0:nq, :], xnTs[:, k, tsl], WV[:, k, :], start=(k == 0), stop=(k == 7))
                    yield
                    S.copy(KcT, pt[:, 0:512].rearrange("p (c k) -> p c k", c=4), eng="vector")
                    Vc3 = Vc.rearrange("p (c e) -> p c e", c=4)
                    S.copy(Vc2[:, :, 0, 0:64], Vc3[:, :, 0:64], eng="vector")
                    S.copy(Vc2[:, :, 1, 64:128], Vc3[:, :, 64:128], eng="gpsimd")
                    yield
                    pv3 = pvn[0:nq, :].rearrange("p (c e) -> p c e", c=4)
                    S.copy(Vn2[0:nq, :, 0, 0:64], pv3[:, :, 0:64], eng="vector")
                    S.copy(Vn2[0:nq, :, 1, 64:128], pv3[:, :, 64:128], eng="vector")
                    yield
                    for c in range(4):
                        E = etab[:, g * 4 + c, :]
                        pss = ps()
                        for hh in range(2):
                            S.mm(pss[:, hh * 8:hh * 8 + nq], KcT[:, c, :], QTs[:, c, hh, tsl])
                        for hh in range(2):
                            S.mm(pss[0:nq, 16 + hh * 8:16 + hh * 8 + nq], KTs[:, c, tsl], QTs[:, c, hh, tsl])
                        yield
                        S.act(P0, pss[:, 0:16], ACTF.Exp, scale=SCALE)
                        S.act(P1[0:nq], pss[0:nq, 16:32], ACTF.Exp, scale=SCALE)
                        yield
                        for hh in range(2):
                            S.tt(P0[:, hh * 8:hh * 8 + nq], P0[:, hh * 8:hh * 8 + nq], E[:, hh * 256:hh * 256 + nq], ALU.mult)
                            S.tt(P1[0:nq, hh * 8:hh * 8 + nq], P1[0:nq, hh * 8:hh * 8 + nq],
                                 E[0:nq, hh * 256 + 128:hh * 256 + 128 + nq], ALU.mult)
                        yield
                        pso = ps()
                        for which, col in ((0, 0), (1, 8)):
                            n_ = 0
                            for hh in range(2):
                                lh0 = Vc2[:, c, hh, :] if which == 0 else ones2[:, hh, :]
                                lh1 = Vn2[0:nq, c, hh, :] if which == 0 else ones2[0:nq, hh, :]
                                S.mm(pso[:, col:col + nq], lh0, P0[:, hh * 8:hh * 8 + nq], start=(n_ == 0), stop=False)
                                n_ += 1
                                S.mm(pso[:, col:col + nq], lh1, P1[0:nq, hh * 8:hh * 8 + nq], start=False, stop=(hh == 1))
                        yield
                        dst = accS[:, c, :, tsl]
                        src = pso[:, 0:16].rearrange("p (a q) -> p a q", a=2)[:, :, 0:nq]
                        if g == 0:
                            S.copy(dst, src, eng="vector")
                        else:
                            S.tt(dst, dst, src, ALU.add)
                        yield
                sbl = [(b, rho) for b in range(4) for rho in range(min(d, 8))]
                for i0 in range(0, len(sbl), 2):
                    lockstep([sblock(sbl[i0 + q][0], sbl[i0 + q][1], q) for q in range(2) if i0 + q < len(sbl)])
            S.recip(accS[:, :, 1, :], accS[:, :, 1, :])
            S.tt(oTs[2], accS[:, :, 0, :], accS[:, :, 1, :], ALU.mult)

            S.barrier()
            Wo = Vw(W0 + 72 * KB, [8, 1024])
            macc = Vw(T0 + 18 * KB, [32], F32); sgm = Vw(T0 + 18 * KB + 128, [32], F32); mtmp = Vw(T0 + 18 * KB + 256, [32], F32)
            wload(Wo, w_out[l], 8)
            bcast_load(gt_a, g_post_mix[l:l + 1, :])
            bcast_load(gt_b, g_pre_ffn[l:l + 1, :])
            def sM(dc, slot_):
                Wg = Vw(W0 + dc * 6 * KB, [3, 8, 128]); Wb = Vw(W0 + 48 * KB + dc * 3 * KB, [3, 4, 128])
                macc = Vw(T0 + 18 * KB + slot_ * 512, [32], F32); sgm = Vw(T0 + 18 * KB + slot_ * 512 + 128, [32], F32)
                mtmp = Vw(T0 + 18 * KB + slot_ * 512 + 256, [32], F32)
                for n in range(3):
                    wload(Wg[:, n, :, :], w_in[l][:, OFF_G + n * 1024 + dc * 128: OFF_G + n * 1024 + (dc + 1) * 128], 8)
                    wload(Wb[:, n, :, :], w_branch[l][n * 512:(n + 1) * 512, dc * 128:(dc + 1) * 128], 4)
                for n in range(3):
                    pg = ps(); pp = ps()
                    for k in range(8):
                        S.mm(pg[:, 0:32], Wg[:, n, k, :], xnTs[:, k, :], start=(k == 0), stop=(k == 7))
                    for k in range(4):
                        S.mm(pp[:, 0:32], Wb[:, n, k, :], oTs[n][:, k, :], start=(k == 0), stop=(k == 3))
                    yield
                    S.act(sgm, pg[:, 0:32], ACTF.Sigmoid)
                    yield
                    if n == 0:
                        S.tt(macc, pp[:, 0:32], sgm, ALU.mult)
                    else:
                        S.tt(mtmp, pp[:, 0:32], sgm, ALU.mult)
                        yield
                        if n == 1:
                            S.tt(macc, macc, mtmp, ALU.add)
                        else:
                            S.tt(mergedTs[:, dc, :], macc, mtmp, ALU.add)
                    yield
            for dc0 in range(0, 8, 2):
                lockstep([sM(dc0, 0), sM(dc0 + 1, 1)])

            def post32(py, gp):
                S.copy(ysb[P32, 0:512], py[0][P32, :], eng="scalar")
                S.copy(ysb[P32, 512:1024], py[1][P32, :], eng="vector")
                ss = sumsq(ysb[P32], junk[P32], P=P32)
                r = rstd_from_ss(ss, D, P=P32)
                S.stt(ysb[P32], ysb[P32], r, gp[P32], ALU.mult, ALU.mult)
                S.tt(xs_res[:], xs_res[:], ysb[P32], ALU.add)

            py = [ps(), ps()]
            for cb in range(2):
                for k in range(8):
                    S.mm(py[cb][P32, :], mergedTs[:, k, :], Wo[:, k, cb * 512:(cb + 1) * 512], start=(k == 0), stop=(k == 7))
            post32(py, gt_a)
            norm32(xs_res, gt_b)

            S.barrier()
            W2 = Vw(W0, [22, 1024])
            wload(W2, ffn_w_out[l], 22)
            W1 = [Vw(W0 + 44 * KB + q * 4 * KB, [2, 8, 128]) for q in range(22)]
            bcast_load(gt_a, g_post_ffn[l:l + 1, :])
            sg = Vw(T0 + 18 * KB, [32], F32)
            def sF(fc, slot_):
                Wc = W1[fc]
                sg_ = Vw(T0 + 18 * KB + slot_ * 128, [32], F32)
                wload(Wc[:, 0, :, :], ffn_w_in[l][:, fc * 128:(fc + 1) * 128], 8)
                wload(Wc[:, 1, :, :], ffn_w_in[l][:, DFF + fc * 128:DFF + (fc + 1) * 128], 8)
                pg = ps(); pu = ps()
                for k in range(8):
                    S.mm(pg[:, 0:32], Wc[:, 0, k, :], xnTs[:, k, :], start=(k == 0), stop=(k == 7))
                for k in range(8):
                    S.mm(pu[:, 0:32], Wc[:, 1, k, :], xnTs[:, k, :], start=(k == 0), stop=(k == 7))
                yield
                S.act(sg_, pg[:, 0:32], ACTF.Silu)
                yield
                S.tt(actTs[:, fc, :], pu[:, 0:32], sg_, ALU.mult)
                yield
            for f0 in range(0, 22, 4):
                lockstep([sF(f0 + q, q) for q in range(4) if f0 + q < 22])
            py = [ps(), ps()]
            for cb in range(2):
                for k in range(22):
                    S.mm(py[cb][P32, :], actTs[:, k, :], W2[:, k, cb * 512:(cb + 1) * 512], start=(k == 0), stop=(k == 21))
            post32(py, gt_a)
            if l == 1:
                S.dma(ys_o, xs_res[:])


        try:
            if not skip_sample:
                run_sample(0)
                ckpt("S0")
                run_sample(1)
                ckpt("S1")
            if sample_only:
                raise _Stop()
            run_pass("A", 0, xw[2048:4096, :], xw[0:2048, :], x1s[0:2048, :], 0, None)
            ckpt("A:F")
            run_pass("B", 0, xw[4096:6144, :], xw[2048:4096, :], x1s[2048:4096, :], 1, 0)
            ckpt("B:F")
            run_pass("C", 1, x1s[2048:4096, :], x1s[0:2048, :], y_o, 2, 1)
        except _Stop:
            pass
        if dbg:
            S.barrier()
            S.dma(dbg_o, arena[:])
        S.emit(st)
    return nc


def _etab():
    e = np.zeros((12, 128, 512), np.float32)
    kk = np.arange(128)[:, None].astype(np.float64)
    qq = np.arange(128)[None, :].astype(np.float64)
    for g in range(3):
        for c in range(4):
            for hh in range(2):
                j = 2 * c + hh
                slope = 2.0 ** (-8.0 * (j * 3 + g + 1.0) / 24.0)
                for half in range(2):
                    step = qq + 128 - kk if half == 0 else qq - kk
                    val = np.exp(-slope * DIL[g] * step)
                    val = np.where((step >= 0) & (step <= 128), val, 0.0)
                    e[g * 4 + c, :, hh * 256 + half * 128: hh * 256 + (half + 1) * 128] = val
    return e


_NC_CACHE = {}


def kernel(**inp):
    f = lambda k: np.ascontiguousarray(np.asarray(inp[k], dtype=np.float32))
    xp = f("x_prompt")
    if "nc" not in _NC_CACHE:
        _NC_CACHE["nc"] = build_nc()
    nc = _NC_CACHE["nc"]
    wnames = ["norm_pre_mix", "norm_post_mix", "norm_pre_ffn", "norm_post_ffn", "w_in", "a_norm_g", "a_norm_b",
              "a_w_s", "a_b_s", "b_conv_w", "b_conv_b", "b_norm_g", "b_norm_b", "w_branch", "w_out", "ffn_w_in",
              "ffn_w_out"]
    shared = {k: f(k) for k in wnames}
    shared["etab"] = _etab()
    shared["ident"] = np.eye(128, dtype=np.float32)
    shared["tril"] = np.tril(np.ones((128, 128), np.float32))
    o2 = np.zeros((128, 2, 128), np.float32)
    o2[:, 0, 0:64] = 1.0
    o2[:, 1, 64:128] = 1.0
    shared["ones2"] = o2.reshape(128, 256)
    bdm = np.zeros((32, 32), np.float32)
    for b_ in range(4):
        for s_ in range(8):
            for t_ in range(s_, 8):
                bdm[b_ * 8 + s_, b_ * 8 + t_] = 1.0
    shared["bdm"] = bdm
    xs_all = f("x_sample"); st_all = f("state_b_conv")
    cch = [f(k) for k in ("cache_c0_k", "cache_c0_v", "cache_c1_k", "cache_c1_v", "cache_c2_k", "cache_c2_v")]
    in_maps = []
    for c in range(8):
        b, seg = c // 4, (c % 4) * 2048
        xw = np.zeros((6144, 1024), np.float32)
        lo = seg - 4096
        s0 = max(lo, 0)
        xw[s0 - lo:] = xp[b, s0:seg + 2048]
        fl = np.zeros((128, 4), np.float32)
        fl[:, 0] = 1.0 if seg >= 4096 else 0.0
        fl[:, 1] = 1.0 if seg >= 2048 else 0.0
        fl[:, 2] = 1.0 if seg >= 2048 else 0.0
        m = dict(shared)
        m["xw"] = xw
        m["flags"] = fl
        bs = slice(c * 4, (c + 1) * 4)
        m["xs"] = np.ascontiguousarray(xs_all[bs].reshape(32, 1024))
        m["st"] = np.ascontiguousarray(st_all[:, bs])
        for nm, arr in zip(("c0k", "c0v", "c1k", "c1v", "c2k", "c2v"), cch):
            m[nm] = np.ascontiguousarray(arr[:, bs].reshape(2, 4, arr.shape[2], 512))
        in_maps.append(m)
    res = run_bass_kernel_spmd(nc, in_maps, core_ids=list(range(8)))
    R = res.results
    y_prompt = np.stack([np.concatenate([R[b * 4 + i]["y_o"] for i in range(4)], axis=0) for b in range(2)], 0)
    gt = np.stack([R[b * 4 + 3]["gt_o"] for b in range(2)], 1)
    new_b_conv_prompt = np.ascontiguousarray(np.transpose(gt, (0, 1, 3, 2))[:, :, 2:, :])
    ko = np.stack([R[b * 4 + 3]["k_o"] for b in range(2)], 2)
    vo = np.stack([R[b * 4 + 3]["v_o"] for b in range(2)], 2)
    kvp = []
    for g, wlen in enumerate((128, 512, 2048)):
        kvp.append(np.ascontiguousarray(ko[:, g, :, NT - wlen:, :]).reshape(2, 2, wlen, 8, 64))
        kvp.append(np.ascontiguousarray(vo[:, g, :, NT - wlen:, :]).reshape(2, 2, wlen, 8, 64))
    y_sample = np.concatenate([R[c]["ys_o"].reshape(4, 8, 1024) for c in range(8)], 0)
    new_b_conv_sample = np.concatenate([R[c]["nbs_o"] for c in range(8)], 1)
    new_a_v_sample = np.concatenate([R[c]["nav_o"].reshape(2, 4, 8, 512) for c in range(8)], 1)
    kvs = []
    for g in range(3):
        kvs.append(np.concatenate([R[c]["ks_o"][:, g].reshape(2, 4, 8, 8, 64) for c in range(8)], 1))
        kvs.append(np.concatenate([R[c]["vs_o"][:, g].reshape(2, 4, 8, 8, 64) for c in range(8)], 1))
    return (y_prompt, y_sample, new_b_conv_prompt, new_b_conv_sample, new_a_v_sample, *kvp, *kvs)
```

```python
import numpy as np
from concourse.bass_utils import run_bass_kernel_spmd
import concourse.bass as bass
import concourse.mybir as mybir

F32 = mybir.dt.float32
BF16 = mybir.dt.bfloat16
ALU = mybir.AluOpType
ACTF = mybir.ActivationFunctionType
AX = mybir.AxisListType

_DSZ = {F32: 4, BF16: 2, mybir.dt.int32: 4, mybir.dt.float32r: 4}


def _region(ap):
    t = ap.tensor
    name = t.name
    dsz = _DSZ.get(ap.dtype, 4)
    dims = list(ap.ap)
    off = int(ap.offset)
    space = str(ap.space)
    if space in ("SB", "PSUM"):
        pstep, pcnt = dims[0]
        if pstep == 0:
            pstep = 1 << 40
        p0 = off // pstep if pstep < (1 << 40) else 0
        f0 = off - p0 * pstep if pstep < (1 << 40) else off
        p1 = p0 + pcnt
        rest = dims[1:]
    else:
        p0, p1 = 0, 1
        f0 = off
        rest = dims
    lo = f0
    hi = f0
    for st, cn in rest:
        if cn <= 0:
            continue
        d = st * (cn - 1)
        if d < 0:
            lo += d
        else:
            hi += d
    return name, p0, p1, lo * dsz, (hi + 1) * dsz


class Sched:
    ENGS = ("tensor", "vector", "scalar", "gpsimd", "sync")

    def __init__(self, nc, n_dma_sems=24):
        self.nc = nc
        self.ops = []
        self.recs = {}
        self.n_dma_sems = n_dma_sems
        self.dma_count = {e: 0 for e in self.ENGS}
        self.dma_hist = {e: [] for e in self.ENGS}
        self.barrier_deps = {e: set() for e in self.ENGS}
        self.last = {e: None for e in self.ENGS}
        self.all_dmas = []

    def _access(self, ap, opid, is_write, deps):
        if str(ap.space) == "PSUM":
            name = ap.tensor.name
            eng = self.ops[opid]["eng"]
            rec = self.recs.setdefault(name, {})
            for e2, (last_any, last_w) in rec.items():
                if e2 != eng:
                    if last_any is not None and last_any != opid:
                        deps.add(last_any)
                else:
                    if is_write:
                        if last_any is not None and last_any != opid:
                            deps.add(last_any)
                    elif last_w is not None and last_w != opid:
                        deps.add(last_w)
            la, lw = rec.get(eng, (None, None))
            rec[eng] = (opid, opid if is_write else lw)
            return
        name, p0, p1, lo, hi = _region(ap)
        lst = self.recs.setdefault(name, [])
        keep = []
        eng = self.ops[opid]["eng"]
        isdma = self.ops[opid]["dma"]
        for r in lst:
            ov = not (r[1] <= p0 or p1 <= r[0] or r[3] <= lo or hi <= r[2])
            if ov and r[4] != opid:
                if is_write or r[5]:
                    deps.add(r[4])
                if is_write and r[0] >= p0 and r[1] <= p1 and r[2] >= lo and r[3] <= hi:
                    continue
            if (not is_write) and (not r[5]) and (not isdma) and r[4] != opid:
                ro = self.ops[r[4]]
                if ro["eng"] == eng and not ro["dma"] and r[0] == p0 and r[1] == p1 and r[2] == lo and r[3] == hi:
                    continue
            keep.append(r)
        keep.append([p0, p1, lo, hi, opid, is_write])
        self.recs[name] = keep

    def op(self, eng, fn, outs=(), ins=(), dma=False):
        opid = len(self.ops)
        o = {"eng": eng, "fn": fn, "deps": set(), "dma": dma}
        self.ops.append(o)
        deps = o["deps"]
        for a in ins:
            self._access(a, opid, False, deps)
        for a in outs:
            self._access(a, opid, True, deps)
        if self.barrier_deps[eng]:
            deps |= self.barrier_deps[eng]
            self.barrier_deps[eng] = set()
        if dma:
            h = self.dma_hist[eng]
            if len(h) >= self.n_dma_sems:
                deps.add(h[-self.n_dma_sems])
            h.append(opid)
            self.all_dmas.append(opid)
        self.last[eng] = opid
        return opid

    def barrier(self):
        d = set(x for x in self.last.values() if x is not None)
        d |= set(self.all_dmas[-64:])
        for e in self.ENGS:
            self.barrier_deps[e] = set(d)

    def dma(self, out, in_, eng="sync", **kw):
        return self.op(eng, lambda e: e.dma_start(out=out, in_=in_, **kw), [out], [in_], dma=True)

    def mm(self, out, lhsT, rhs, start=True, stop=True, **kw):
        return self.op("tensor", lambda e: e.matmul(out, lhsT, rhs, start=start, stop=stop, **kw),
                       [out], [lhsT, rhs])

    def tr(self, out, in_, ident):
        return self.op("tensor", lambda e: e.transpose(out, in_, ident), [out], [in_, ident])

    def act(self, out, in_, func, bias=None, scale=None, accum_out=None, eng="scalar"):
        kw = {}
        ins = [in_]
        outs = [out]
        if bias is not None:
            kw["bias"] = bias
            if not isinstance(bias, (int, float)):
                ins.append(bias)
        if scale is not None:
            kw["scale"] = scale
            if not isinstance(scale, (int, float)):
                ins.append(scale)
        if accum_out is not None:
            kw["accum_out"] = accum_out
            outs.append(accum_out)
        return self.op(eng, lambda e: e.activation(out, in_, func, **kw), outs, ins)

    def tt(self, out, in0, in1, op, eng="vector"):
        return self.op(eng, lambda e: e.tensor_tensor(out, in0, in1, op), [out], [in0, in1])

    def ts(self, out, in0, s1, s2, op0, op1=None, eng="vector", accum_out=None):
        ins = [in0] + [s for s in (s1, s2) if s is not None and not isinstance(s, (int, float))]
        outs = [out] + ([accum_out] if accum_out is not None else [])
        if op1 is None:
            return self.op(eng, lambda e: e.tensor_scalar(out, in0, s1, s2, op0), outs, ins)
        if accum_out is not None:
            return self.op(eng, lambda e: e.tensor_scalar(out, in0, s1, s2, op0, op1, accum_out), outs, ins)
        return self.op(eng, lambda e: e.tensor_scalar(out, in0, s1, s2, op0, op1), outs, ins)

    def stt(self, out, in0, scalar, in1, op0, op1, eng="vector"):
        ins = [in0, in1] + ([scalar] if not isinstance(scalar, (int, float)) else [])
        return self.op(eng, lambda e: e.scalar_tensor_tensor(out, in0, scalar, in1, op0, op1), [out], ins)

    def copy(self, out, in_, eng="vector"):
        if eng == "scalar":
            return self.op(eng, lambda e: e.copy(out, in_), [out], [in_])
        return self.op(eng, lambda e: e.tensor_copy(out, in_), [out], [in_])

    def memset(self, ap, val, eng="vector"):
        return self.op(eng, lambda e: e.memset(ap, val), [ap], [])

    def reduce(self, out, in_, op, axis=AX.X, eng="vector"):
        return self.op(eng, lambda e: e.tensor_reduce(out, in_, axis, op), [out], [in_])

    def recip(self, out, in_):
        return self.op("vector", lambda e: e.reciprocal(out, in_), [out], [in_])

    def emit(self, stack):
        nc = self.nc
        ops = self.ops
        needed = set()
        for o in ops:
            for d in o["deps"]:
                do = ops[d]
                if o["eng"] == "tensor" and do["eng"] == "tensor" and not do["dma"] and not o["dma"]:
                    continue
                needed.add(d)
        final_dmas = list(self.all_dmas)
        eng_sem = {e: stack.enter_context(nc.semaphore("se_" + e)) for e in self.ENGS}
        dma_sems = {e: [stack.enter_context(nc.semaphore("sd_%s_%d" % (e, i)))
                        for i in range(self.n_dma_sems)]
                    for e in self.ENGS if self.dma_count is not None and any(
                        (o["dma"] and o["eng"] == e) for o in ops)}
        cnt = {e: 0 for e in self.ENGS}
        dcount = {e: 0 for e in self.ENGS}
        dsemcnt = {}
        sig = {}
        per_eng = {e: [] for e in self.ENGS}
        for i, o in enumerate(ops):
            e = o["eng"]
            per_eng[e].append(i)
            if o["dma"]:
                k = dcount[e] % self.n_dma_sems
                dcount[e] += 1
                s = dma_sems[e][k]
                dsemcnt[(e, k)] = dsemcnt.get((e, k), 0) + 16
                sig[i] = (s, dsemcnt[(e, k)])
            elif i in needed:
                cnt[e] += 1
                sig[i] = (eng_sem[e], cnt[e])
        self.sig = sig

        plan = {e: [] for e in self.ENGS}
        for ename in self.ENGS:
            seen = {}
            for i in per_eng[ename]:
                o = ops[i]
                waits = []
                for d in sorted(o["deps"]):
                    do = ops[d]
                    if (not do["dma"]) and do["eng"] == ename and (ename == "tensor"):
                        continue
                    s_, v = sig[d]
                    if seen.get(s_.name, 0) >= v:
                        continue
                    seen[s_.name] = v
                    waits.append((s_.name, v))
                plan[ename].append((i, waits, (sig[i][0].name, 16 if o["dma"] else 1) if i in sig else None))
        semv = {}
        pc = {e: 0 for e in self.ENGS}
        progress = True
        while progress:
            progress = False
            for e in self.ENGS:
                while pc[e] < len(plan[e]):
                    i, waits, sg = plan[e][pc[e]]
                    if all(semv.get(n, 0) >= v for n, v in waits):
                        if sg is not None:
                            semv[sg[0]] = semv.get(sg[0], 0) + sg[1]
                        pc[e] += 1
                        progress = True
                    else:
                        break
        stuck = {e: (pc[e], len(plan[e])) for e in self.ENGS if pc[e] < len(plan[e])}
        if stuck:
            for e in stuck:
                i, waits, sg = plan[e][pc[e]]
                print("DEADLOCK", e, "op", i, "waits", [(n, v, semv.get(n, 0)) for n, v in waits])
            raise RuntimeError("scheduler deadlock: %s" % stuck)
        self.max_sem = dict(semv)

        block = stack.enter_context(nc.Block())

        def make(ename):
            def body(eh):
                seen = {}
                for i in per_eng[ename]:
                    o = ops[i]
                    for d in sorted(o["deps"]):
                        do = ops[d]
                        if (not do["dma"]) and do["eng"] == ename and (ename == "tensor"):
                            continue
                        s, v = sig[d]
                        if seen.get(s.name, 0) >= v:
                            continue
                        seen[s.name] = v
                        eh.wait_ge(s, v)
                    ins = o["fn"](eh)
                    if i in sig:
                        ins.then_inc(sig[i][0], 16 if o["dma"] else 1)
                if ename == "sync":
                    for d in final_dmas:
                        s, v = sig[d]
                        if seen.get(s.name, 0) >= v:
                            continue
                        seen[s.name] = v
                        eh.wait_ge(s, v)
                    for e2 in self.ENGS:
                        if e2 != "sync" and cnt[e2] > 0:
                            eh.wait_ge(eng_sem[e2], cnt[e2])
            return body

        for ename in self.ENGS:
            if per_eng[ename] or ename == "sync":
                getattr(block, ename)(make(ename))

import numpy as np
from contextlib import ExitStack

D = 1024
NT = 2048
DIN = 9728
OFF_AU, OFF_AV, OFF_B, OFF_CQ, OFF_CK, OFF_CV, OFF_G = 0, 512, 1024, 2048, 3584, 5120, 6656
DFF = 2816
DIL = (1, 4, 16)
EPS = 1e-6
SCALE = 0.125
KB = 1024
XF, XH, OT, PL = 0, 32 * KB, 64 * KB, 112 * KB
ARENA = 188 * KB


class _Stop(Exception):
    pass


def build_nc(stop=None, dbg=False, step=None, skip_sample=False, sample_only=False):
    nc = bass.Bass("TRN2", target_bir_lowering=False)
    din = lambda n, s: nc.dram_tensor(n, list(s), F32, kind="ExternalInput").ap()
    dout = lambda n, s: nc.dram_tensor(n, list(s), F32, kind="ExternalOutput").ap()
    xw = din("xw", [6144, D])
    flags_d = din("flags", [128, 4])
    etab_d = din("etab", [12, 128, 512])
    ident_d = din("ident", [128, 128])
    tril_d = din("tril", [128, 128])
    ones2_d = din("ones2", [128, 256])
    g_pre_mix = din("norm_pre_mix", [2, D]); g_post_mix = din("norm_post_mix", [2, D])
    g_pre_ffn = din("norm_pre_ffn", [2, D]); g_post_ffn = din("norm_post_ffn", [2, D])
    w_in = din("w_in", [2, D, DIN])
    a_norm_g = din("a_norm_g", [2, 512]); a_norm_b = din("a_norm_b", [2, 512])
    a_w_s = din("a_w_s", [2, 4, 128, 128]); a_b_s = din("a_b_s", [2, 4, 128])
    b_conv_w = din("b_conv_w", [2, 31, 512]); b_conv_b = din("b_conv_b", [2, 512])
    b_norm_g = din("b_norm_g", [2, 512]); b_norm_b = din("b_norm_b", [2, 512])
    w_branch = din("w_branch", [2, 1536, D]); w_out = din("w_out", [2, D, D])
    ffn_w_in = din("ffn_w_in", [2, D, 2 * DFF]); ffn_w_out = din("ffn_w_out", [2, DFF, D])

    xs_d = din("xs", [32, D])
    st_d = din("st", [2, 4, 30, 512])
    bdm_d = din("bdm", [32, 32])
    c0k = din("c0k", [2, 4, 128, 512]); c0v = din("c0v", [2, 4, 128, 512])
    c1k = din("c1k", [2, 4, 512, 512]); c1v = din("c1v", [2, 4, 512, 512])
    c2k = din("c2k", [2, 4, 2048, 512]); c2v = din("c2v", [2, 4, 2048, 512])
    ys_o = dout("ys_o", [32, D])
    nbs_o = dout("nbs_o", [2, 4, 30, 512])
    nav_o = dout("nav_o", [2, 32, 512])
    ks_o = dout("ks_o", [2, 3, 32, 512])
    vs_o = dout("vs_o", [2, 3, 32, 512])
    if dbg:
        x1s = nc.dram_tensor("x1s", [4096, D], F32, kind="ExternalOutput").ap()
        xmid = nc.dram_tensor("xmid", [NT, D], F32, kind="ExternalOutput").ap()
    else:
        x1s = nc.dram_tensor("x1s", [4096, D], F32).ap()
        xmid = nc.dram_tensor("xmid", [NT, D], F32).ap()

    y_o = dout("y_o", [NT, D])
    k_o = dout("k_o", [2, 3, NT, 512])
    v_o = dout("v_o", [2, 3, NT, 512])
    gt_o = dout("gt_o", [2, 512, 32])
    dbg_o = nc.dram_tensor("dbg_o", [128, ARENA // 2], BF16, kind="ExternalOutput").ap() if dbg else None

    with ExitStack() as st:
        sbt = lambda n, s, d: st.enter_context(nc.sbuf_tensor(n, list(s), d))
        arena = sbt("arena", [128, ARENA // 2], BF16)
        identb = sbt("identb", [128, 128], BF16)
        identf = sbt("identf", [128, 128], F32)
        trilf = sbt("trilf", [128, 128], F32)
        ones2 = sbt("ones2s", [128, 2, 128], BF16)
        etab = sbt("etabs", [128, 12, 512], BF16)
        flags = sbt("flagss", [128, 4], F32)
        stat = sbt("stat", [128, 256], F32)
        bs_sb = sbt("bs_sb", [128, 4], F32)
        bs32 = sbt("bs32", [32, 4], F32)
        bdm = sbt("bdm_s", [32, 32], F32)
        xs_res = sbt("xs_res", [32, D], F32)
        psb = [st.enter_context(nc.psum_tensor("psb%d" % i, [128, 512], F32)) for i in range(8)]
        S = Sched(nc)
        state = {"ps": 0, "st": 0, "alt": 0}

        def ps():
            state["ps"] = (state["ps"] + 1) % 8
            return psb[state["ps"]]

        def stc(n=1):
            i = state["st"]
            if i + n > 256:
                i = 0
            state["st"] = i + n
            return stat[:, i:i + n]

        def alt(a="vector", b="scalar"):
            state["alt"] ^= 1
            return a if state["alt"] else b

        def Vw(off, shape, dt=BF16):
            n = 1
            for s_ in shape:
                n *= s_
            dsz = 2 if dt == BF16 else 4
            v = arena[:, off // 2: off // 2 + n * dsz // 2]
            if dt != BF16:
                v = v.bitcast(dt)
            if len(shape) == 2:
                v = v.rearrange("p (a b) -> p a b", a=shape[0])
            elif len(shape) == 3:
                v = v.rearrange("p (a b c) -> p a b c", a=shape[0], b=shape[1])
            return v

        def dstep(n):
            if step == n and stop is not None and state.get("pass") == stop.split(":")[0]:
                raise _Stop()

        def ckpt(name):
            if stop == name:
                raise _Stop()

        def bcast_load(dst, row):
            S.dma(dst, row.partition_broadcast(128))

        def evac(out, in_, eng=None):
            eng = eng or alt()
            S.copy(out, in_, eng=eng)

        S.dma(identf[:], ident_d)
        S.dma(trilf[:], tril_d)
        S.dma(flags[:], flags_d)
        S.dma(ones2[:].rearrange("p a b -> p (a b)"), ones2_d, eng="gpsimd")
        S.dma(etab[:], etab_d.rearrange("n p c -> p n c"), eng="gpsimd")
        S.copy(identb[:], identf[:])
        S.dma(bdm[:], bdm_d)

        if step == 777:
            S.dma(v_o[0, 0][0:128, 0:128], identf[:])
        try:
            ckpt("const")
        except _Stop:
            S.barrier()
            S.dma(dbg_o, arena[:])
            S.emit(st)
            return nc

        def lockstep(gens):
            gens = list(gens)
            while gens:
                nxt = []
                for g_ in gens:
                    try:
                        next(g_)
                        nxt.append(g_)
                    except StopIteration:
                        pass
                gens = nxt

        def run1(gen):
            for _ in gen:
                pass

        def g_rstd(ss, n, eps=EPS, P=slice(0, 128)):
            m = stc()[P, :]
            S.ts(m, ss, 1.0 / n, eps, ALU.mult, ALU.add)
            yield
            S.act(m, m, ACTF.Sqrt)
            yield
            r = stc()[P, :]
            S.recip(r, m)
            yield
            return r

        def g_sumsq(src, junk, P=slice(0, 128)):
            ss = stc()[P, :]
            S.memset(ss, 0.0)
            yield
            S.act(junk, src, ACTF.Square, accum_out=ss)
            yield
            return ss

        def rstd_from_ss(ss, n, eps=EPS, P=slice(0, 128)):
            g_ = g_rstd(ss, n, eps, P)
            try:
                while True:
                    next(g_)
            except StopIteration as e_:
                return e_.value

        def sumsq(src, junk, P=slice(0, 128)):
            g_ = g_sumsq(src, junk, P)
            try:
                while True:
                    next(g_)
            except StopIteration as e_:
                return e_.value

        def g_norm_to_T(xt, gtile, dstT, tile, junk, xnb):
            ss = yield from g_sumsq(xt, junk)
            r = yield from g_rstd(ss, D)
            S.stt(xnb, xt, r, gtile, ALU.mult, ALU.mult)
            yield
            pt = ps()[:].bitcast(BF16)
            for k in range(8):
                S.tr(pt[:, k * 128:(k + 1) * 128], xnb[:, k * 128:(k + 1) * 128], identb[:])
            yield
            evac(dstT[:, :, tile * 128:(tile + 1) * 128], pt[:, 0:1024].rearrange("p (k c) -> p k c", k=8))
            yield

        def phase_norm(src, grow, dstT, ntiles, tile0=0):
            gtile = Vw(PL + 28 * KB, [1024], F32)
            bcast_load(gtile, grow)
            junk_ = Vw(PL + 24 * KB, [1024])

            def body(i, sl_):
                xt = Vw(PL + sl_ * 4 * KB, [1024], F32)
                S.dma(xt, src[i * 128:(i + 1) * 128, :])
                yield
                yield from g_norm_to_T(xt, gtile, dstT, tile0 + i, junk_, Vw(PL + 16 * KB + sl_ * 2 * KB, [1024]))
            for i0 in range(0, ntiles, 4):
                lockstep([body(i0 + q, q) for q in range(4)])

        def g_layernorm512(src, gt, bt, out, toff):
            junk = Vw(toff, [512], F32)
            tmp = Vw(toff + 2 * KB, [512], F32)
            sm = stc()
            S.reduce(sm, src, ALU.add)
            yield
            sq = yield from g_sumsq(src, junk)
            mean = stc()
            S.ts(mean, sm, 1.0 / 512, 0.0, ALU.mult, ALU.add)
            yield
            msq = stc()
            S.tt(msq, mean, mean, ALU.mult)
            yield
            var = stc()
            S.stt(var, sq, 1.0 / 512, msq, ALU.mult, ALU.subtract)
            yield
            S.ts(var, var, 1.0, EPS, ALU.mult, ALU.add)
            yield
            S.act(var, var, ACTF.Sqrt)
            yield
            r = stc()
            S.recip(r, var)
            yield
            S.ts(tmp, src, mean, r, ALU.subtract, ALU.mult)
            yield
            S.tt(tmp, tmp, gt, ALU.mult)
            yield
            S.tt(out, tmp, bt, ALU.add)
            yield

        def to_featT(src_bf, dstT, tile):
            pt = ps()[:].bitcast(BF16)
            for c in range(4):
                S.tr(pt[:, c * 128:(c + 1) * 128], src_bf[:, c * 128:(c + 1) * 128], identb[:])
            evac(dstT[:, :, tile * 128:(tile + 1) * 128],
                 pt[:, 0:512].rearrange("p (c t) -> p c t", c=4))

        def wload(dst, src2d, kc):
            S.dma(dst, src2d.rearrange("(k p) c -> p k c", p=128), eng="gpsimd")

        def run_pass(pname, l, xsrc, hsrc, xdst, fcol, out_l):
            state["pass"] = pname
            xnT_f = Vw(XF, [8, NT])
            xnT_h = Vw(XH, [8, NT])
            mergedT = xnT_h
            oT = [Vw(OT + n * 16 * KB, [4, NT]) for n in range(3)]
            S.barrier()
            phase_norm(hsrc, g_pre_mix[l:l + 1, :], xnT_h, 16)
            phase_norm(xsrc, g_pre_mix[l:l + 1, :], xnT_f, 16)

            ckpt("%s:N" % pname)
            WA = Vw(PL, [8, 1024])
            wload(WA, w_in[l][:, OFF_AU:OFF_AU + 1024], 8)
            agt = Vw(PL + 16 * KB, [512], F32); abt = Vw(PL + 18 * KB, [512], F32)
            bcast_load(agt, a_norm_g[l:l + 1, :]); bcast_load(abt, a_norm_b[l:l + 1, :])
            WsT = Vw(PL + 20 * KB, [4, 128])
            wtmp = Vw(PL + 21 * KB, [4, 128], F32)
            wtmpb = Vw(PL + 23 * KB, [4, 128])
            S.dma(wtmp, a_w_s[l].rearrange("g t s -> t g s"))
            S.dma(bs_sb[:], a_b_s[l].rearrange("g t -> t g"), allow_slow_non_contiguous=True)
            for g in range(4):
                S.tt(wtmpb[:, g, :], wtmp[:, g, :], trilf[:], ALU.mult)
            pt = ps()[:].bitcast(BF16)
            for g in range(4):
                S.tr(pt[:, g * 128:(g + 1) * 128], wtmpb[:, g, :], identb[:])
            evac(WsT, pt[:, 0:512].rearrange("p (g t) -> p g t", g=4))
            def bodyA(i, sl_):
                TA = PL + 24 * KB + sl_ * 14 * KB
                u_sb = Vw(TA, [512], F32)
                v_sb = Vw(TA + 2 * KB, [512], F32)
                vnb = Vw(TA + 4 * KB, [512])
                oab = Vw(TA + 5 * KB, [512])
                pm_sb = Vw(TA + 10 * KB, [512], F32)
                pu = ps(); pv = ps()
                for k in range(8):
                    S.mm(pu[:], xnT_f[:, k, i * 128:(i + 1) * 128], WA[:, k, 0:512], start=(k == 0), stop=(k == 7))
                for k in range(8):
                    S.mm(pv[:], xnT_f[:, k, i * 128:(i + 1) * 128], WA[:, k, 512:1024], start=(k == 0), stop=(k == 7))
                yield
                S.copy(u_sb, pu[:], eng="scalar")
                S.copy(v_sb, pv[:], eng="vector")
                yield
                yield from g_layernorm512(v_sb, agt, abt, vnb, TA + 6 * KB)
                pm = ps()
                for g in range(4):
                    S.mm(pm[:, g * 128:(g + 1) * 128], WsT[:, g, :], vnb[:, g * 128:(g + 1) * 128])
                yield
                S.copy(pm_sb, pm[:], eng="scalar")
                yield
                for g in range(4):
                    S.stt(oab[:, g * 128:(g + 1) * 128], pm_sb[:, g * 128:(g + 1) * 128], bs_sb[:, g:g + 1],
                          u_sb[:, g * 128:(g + 1) * 128], ALU.add, ALU.mult)
                yield
                pt = ps()[:].bitcast(BF16)
                for c in range(4):
                    S.tr(pt[:, c * 128:(c + 1) * 128], oab[:, c * 128:(c + 1) * 128], identb[:])
                yield
                evac(oT[0][:, :, i * 128:(i + 1) * 128], pt[:, 0:512].rearrange("p (c t) -> p c t", c=4))
                yield
            for i0 in range(0, 16, 2):
                lockstep([bodyA(i0, 0), bodyA(i0 + 1, 1)])

            ckpt("%s:A" % pname)
            S.barrier()
            WB = Vw(PL, [8, 1024])
            wload(WB, w_in[l][:, OFF_B:OFF_B + 1024], 8)
            gluT = Vw(PL + 16 * KB, [4, 2176])
            diag = Vw(PL + 33 * KB, [124, 128])
            TB = OT + 32 * KB
            bgt = Vw(TB, [512], F32); bbt = Vw(TB + 2 * KB, [512], F32); cbt = Vw(TB + 4 * KB, [512], F32)
            bcast_load(bgt, b_norm_g[l:l + 1, :]); bcast_load(bbt, b_norm_b[l:l + 1, :]); bcast_load(cbt, b_conv_b[l:l + 1, :])
            ysb = Vw(TB + 6 * KB, [512], F32)
            sig = Vw(TB + 8 * KB, [512], F32)
            obb = Vw(TB + 10 * KB, [512])
            cw = Vw(TB + 11 * KB, [512], F32)
            cwT = Vw(TB + 13 * KB, [4, 32], F32)
            gt32 = Vw(TB + 13 * KB + 512, [4, 32], F32)
            lnout = Vw(TB + 14 * KB, [512], F32)
            for j in range(31):
                S.dma(cwT[:, :, j], b_conv_w[l, j].rearrange("(c p) -> p c", p=128), allow_slow_non_contiguous=True)
            for c in range(4):
                for j in range(31):
                    S.ts(diag[:, c * 31 + j, :], identb[:], cwT[:, c, j:j + 1], 1.0, ALU.mult, ALU.mult,
                         eng=("vector" if j % 2 == 0 else "gpsimd"))
            def glu_block(rhs_of_k, n, dst_cols, tail=None):
                for c in range(4):
                    pa = ps(); pg = ps()
                    for k in range(8):
                        S.mm(pa[:, 0:n], WB[:, k, c * 128:(c + 1) * 128], rhs_of_k(k), start=(k == 0), stop=(k == 7))
                    for k in range(8):
                        S.mm(pg[:, 0:n], WB[:, k, 512 + c * 128:512 + (c + 1) * 128], rhs_of_k(k), start=(k == 0), stop=(k == 7))
                    S.act(sig[:, 0:n], pg[:, 0:n], ACTF.Sigmoid)
                    S.tt(gluT[:, c, dst_cols:dst_cols + n], pa[:, 0:n], sig[:, 0:n], ALU.mult)
                    if tail is not None:
                        S.tt(gt32[:, c, :], pa[:, n - 32:n], sig[:, n - 32:n], ALU.mult)
            glu_block(lambda k: xnT_h[:, k, NT - 128:NT], 128, 0)
            for c in range(4):
                S.ts(gluT[:, c, 0:128], gluT[:, c, 0:128], flags[:, fcol:fcol + 1], 1.0, ALU.mult, ALU.mult)
            for w in range(4):
                glu_block(lambda k, w=w: xnT_f[:, k, w * 512:(w + 1) * 512], 512, 128 + w * 512,
                          tail=(out_l is not None and w == 3) or None)
            if out_l is not None:
                S.dma(gt_o[out_l].rearrange("(c p) t -> p c t", p=128), gt32)
            def bodyB(i, sl_):
                ysb_ = ysb if sl_ == 0 else Vw(TB + 11 * KB, [512], F32)
                lnout_ = lnout if sl_ == 0 else Vw(PL + 72 * KB, [512], F32)
                obb_ = obb if sl_ == 0 else Vw(PL + 74 * KB, [512])
                pc = ps()
                for c in range(4):
                    for j in range(31):
                        s0 = 128 + i * 128 - 30 + j
                        S.mm(pc[:, c * 128:(c + 1) * 128], gluT[:, c, s0:s0 + 128], diag[:, c * 31 + j, :],
                             start=(j == 0), stop=(j == 30))
                yield
                S.tt(ysb_, pc[:], cbt, ALU.add)
                yield
                yield from g_layernorm512(ysb_, bgt, bbt, lnout_, PL + 64 * KB + sl_ * 4 * KB)
                S.act(obb_, lnout_, ACTF.Silu)
                yield
                pt = ps()[:].bitcast(BF16)
                for c in range(4):
                    S.tr(pt[:, c * 128:(c + 1) * 128], obb_[:, c * 128:(c + 1) * 128], identb[:])
                yield
                evac(oT[1][:, :, i * 128:(i + 1) * 128], pt[:, 0:512].rearrange("p (c t) -> p c t", c=4))
                yield
            for i0 in range(0, 16, 2):
                lockstep([bodyB(i0, 0), bodyB(i0 + 1, 1)])

            ckpt("%s:B" % pname)
            S.barrier()
            WC = Vw(PL, [9, 8, 128])
            QT = Vw(PL + 18 * KB, [2, NT])
            KTb = Vw(PL + 26 * KB, [1, 4096])[:, 0, :]
            Vt = Vw(PL + 34 * KB, [32, 2, 128])
            acc = Vw(PL + 50 * KB, [2, NT], F32)
            Pt2 = [Vw(PL + 66 * KB, [512]), Vw(PL + 67 * KB, [512]), Vw(PL + 73 * KB, [512]), Vw(PL + 74 * KB, [512])]
            kst = Vw(PL + 68 * KB, [512], F32)
            import os
            vst = Vw(PL + (68 if os.environ.get("VST68") else 70) * KB, [512], F32)
            Eh = Vw(PL + 72 * KB, [512])
            S.memset(Vt.rearrange("p a b c -> p (a b c)"), 0.0, eng="gpsimd")
            S.memset(QT[64:128, 0, :], 0.0, eng="gpsimd")
            S.memset(QT[0:64, 1, :], 0.0, eng="gpsimd")
            for c in range(4):
                for g in range(3):
                    for j, off in enumerate((OFF_CQ, OFF_CK, OFF_CV)):
                        wload(WC[:, g * 3 + j, :, :], w_in[l][:, off + g * 512 + c * 128: off + g * 512 + (c + 1) * 128], 8)
                for g in range(3):
                    d = DIL[g]
                    Lh = 128 * d
                    nb = 16 // d
                    Wq, Wk, Wv = WC[:, g * 3 + 0], WC[:, g * 3 + 1], WC[:, g * 3 + 2]
                    E = etab[:, g * 4 + c, :]
                    for hh in range(2):
                        S.ts(Eh[:, hh * 256:hh * 256 + 128], E[:, hh * 256:hh * 256 + 128], flags[:, fcol:fcol + 1], 1.0,
                             ALU.mult, ALU.mult)
                        S.copy(Eh[:, hh * 256 + 128:hh * 256 + 256], E[:, hh * 256 + 128:hh * 256 + 256], eng="gpsimd")
                    for w in range(4):
                        pq = ps(); pk = ps()
                        for k in range(8):
                            S.mm(pq[:], Wq[:, k, :], xnT_f[:, k, w * 512:(w + 1) * 512], start=(k == 0), stop=(k == 7))
                        for k in range(8):
                            S.mm(pk[:], Wk[:, k, :], xnT_f[:, k, w * 512:(w + 1) * 512], start=(k == 0), stop=(k == 7))
                        S.copy(QT[0:64, 0, w * 512:(w + 1) * 512], pq[0:64, :], eng="vector")
                        S.copy(QT[64:128, 1, w * 512:(w + 1) * 512], pq[64:128, :], eng="scalar")
                        evac(KTb[:, Lh + w * 512:Lh + (w + 1) * 512], pk[:])
                    hw = min(512, Lh)
                    for w in range(Lh // hw):
                        pk = ps()
                        c0 = NT - Lh + w * hw
                        for k in range(8):
                            S.mm(pk[:, 0:hw], Wk[:, k, :], xnT_h[:, k, c0:c0 + hw], start=(k == 0), stop=(k == 7))
                        evac(KTb[:, w * hw:(w + 1) * hw], pk[:, 0:hw])
                    htiles = [("h", r, 0) for r in range(d)]
                    ftiles = [("f", r, jb) for r in range(d) for jb in range(nb)]
                    groups = [(t0, htiles[t0:t0 + 4]) for t0 in range(0, d, 4)] + \
                             [(d + t0, ftiles[t0:t0 + 4]) for t0 in range(0, 16, 4)]
                    vo2 = v_o[out_l, g] if out_l is not None else None
                    for (t0, grp) in groups:
                        pv = ps()
                        for q, (kind, r, jb) in enumerate(grp):
                            if kind == "h":
                                srcT = xnT_h; s0 = NT - Lh + r
                            else:
                                srcT = xnT_f; s0 = r + d * 128 * jb
                            for k in range(8):
                                S.mm(pv[:, q * 128:(q + 1) * 128], srcT[:, k, s0:s0 + 127 * d + 1:d],
                                     Wv[:, k, :], start=(k == 0), stop=(k == 7))
                        n = len(grp)
                        pv3 = pv[:, 0:n * 128].rearrange("p (t e) -> p t e", t=n)
                        S.copy(Vt[:, t0:t0 + n, 0, 0:64], pv3[:, :, 0:64], eng="vector")
                        S.copy(Vt[:, t0:t0 + n, 1, 64:128], pv3[:, :, 64:128], eng="scalar")
                        if out_l is not None and grp[0][0] == "f":
                            S.copy(vst, pv[:], eng="scalar")
                            cs = slice(c * 128, (c + 1) * 128)
                            _, r0, jb0 = grp[0]
                            if g == 0:
                                dst = vo2.rearrange("(q p) e -> p q e", p=128)[:, jb0:jb0 + 4, cs]
                            elif g == 1:
                                dst = vo2.rearrange("(q p dd) e -> p q dd e", p=128, dd=4)[:, :, r0, cs]
                            else:
                                dst = vo2.rearrange("(p dd) e -> p dd e", dd=16)[:, r0:r0 + 4, cs]
                            S.dma(dst, vst.rearrange("p (t e) -> p t e", t=4))
                    import os
                    if out_l is not None and not os.environ.get("NOKOUT"):
                        for t0 in range(0, 16, 4):
                            pk = ps()
                            for q in range(4):
                                i = t0 + q
                                for k in range(8):
                                    S.mm(pk[:, q * 128:(q + 1) * 128], xnT_f[:, k, i * 128:(i + 1) * 128], Wk[:, k, :],
                                         start=(k == 0), stop=(k == 7))
                            S.copy(kst, pk[:], eng="scalar")
                            S.dma(k_o[out_l, g][t0 * 128:(t0 + 4) * 128, c * 128:(c + 1) * 128].rearrange("(t p) e -> p t e", p=128),
                                  kst.rearrange("p (t e) -> p t e", t=4))
                    def blockC(r, jb, Pt, g=g, d=d, Lh=Lh, nb=nb, E=E):
                        qs = r + d * 128 * jb
                        sl = lambda s_: slice(s_, s_ + 127 * d + 1, d)
                        pss = ps()
                        for hh in range(2):
                            S.mm(pss[:, hh * 256:hh * 256 + 128], KTb[:, sl(Lh + qs - 128 * d)], QT[:, hh, sl(qs)])
                            S.mm(pss[:, hh * 256 + 128:hh * 256 + 256], KTb[:, sl(Lh + qs)], QT[:, hh, sl(qs)])
                        yield
                        S.act(Pt, pss[:], ACTF.Exp, scale=SCALE)
                        yield
                        S.tt(Pt, Pt, (Eh if jb == 0 else E), ALU.mult, eng="gpsimd")
                        yield
                        t1 = d + r * nb + jb
                        th0 = r if jb == 0 else t1 - 1
                        pso = ps()
                        seq = [(hh, half) for hh in range(2) for half in range(2)]
                        for n_, (hh, half) in enumerate(seq):
                            S.mm(pso[:, 0:128], Vt[:, (th0 if half == 0 else t1), hh, :],
                                 Pt[:, hh * 256 + half * 128:hh * 256 + (half + 1) * 128], start=(n_ == 0), stop=(n_ == 3))
                        for n_, (hh, half) in enumerate(seq):
                            S.mm(pso[:, 128:256], ones2[:, hh, :],
                                 Pt[:, hh * 256 + half * 128:hh * 256 + (half + 1) * 128], start=(n_ == 0), stop=(n_ == 3))
                        yield
                        dst = acc[:, :, sl(qs)]
                        src = pso[:, 0:256].rearrange("p (a q) -> p a q", a=2)
                        if g == 0:
                            S.copy(dst, src, eng="vector")
                        else:
                            S.tt(dst, dst, src, ALU.add)
                        yield
                    blks = [(r, jb) for r in range(d) for jb in range(nb)]
                    for b0 in range(0, 16, 4):
                        lockstep([blockC(blks[b0 + q][0], blks[b0 + q][1], Pt2[q]) for q in range(4)])
                S.recip(acc[:, 1, :], acc[:, 1, :])
                S.tt(oT[2][:, c, :], acc[:, 0, :], acc[:, 1, :], ALU.mult)

            ckpt("%s:C" % pname)
            S.barrier()
            Wg = Vw(PL, [3, 8, 128])
            Wb = Vw(PL + 6 * KB, [3, 4, 128])
            macc = Vw(PL + 10 * KB, [512], F32)
            sgm = Vw(PL + 12 * KB, [512], F32)
            mtmp = Vw(PL + 14 * KB, [512], F32)
            Wo = Vw(PL + 16 * KB, [8, 1024])
            gpost = Vw(PL + 32 * KB, [1024], F32)
            wload(Wo, w_out[l], 8)
            bcast_load(gpost, g_post_mix[l:l + 1, :])
            for dc in range(8):
                for n in range(3):
                    wload(Wg[:, n, :, :], w_in[l][:, OFF_G + n * 1024 + dc * 128: OFF_G + n * 1024 + (dc + 1) * 128], 8)
                    wload(Wb[:, n, :, :], w_branch[l][n * 512:(n + 1) * 512, dc * 128:(dc + 1) * 128], 4)
                for w in range(4):
                    ws = slice(w * 512, (w + 1) * 512)
                    for n in range(3):
                        pg = ps(); pp = ps()
                        for k in range(8):
                            S.mm(pg[:], Wg[:, n, k, :], xnT_f[:, k, ws], start=(k == 0), stop=(k == 7))
                        for k in range(4):
                            S.mm(pp[:], Wb[:, n, k, :], oT[n][:, k, ws], start=(k == 0), stop=(k == 3))
                        S.act(sgm, pg[:], ACTF.Sigmoid)
                        if n == 0:
                            S.tt(macc, pp[:], sgm, ALU.mult)
                        else:
                            S.tt(mtmp, pp[:], sgm, ALU.mult)
                            if n == 1:
                                S.tt(macc, macc, mtmp, ALU.add, eng="gpsimd")
                            else:
                                S.tt(mergedT[:, dc, ws], macc, mtmp, ALU.add, eng="gpsimd")
            gpf = Vw(PL + 72 * KB, [1024], F32)
            bcast_load(gpf, g_pre_ffn[l:l + 1, :])

            def bodyM(i, sl_):
                TM = PL + 36 * KB + sl_ * 18 * KB
                ysb2 = Vw(TM, [1024], F32)
                junk2 = Vw(TM + 4 * KB, [1024], F32)
                xt = Vw(TM + 8 * KB, [1024], F32)
                njunk = Vw(TM + 12 * KB, [1024], F32)
                nxnb = Vw(TM + 16 * KB, [1024])
                S.dma(xt, xsrc[i * 128:(i + 1) * 128, :])
                py = [ps(), ps()]
                for cb in range(2):
                    for k in range(8):
                        S.mm(py[cb][:], mergedT[:, k, i * 128:(i + 1) * 128], Wo[:, k, cb * 512:(cb + 1) * 512],
                             start=(k == 0), stop=(k == 7))
                yield
                S.copy(ysb2[:, 0:512], py[0][:], eng="scalar")
                S.copy(ysb2[:, 512:1024], py[1][:], eng="vector")
                yield
                ss = yield from g_sumsq(ysb2, junk2)
                r = yield from g_rstd(ss, D)
                S.stt(ysb2, ysb2, r, gpost, ALU.mult, ALU.mult)
                yield
                S.tt(xt, xt, ysb2, ALU.add)
                yield
                S.dma(xmid[i * 128:(i + 1) * 128, :], xt)
                yield from g_norm_to_T(xt, gpf, xnT_f, i, njunk, nxnb)
            for i0 in range(0, 16, 2):
                lockstep([bodyM(i0, 0), bodyM(i0 + 1, 1)])

            ckpt("%s:M" % pname)
            S.barrier()
            actT = Vw(OT, [22, 1024])
            W2 = Vw(PL, [22, 1024])
            wload(W2, ffn_w_out[l], 22)
            W1 = [Vw(PL + 44 * KB + q * 4 * KB, [2, 8, 128]) for q in range(2)]
            sg = Vw(PL + 52 * KB, [512], F32)
            gpost2 = Vw(PL + 54 * KB, [1024], F32)
            bcast_load(gpost2, g_post_ffn[l:l + 1, :])
            ysb2 = Vw(PL + 58 * KB, [1024], F32)
            junk2 = Vw(PL + 62 * KB, [1024], F32)
            for hf in range(2):
                for fc in range(22):
                    Wc = W1[fc % 2]
                    wload(Wc[:, 0, :, :], ffn_w_in[l][:, fc * 128:(fc + 1) * 128], 8)
                    wload(Wc[:, 1, :, :], ffn_w_in[l][:, DFF + fc * 128:DFF + (fc + 1) * 128], 8)
                    for w in range(2):
                        ws = slice(hf * 1024 + w * 512, hf * 1024 + (w + 1) * 512)
                        pg = ps(); pu = ps()
                        for k in range(8):
                            S.mm(pg[:], Wc[:, 0, k, :], xnT_f[:, k, ws], start=(k == 0), stop=(k == 7))
                        for k in range(8):
                            S.mm(pu[:], Wc[:, 1, k, :], xnT_f[:, k, ws], start=(k == 0), stop=(k == 7))
                        S.act(sg, pg[:], ACTF.Silu)
                        S.tt(actT[:, fc, w * 512:(w + 1) * 512], pu[:], sg, ALU.mult)
                def bodyF(i8, sl_, hf=hf):
                    i = hf * 8 + i8
                    ysb_ = Vw(PL + 58 * KB + sl_ * 8 * KB, [1024], F32)
                    xt = Vw(PL + 62 * KB + sl_ * 8 * KB, [1024], F32)
                    junk_ = Vw(PL + 74 * KB, [1024])
                    S.dma(xt, xmid[i * 128:(i + 1) * 128, :])
                    py = [ps(), ps()]
                    for cb in range(2):
                        for k in range(22):
                            S.mm(py[cb][:], actT[:, k, i8 * 128:(i8 + 1) * 128], W2[:, k, cb * 512:(cb + 1) * 512],
                                 start=(k == 0), stop=(k == 21))
                    yield
                    S.copy(ysb_[:, 0:512], py[0][:], eng="scalar")
                    S.copy(ysb_[:, 512:1024], py[1][:], eng="vector")
                    yield
                    ss = yield from g_sumsq(ysb_, junk_)
                    r = yield from g_rstd(ss, D)
                    S.stt(ysb_, ysb_, r, gpost2, ALU.mult, ALU.mult)
                    yield
                    S.tt(xt, xt, ysb_, ALU.add)
                    yield
                    S.dma(xdst[i * 128:(i + 1) * 128, :], xt)
                    yield
                for i0 in range(0, 8, 2):
                    lockstep([bodyF(i0, 0), bodyF(i0 + 1, 1)])

        def run_sample(l):
            state["pass"] = "S%d" % l
            S.barrier()
            A0 = 0
            xnTs = Vw(A0, [8, 32])
            oTs = [Vw(A0 + 1 * KB + n * 256, [4, 32]) for n in range(3)]
            mergedTs = Vw(A0 + 2 * KB, [8, 32])
            actTs = Vw(A0 + 3 * KB, [22, 32])
            T0 = 8 * KB
            junk = Vw(T0, [1024], F32)
            ysb = Vw(T0 + 4 * KB, [1024], F32)
            gt_a = Vw(T0 + 8 * KB, [1024], F32)
            gt_b = Vw(T0 + 12 * KB, [1024], F32)
            xnb = Vw(T0 + 16 * KB, [1024])
            W0 = 56 * KB

            def norm32(src, gtile):
                ss = sumsq(src[0:32, :], junk[0:32, :], P=slice(0, 32))
                r = rstd_from_ss(ss, D, P=slice(0, 32))
                S.stt(xnb[0:32, :], src[0:32, :], r, gtile[0:32, :], ALU.mult, ALU.mult)
                pt = ps()[:].bitcast(BF16)
                for k in range(8):
                    S.tr(pt[:, k * 32:(k + 1) * 32], xnb[0:32, k * 128:(k + 1) * 128], identb[0:32, 0:32])
                S.copy(xnTs, pt[:, 0:256].rearrange("p (k c) -> p k c", k=8), eng="vector")

            def ln32(src, gt, bt, out, toff):
                jk = Vw(toff, [512], F32)
                tmp = Vw(toff + 2 * KB, [512], F32)
                P = slice(0, 32)
                sm = stc(); S.reduce(sm[P, :], src[P, :], ALU.add)
                sq = stc(); S.memset(sq[P, :], 0.0); S.act(jk[P, :], src[P, :], ACTF.Square, accum_out=sq[P, :])
                mean = stc(); S.ts(mean[P, :], sm[P, :], 1.0 / 512, 0.0, ALU.mult, ALU.add)
                msq = stc(); S.tt(msq[P, :], mean[P, :], mean[P, :], ALU.mult)
                var = stc(); S.stt(var[P, :], sq[P, :], 1.0 / 512, msq[P, :], ALU.mult, ALU.subtract)
                S.ts(var[P, :], var[P, :], 1.0, EPS, ALU.mult, ALU.add)
                S.act(var[P, :], var[P, :], ACTF.Sqrt)
                r = stc(); S.recip(r[P, :], var[P, :])
                S.ts(tmp[P, :], src[P, :], mean[P, :], r[P, :], ALU.subtract, ALU.mult)
                S.tt(tmp[P, :], tmp[P, :], gt[P, :], ALU.mult)
                S.tt(out[P, :], tmp[P, :], bt[P, :], ALU.add)

            def featT32(src_bf, dstT):
                pt = ps()[:].bitcast(BF16)
                for c in range(4):
                    S.tr(pt[:, c * 32:(c + 1) * 32], src_bf[0:32, c * 128:(c + 1) * 128], identb[0:32, 0:32])
                S.copy(dstT, pt[:, 0:128].rearrange("p (c t) -> p c t", c=4), eng="vector")

            P32 = slice(0, 32)
            bcast_load(gt_a, g_pre_mix[l:l + 1, :])
            if l == 0:
                S.dma(xs_res[:], xs_d)
            norm32(xs_res, gt_a)

            WA = Vw(W0, [8, 1024])
            wload(WA, w_in[l][:, OFF_AU:OFF_AU + 1024], 8)
            agt = Vw(T0 + 18 * KB, [512], F32); abt = Vw(T0 + 20 * KB, [512], F32)
            bcast_load(agt, a_norm_g[l:l + 1, :]); bcast_load(abt, a_norm_b[l:l + 1, :])
            BDf = Vw(T0 + 22 * KB, [4, 32], F32)
            BDb = Vw(T0 + 23 * KB, [4, 32])
            S.memset(BDf[P32], 0.0)
            for b in range(4):
                for g in range(4):
                    S.dma(BDf[b * 8:(b + 1) * 8, g, b * 8:(b + 1) * 8], a_w_s[l, g, 0:8, 0:8].rearrange("t s -> s t"),
                          allow_slow_non_contiguous=True)
                S.dma(bs32[b * 8:(b + 1) * 8, :], a_b_s[l][:, 0:8].rearrange("g t -> t g"), allow_slow_non_contiguous=True)
            for g in range(4):
                S.tt(BDb[P32, g, :], BDf[P32, g, :], bdm[:], ALU.mult)
            u_sb = Vw(T0 + 24 * KB, [512], F32); v_sb = Vw(T0 + 26 * KB, [512], F32)
            vn32 = Vw(T0 + 28 * KB, [512], F32); vnb = Vw(T0 + 30 * KB, [512]); oab = Vw(T0 + 31 * KB, [512])
            pm_sb = Vw(T0 + 32 * KB, [512], F32)
            pu = ps(); pv = ps()
            for k in range(8):
                S.mm(pu[P32, :], xnTs[:, k, :], WA[:, k, 0:512], start=(k == 0), stop=(k == 7))
            for k in range(8):
                S.mm(pv[P32, :], xnTs[:, k, :], WA[:, k, 512:1024], start=(k == 0), stop=(k == 7))
            S.copy(u_sb[P32], pu[P32, :], eng="scalar")
            S.copy(v_sb[P32], pv[P32, :], eng="vector")
            ln32(v_sb, agt, abt, vn32, T0 + 34 * KB)
            S.dma(nav_o[l], vn32[P32])
            S.copy(vnb[P32], vn32[P32], eng="vector")
            pm = ps()
            for g in range(4):
                S.mm(pm[P32, g * 128:(g + 1) * 128], BDb[P32, g, :], vnb[P32, g * 128:(g + 1) * 128])
            S.copy(pm_sb[P32], pm[P32, :], eng="scalar")
            for g in range(4):
                S.stt(oab[P32, g * 128:(g + 1) * 128], pm_sb[P32, g * 128:(g + 1) * 128], bs32[:, g:g + 1],
                      u_sb[P32, g * 128:(g + 1) * 128], ALU.add, ALU.mult)
            featT32(oab, oTs[0])

            S.barrier()
            WB = Vw(W0 + 16 * KB, [8, 1024])
            wload(WB, w_in[l][:, OFF_B:OFF_B + 1024], 8)
            diag = Vw(W0 + 32 * KB, [124, 128])
            cwT = Vw(T0 + 18 * KB, [4, 32], F32)
            for j in range(31):
                S.dma(cwT[:, :, j], b_conv_w[l, j].rearrange("(c p) -> p c", p=128), allow_slow_non_contiguous=True)
            for c in range(4):
                for j in range(31):
                    S.ts(diag[:, c * 31 + j, :], identb[:], cwT[:, c, j:j + 1], 1.0, ALU.mult, ALU.mult,
                         eng=("vector" if j % 2 == 0 else "gpsimd"))
            bgt = Vw(T0 + 20 * KB, [512], F32); bbt = Vw(T0 + 22 * KB, [512], F32)
            bcast_load(bgt, b_norm_g[l:l + 1, :]); bcast_load(bbt, b_norm_b[l:l + 1, :])
            cbT = Vw(T0 + 19 * KB, [4], F32)
            S.dma(cbT, b_conv_b[l].rearrange("(c p) -> p c", p=128), allow_slow_non_contiguous=True)
            pa = ps(); pg = ps()
            for k in range(8):
                S.mm(pa[P32, :], xnTs[:, k, :], WB[:, k, 0:512], start=(k == 0), stop=(k == 7))
            for k in range(8):
                S.mm(pg[P32, :], xnTs[:, k, :], WB[:, k, 512:1024], start=(k == 0), stop=(k == 7))
            sig = Vw(T0 + 24 * KB, [512], F32); glut = Vw(T0 + 26 * KB, [512], F32)
            S.act(sig[P32], pg[P32, :], ACTF.Sigmoid)
            S.tt(glut[P32], pa[P32, :], sig[P32], ALU.mult)
            for b in range(4):
                S.dma(nbs_o[l][b, 22:30, :], glut[b * 8:(b + 1) * 8, :])
            S.dma(nbs_o[l][:, 0:22, :], st_d[l][:, 8:30, :])
            padT = Vw(T0 + 28 * KB, [4, 4, 38])
            gl_f = Vw(T0 + 30 * KB, [4, 32], F32)
            sgf = Vw(T0 + 31 * KB, [32], F32)
            for c in range(4):
                pa = ps(); pg = ps()
                for k in range(8):
                    S.mm(pa[:, 0:32], WB[:, k, c * 128:(c + 1) * 128], xnTs[:, k, :], start=(k == 0), stop=(k == 7))
                for k in range(8):
                    S.mm(pg[:, 0:32], WB[:, k, 512 + c * 128:512 + (c + 1) * 128], xnTs[:, k, :], start=(k == 0), stop=(k == 7))
                S.act(sgf, pg[:, 0:32], ACTF.Sigmoid)
                S.tt(padT[:, c, :, 30:38], pa[:, 0:32].rearrange("p (b t) -> p b t", b=4),
                     sgf.rearrange("p (b t) -> p b t", b=4), ALU.mult)
            stf = Vw(T0 + 32 * KB, [4, 512], F32)
            stb = Vw(T0 + 40 * KB, [4, 512])
            S.dma(stf[0:30], st_d[l].rearrange("b j c -> j b c"))
            S.copy(stb[0:30], stf[0:30], eng="vector")
            pt = ps()[:].bitcast(BF16)
            for b in range(4):
                for c in range(4):
                    S.tr(pt[:, (b * 4 + c) * 32:(b * 4 + c) * 32 + 30], stb[0:30, b, c * 128:(c + 1) * 128], identb[0:30, 0:30])
            S.copy(padT[:, :, :, 0:30], pt[:, 0:512].rearrange("p (b c j) -> p c b j", b=4, c=4)[:, :, :, 0:30], eng="vector")
            pc = ps()
            for c in range(4):
                for j in range(31):
                    S.mm(pc[:, c * 32:(c + 1) * 32], diag[:, c * 31 + j, :], padT[:, c, :, j:j + 8],
                         start=(j == 0), stop=(j == 30))
            ycT = Vw(T0 + 44 * KB, [4, 32])
            for c in range(4):
                S.copy(sgf, pc[:, c * 32:(c + 1) * 32], eng="scalar")
                S.ts(ycT[:, c, :], sgf, cbT[:, c:c + 1], 1.0, ALU.add, ALU.mult)
            pt = ps()[:].bitcast(BF16)
            for c in range(4):
                S.tr(pt[P32, c * 128:(c + 1) * 128], ycT[:, c, :], identb[:])
            yc = Vw(T0 + 24 * KB, [512], F32)
            S.copy(yc[P32], pt[P32, 0:512], eng="vector")
            lnout = Vw(T0 + 26 * KB, [512], F32)
            ln32(yc, bgt, bbt, lnout, T0 + 34 * KB)
            obb = Vw(T0 + 45 * KB, [512])
            S.act(obb[P32], lnout[P32], ACTF.Silu)
            featT32(obb, oTs[1])

            S.barrier()
            QTs = Vw(T0 + 18 * KB, [4, 2, 32])
            KTs = Vw(T0 + 19 * KB, [4, 32])
            accS = Vw(T0 + 20 * KB, [4, 2, 32], F32)
            def cslot(q):
                o = T0 + 22 * KB + q * 12 * KB
                return (Vw(o, [512]), Vw(o + 1 * KB, [512]), Vw(o + 2 * KB, [4, 128]), Vw(o + 3 * KB, [4, 2, 128]),
                        Vw(o + 5 * KB, [4, 2, 128]), Vw(o + 7 * KB, [16]), Vw(o + 7 * KB + 64, [16]))
            cslots = [cslot(0), cslot(1)]
            kvst = Vw(T0 + 30 * KB, [512], F32)
            for q in range(2):
                S.memset(cslots[q][3].rearrange("p a b c -> p (a b c)"), 0.0, eng="gpsimd")
                S.memset(cslots[q][4].rearrange("p a b c -> p (a b c)"), 0.0, eng="gpsimd")
            cctr = [0]
            S.memset(QTs.rearrange("p a b c -> p (a b c)"), 0.0, eng="gpsimd")
            caches = ((c0k, c0v), (c1k, c1v), (c2k, c2v))
            for g in range(3):
                d = DIL[g]
                WS = W0 + 64 * KB + (g % 2) * 24 * KB
                WQ = Vw(WS, [8, 512]); WK = Vw(WS + 8 * KB, [8, 512]); WV = Vw(WS + 16 * KB, [8, 512])
                wload(WQ, w_in[l][:, OFF_CQ + g * 512:OFF_CQ + (g + 1) * 512], 8)
                wload(WK, w_in[l][:, OFF_CK + g * 512:OFF_CK + (g + 1) * 512], 8)
                wload(WV, w_in[l][:, OFF_CV + g * 512:OFF_CV + (g + 1) * 512], 8)
                for W_, dst_o in ((WK, ks_o), (WV, vs_o)):
                    pk = ps()
                    for k in range(8):
                        S.mm(pk[P32, :], xnTs[:, k, :], W_[:, k, :], start=(k == 0), stop=(k == 7))
                    S.copy(kvst[P32], pk[P32, :], eng="scalar")
                    S.dma(dst_o[l, g], kvst[P32])
                for c in range(4):
                    pq = ps()
                    for k in range(8):
                        S.mm(pq[:, 0:32], WQ[:, k, c * 128:(c + 1) * 128], xnTs[:, k, :], start=(k == 0), stop=(k == 7))
                    for k in range(8):
                        S.mm(pq[:, 32:64], WK[:, k, c * 128:(c + 1) * 128], xnTs[:, k, :], start=(k == 0), stop=(k == 7))
                    S.copy(QTs[0:64, c, 0, :], pq[0:64, 0:32], eng="vector")
                    S.copy(QTs[64:128, c, 1, :], pq[64:128, 0:32], eng="vector")
                    S.copy(KTs[:, c, :], pq[:, 32:64], eng="vector")
                def sblock(b, rho, slot_, g=g, d=d, WV=WV):
                    toks = list(range(rho, 8, d))
                    nq = len(toks)
                    tsl = slice(b * 8 + rho, b * 8 + rho + (nq - 1) * d + 1, d)
                    ck, cv = caches[g]
                    Kc, Vc, KcT, Vc2, Vn2, P0, P1 = cslots[slot_]
                    S.dma(Kc, ck[l, b][rho:rho + 127 * d + 1:d, :], eng="gpsimd")
                    S.dma(Vc, cv[l, b][rho:rho + 127 * d + 1:d, :], eng="gpsimd")
                    yield
                    pt = ps()[:].bitcast(BF16)
                    for c in range(4):
                        S.tr(pt[:, c * 128:(c + 1) * 128], Kc[:, c * 128:(c + 1) * 128], identb[:])
                    pvn = ps()
                    for k in range(8):
                        S.mm(pvn[0:nq, :], xnTs[:, k, tsl], WV[:, k, :], start=(k == 0), stop=(k == 7))
                    yield
                    S.copy(KcT, pt[:, 0:512].rearrange("p (c k) -> p c k", c=4), eng="vector")
                    Vc3 = Vc.rearrange("p (c e) -> p c e", c=4)
                    S.copy(Vc2[:, :, 0, 0:64], Vc3[:, :, 0:64], eng="vector")
                    S.copy(Vc2[:, :, 1, 64:128], Vc3[:, :, 64:128], eng="gpsimd")
                    yield
                    pv3 = pvn[0:nq, :].rearrange("p (c e) -> p c e", c=4)
                    S.copy(Vn2[0:nq, :, 0, 0:64], pv3[:, :, 0:64], eng="vector")
                    S.copy(Vn2[0:nq, :, 1, 64:128], pv3[:, :, 64:128], eng="vector")
                    yield
                    for c in range(4):
                        E = etab[:, g * 4 + c, :]
                        pss = ps()
                        for hh in range(2):
                            S.mm(pss[:, hh * 8:hh * 8 + nq], KcT[:, c, :], QTs[:, c, hh, tsl])
                        for hh in range(2):
                            S.mm(pss[0:nq, 16 + hh * 8:16 + hh * 8 + nq], KTs[:, c, tsl], QTs[:, c, hh, tsl])
                        yield
                        S.act(P0, pss[:, 0:16], ACTF.Exp, scale=SCALE)
                        S.act(P1[0:nq], pss[0:nq, 16:32], ACTF.Exp, scale=SCALE)
                        yield
                        for hh in range(2):
                            S.tt(P0[:, hh * 8:hh * 8 + nq], P0[:, hh * 8:hh * 8 + nq], E[:, hh * 256:hh * 256 + nq], ALU.mult)
                            S.tt(P1[0:nq, hh * 8:hh * 8 + nq], P1[0:nq, hh * 8:hh * 8 + nq],
                                 E[0:nq, hh * 256 + 128:hh * 256 + 128 + nq], ALU.mult)
                        yield
                        pso = ps()
                        for which, col in ((0, 0), (1, 8)):
                            n_ = 0
                            for hh in range(2):
                                lh0 = Vc2[:, c, hh, :] if which == 0 else ones2[:, hh, :]
                                lh1 = Vn2[0:nq, c, hh, :] if which == 0 else ones2[0:nq, hh, :]
                                S.mm(pso[:, col:col + nq], lh0, P0[:, hh * 8:hh * 8 + nq], start=(n_ == 0), stop=False)
                                n_ += 1
                                S.mm(pso[:, col:col + nq], lh1, P1[0:nq, hh * 8:hh * 8 + nq], start=False, stop=(hh == 1))
                        yield
                        dst = accS[:, c, :, tsl]
                        src = pso[:, 0:16].rearrange("p (a q) -> p a q", a=2)[:, :, 0:nq]
                        if g == 0:
                            S.copy(dst, src, eng="vector")
                        else:
                            S.tt(dst, dst, src, ALU.add)
                        yield
                sbl = [(b, rho) for b in range(4) for rho in range(min(d, 8))]
                for i0 in range(0, len(sbl), 2):
                    lockstep([sblock(sbl[i0 + q][0], sbl[i0 + q][1], q) for q in range(2) if i0 + q < len(sbl)])
            S.recip(accS[:, :, 1, :], accS[:, :, 1, :])
            S.tt(oTs[2], accS[:, :, 0, :], accS[:, :, 1, :], ALU.mult)

            S.barrier()
            Wo = Vw(W0 + 72 * KB, [8, 1024])
            macc = Vw(T0 + 18 * KB, [32], F32); sgm = Vw(T0 + 18 * KB + 128, [32], F32); mtmp = Vw(T0 + 18 * KB + 256, [32], F32)
            wload(Wo, w_out[l], 8)
            bcast_load(gt_a, g_post_mix[l:l + 1, :])
            bcast_load(gt_b, g_pre_ffn[l:l + 1, :])
            def sM(dc, slot_):
                Wg = Vw(W0 + dc * 6 * KB, [3, 8, 128]); Wb = Vw(W0 + 48 * KB + dc * 3 * KB, [3, 4, 128])
                macc = Vw(T0 + 18 * KB + slot_ * 512, [32], F32); sgm = Vw(T0 + 18 * KB + slot_ * 512 + 128, [32], F32)
                mtmp = Vw(T0 + 18 * KB + slot_ * 512 + 256, [32], F32)
                for n in range(3):
                    wload(Wg[:, n, :, :], w_in[l][:, OFF_G + n * 1024 + dc * 128: OFF_G + n * 1024 + (dc + 1) * 128], 8)
                    wload(Wb[:, n, :, :], w_branch[l][n * 512:(n + 1) * 512, dc * 128:(dc + 1) * 128], 4)
                for n in range(3):
                    pg = ps(); pp = ps()
                    for k in range(8):
                        S.mm(pg[:, 0:32], Wg[:, n, k, :], xnTs[:, k, :], start=(k == 0), stop=(k == 7))
                    for k in range(4):
                        S.mm(pp[:, 0:32], Wb[:, n, k, :], oTs[n][:, k, :], start=(k == 0), stop=(k == 3))
                    yield
                    S.act(sgm, pg[:, 0:32], ACTF.Sigmoid)
                    yield
                    if n == 0:
                        S.tt(macc, pp[:, 0:32], sgm, ALU.mult)
                    else:
                        S.tt(mtmp, pp[:, 0:32], sgm, ALU.mult)
                        yield
                        if n == 1:
                            S.tt(macc, macc, mtmp, ALU.add)
                        else:
                            S.tt(mergedTs[:, dc, :], macc, mtmp, ALU.add)
                    yield
            for dc0 in range(0, 8, 2):
                lockstep([sM(dc0, 0), sM(dc0 + 1, 1)])

            def post32(py, gp):
                S.copy(ysb[P32, 0:512], py[0][P32, :], eng="scalar")
                S.copy(ysb[P32, 512:1024], py[1][P32, :], eng="vector")
                ss = sumsq(ysb[P32], junk[P32], P=P32)
                r = rstd_from_ss(ss, D, P=P32)
                S.stt(ysb[P32], ysb[P32], r, gp[P32], ALU.mult, ALU.mult)
                S.tt(xs_res[:], xs_res[:], ysb[P32], ALU.add)

            py = [ps(), ps()]
            for cb in range(2):
                for k in range(8):
                    S.mm(py[cb][P32, :], mergedTs[:, k, :], Wo[:, k, cb * 512:(cb + 1) * 512], start=(k == 0), stop=(k == 7))
            post32(py, gt_a)
            norm32(xs_res, gt_b)

            S.barrier()
            W2 = Vw(W0, [22, 1024])
            wload(W2, ffn_w_out[l], 22)
            W1 = [Vw(W0 + 44 * KB + q * 4 * KB, [2, 8, 128]) for q in range(22)]
            bcast_load(gt_a, g_post_ffn[l:l + 1, :])
            sg = Vw(T0 + 18 * KB, [32], F32)
            def sF(fc, slot_):
                Wc = W1[fc]
                sg_ = Vw(T0 + 18 * KB + slot_ * 128, [32], F32)
                wload(Wc[:, 0, :, :], ffn_w_in[l][:, fc * 128:(fc + 1) * 128], 8)
                wload(Wc[:, 1, :, :], ffn_w_in[l][:, DFF + fc * 128:DFF + (fc + 1) * 128], 8)
                pg = ps(); pu = ps()
                for k in range(8):
                    S.mm(pg[:, 0:32], Wc[:, 0, k, :], xnTs[:, k, :], start=(k == 0), stop=(k == 7))
                for k in range(8):
                    S.mm(pu[:, 0:32], Wc[:, 1, k, :], xnTs[:, k, :], start=(k == 0), stop=(k == 7))
                yield
                S.act(sg_, pg[:, 0:32], ACTF.Silu)
                yield
                S.tt(actTs[:, fc, :], pu[:, 0:32], sg_, ALU.mult)
                yield
            for f0 in range(0, 22, 4):
                lockstep([sF(f0 + q, q) for q in range(4) if f0 + q < 22])
            py = [ps(), ps()]
            for cb in range(2):
                for k in range(22):
                    S.mm(py[cb][P32, :], actTs[:, k, :], W2[:, k, cb * 512:(cb + 1) * 512], start=(k == 0), stop=(k == 21))
            post32(py, gt_a)
            if l == 1:
                S.dma(ys_o, xs_res[:])


        try:
            if not skip_sample:
                run_sample(0)
                ckpt("S0")
                run_sample(1)
                ckpt("S1")
            if sample_only:
                raise _Stop()
            run_pass("A", 0, xw[2048:4096, :], xw[0:2048, :], x1s[0:2048, :], 0, None)
            ckpt("A:F")
            run_pass("B", 0, xw[4096:6144, :], xw[2048:4096, :], x1s[2048:4096, :], 1, 0)
            ckpt("B:F")
            run_pass("C", 1, x1s[2048:4096, :], x1s[0:2048, :], y_o, 2, 1)
        except _Stop:
            pass
        if dbg:
            S.barrier()
            S.dma(dbg_o, arena[:])
        S.emit(st)
    return nc


def _etab():
    e = np.zeros((12, 128, 512), np.float32)
    kk = np.arange(128)[:, None].astype(np.float64)
    qq = np.arange(128)[None, :].astype(np.float64)
    for g in range(3):
        for c in range(4):
            for hh in range(2):
                j = 2 * c + hh
                slope = 2.0 ** (-8.0 * (j * 3 + g + 1.0) / 24.0)
                for half in range(2):
                    step = qq + 128 - kk if half == 0 else qq - kk
                    val = np.exp(-slope * DIL[g] * step)
                    val = np.where((step >= 0) & (step <= 128), val, 0.0)
                    e[g * 4 + c, :, hh * 256 + half * 128: hh * 256 + (half + 1) * 128] = val
    return e


_NC_CACHE = {}


def kernel(**inp):
    f = lambda k: np.ascontiguousarray(np.asarray(inp[k], dtype=np.float32))
    xp = f("x_prompt")
    if "nc" not in _NC_CACHE:
        _NC_CACHE["nc"] = build_nc()
    nc = _NC_CACHE["nc"]
    wnames = ["norm_pre_mix", "norm_post_mix", "norm_pre_ffn", "norm_post_ffn", "w_in", "a_norm_g", "a_norm_b",
              "a_w_s", "a_b_s", "b_conv_w", "b_conv_b", "b_norm_g", "b_norm_b", "w_branch", "w_out", "ffn_w_in",
              "ffn_w_out"]
    shared = {k: f(k) for k in wnames}
    shared["etab"] = _etab()
    shared["ident"] = np.eye(128, dtype=np.float32)
    shared["tril"] = np.tril(np.ones((128, 128), np.float32))
    o2 = np.zeros((128, 2, 128), np.float32)
    o2[:, 0, 0:64] = 1.0
    o2[:, 1, 64:128] = 1.0
    shared["ones2"] = o2.reshape(128, 256)
    bdm = np.zeros((32, 32), np.float32)
    for b_ in range(4):
        for s_ in range(8):
            for t_ in range(s_, 8):
                bdm[b_ * 8 + s_, b_ * 8 + t_] = 1.0
    shared["bdm"] = bdm
    xs_all = f("x_sample"); st_all = f("state_b_conv")
    cch = [f(k) for k in ("cache_c0_k", "cache_c0_v", "cache_c1_k", "cache_c1_v", "cache_c2_k", "cache_c2_v")]
    in_maps = []
    for c in range(8):
        b, seg = c // 4, (c % 4) * 2048
        xw = np.zeros((6144, 1024), np.float32)
        lo = seg - 4096
        s0 = max(lo, 0)
        xw[s0 - lo:] = xp[b, s0:seg + 2048]
        fl = np.zeros((128, 4), np.float32)
        fl[:, 0] = 1.0 if seg >= 4096 else 0.0
        fl[:, 1] = 1.0 if seg >= 2048 else 0.0
        fl[:, 2] = 1.0 if seg >= 2048 else 0.0
        m = dict(shared)
        m["xw"] = xw
        m["flags"] = fl
        bs = slice(c * 4, (c + 1) * 4)
        m["xs"] = np.ascontiguousarray(xs_all[bs].reshape(32, 1024))
        m["st"] = np.ascontiguousarray(st_all[:, bs])
        for nm, arr in zip(("c0k", "c0v", "c1k", "c1v", "c2k", "c2v"), cch):
            m[nm] = np.ascontiguousarray(arr[:, bs].reshape(2, 4, arr.shape[2], 512))
        in_maps.append(m)
    res = run_bass_kernel_spmd(nc, in_maps, core_ids=list(range(8)))
    R = res.results
    y_prompt = np.stack([np.concatenate([R[b * 4 + i]["y_o"] for i in range(4)], axis=0) for b in range(2)], 0)
    gt = np.stack([R[b * 4 + 3]["gt_o"] for b in range(2)], 1)
    new_b_conv_prompt = np.ascontiguousarray(np.transpose(gt, (0, 1, 3, 2))[:, :, 2:, :])
    ko = np.stack([R[b * 4 + 3]["k_o"] for b in range(2)], 2)
    vo = np.stack([R[b * 4 + 3]["v_o"] for b in range(2)], 2)
    kvp = []
    for g, wlen in enumerate((128, 512, 2048)):
        kvp.append(np.ascontiguousarray(ko[:, g, :, NT - wlen:, :]).reshape(2, 2, wlen, 8, 64))
        kvp.append(np.ascontiguousarray(vo[:, g, :, NT - wlen:, :]).reshape(2, 2, wlen, 8, 64))
    y_sample = np.concatenate([R[c]["ys_o"].reshape(4, 8, 1024) for c in range(8)], 0)
    new_b_conv_sample = np.concatenate([R[c]["nbs_o"] for c in range(8)], 1)
    new_a_v_sample = np.concatenate([R[c]["nav_o"].reshape(2, 4, 8, 512) for c in range(8)], 1)
    kvs = []
    for g in range(3):
        kvs.append(np.concatenate([R[c]["ks_o"][:, g].reshape(2, 4, 8, 8, 64) for c in range(8)], 1))
        kvs.append(np.concatenate([R[c]["vs_o"][:, g].reshape(2, 4, 8, 8, 64) for c in range(8)], 1))
    return (y_prompt, y_sample, new_b_conv_prompt, new_b_conv_sample, new_a_v_sample, *kvp, *kvs)
```
